# Optimizing a Trainium2 kernel written in Bass

```python
import jax, jax.numpy as jnp
from jax import lax
import numpy as np

D_MODEL = 1024
BATCH = 8
SEQ = 4096
DEPTH = 2

N_MIXERS = 2
N_ATTN_LAYERS = (DEPTH + 1) // 2
N_POOL_LAYERS = DEPTH // 2
GRID_W = 64
EPS = 1e-6

HEAD_DIM = 64
N_HEADS = D_MODEL // HEAD_DIM
N_KV_HEADS = 4
GQA_GROUP = N_HEADS // N_KV_HEADS
Q_BLOCK = 128
ROPE_THETA = 10000.0
ROPE_PAIRS = HEAD_DIM // 4
QKV_DIM = (N_HEADS + 2 * N_KV_HEADS) * HEAD_DIM

POOL_WINDOWS = (2, 4, 8, 16)
N_POOL_GROUPS = len(POOL_WINDOWS)
POOL_GROUP_W = D_MODEL // N_POOL_GROUPS

MEM_LEN = 256
X_HEADS = 4
X_HEAD_DIM = D_MODEL // X_HEADS

D_FF = 2816
CONV_W = 3

kernel_name = "hybrid_axial_gqa_pool_encoder"


def rmsnorm(x, gain):
    x32 = x.astype(jnp.float32)
    y = x32 * lax.rsqrt(jnp.mean(x32 * x32, axis=-1, keepdims=True) + EPS)
    return (y * gain.astype(jnp.float32)).astype(x.dtype)


def axial_rope_tables(seq_len):
    n_rows = seq_len // GRID_W
    row = jnp.repeat(jnp.arange(n_rows, dtype=jnp.float32), GRID_W)
    col = jnp.tile(jnp.arange(GRID_W, dtype=jnp.float32), n_rows)
    inv_freq = ROPE_THETA ** (-jnp.arange(ROPE_PAIRS, dtype=jnp.float32) / ROPE_PAIRS)
    ang = jnp.stack([row[:, None] * inv_freq, col[:, None] * inv_freq], axis=1)
    return jnp.cos(ang), jnp.sin(ang)


def apply_axial_rope(x, cos, sin):
    b, s, h, d = x.shape
    xs = x.astype(jnp.float32).reshape(b, s, h, 2, 2, ROPE_PAIRS)
    x1, x2 = xs[..., 0, :], xs[..., 1, :]
    c = cos[None, :, None]
    sn = sin[None, :, None]
    out = jnp.stack([x1 * c - x2 * sn, x2 * c + x1 * sn], axis=-2)
    return out.reshape(b, s, h, d).astype(x.dtype)


def axial_gqa_attention(h, w_qkv, q_gain, k_gain, w_o):
    b, s, _ = h.shape
    qkv = h @ w_qkv
    q_end = N_HEADS * HEAD_DIM
    k_end = q_end + N_KV_HEADS * HEAD_DIM
    q = qkv[..., :q_end].reshape(b, s, N_HEADS, HEAD_DIM)
    k = qkv[..., q_end:k_end].reshape(b, s, N_KV_HEADS, HEAD_DIM)
    v = qkv[..., k_end:].reshape(b, s, N_KV_HEADS, HEAD_DIM)
    q = rmsnorm(q, q_gain)
    k = rmsnorm(k, k_gain)
    cos, sin = axial_rope_tables(s)
    q = apply_axial_rope(q, cos, sin) * (HEAD_DIM ** -0.5)
    k = apply_axial_rope(k, cos, sin)
    n_blk = s // Q_BLOCK
    qb = q.reshape(b, n_blk, Q_BLOCK, N_KV_HEADS, GQA_GROUP, HEAD_DIM).transpose(1, 0, 3, 4, 2, 5)
    kt = k.transpose(0, 2, 1, 3)
    vt = v.transpose(0, 2, 1, 3)

    def attend_block(q_blk):
        scores = jnp.einsum('bkgqd,bksd->bkgqs', q_blk, kt).astype(jnp.float32)
        p = jax.nn.softmax(scores, axis=-1).astype(vt.dtype)
        return jnp.einsum('bkgqs,bksd->bkgqd', p, vt)

    o = lax.map(attend_block, qb)
    o = o.transpose(1, 0, 4, 2, 3, 5).reshape(b, s, N_HEADS * HEAD_DIM)
    return o @ w_o


def multiscale_pool_mixer(h, w_grp, scale):
    b, s, d = h.shape
    hg = h.astype(jnp.float32).reshape(b, s, N_POOL_GROUPS, POOL_GROUP_W)
    csum = jnp.concatenate(
        [jnp.zeros((b, 1, N_POOL_GROUPS, POOL_GROUP_W), jnp.float32), jnp.cumsum(hg, axis=1)], axis=1)
    t = jnp.arange(s)[:, None]
    win = jnp.array(POOL_WINDOWS, dtype=jnp.int32)[None, :]
    lo = jnp.clip(t - win // 2, 0, s)
    hi = jnp.clip(t + win - win // 2, 0, s)
    s_hi = jnp.take_along_axis(csum, hi[None, :, :, None], axis=1)
    s_lo = jnp.take_along_axis(csum, lo[None, :, :, None], axis=1)
    count = (hi - lo).astype(jnp.float32)[None, :, :, None]
    mixed = (s_hi - s_lo) / count - hg
    y = jnp.einsum('bsgc,gcd->bsgd', mixed, w_grp.astype(jnp.float32)).reshape(b, s, d)
    return (y * scale.astype(jnp.float32)).astype(h.dtype)


def memory_cross_attention(h, mem_n, w_q, w_kv, w_o):
    b, s, _ = h.shape
    m = mem_n.shape[1]
    q = (h @ w_q).reshape(b, s, X_HEADS, X_HEAD_DIM) * (X_HEAD_DIM ** -0.5)
    kv = mem_n @ w_kv
    k = kv[..., :D_MODEL].reshape(b, m, X_HEADS, X_HEAD_DIM)
    v = kv[..., D_MODEL:].reshape(b, m, X_HEADS, X_HEAD_DIM)
    scores = jnp.einsum('bshd,bmhd->bhsm', q, k).astype(jnp.float32)
    p = jax.nn.softmax(scores, axis=-1).astype(v.dtype)
    o = jnp.einsum('bhsm,bmhd->bshd', p, v).reshape(b, s, D_MODEL)
    return o @ w_o


def conv_gated_ffn(h, w_up, conv_w, conv_b, w_down):
    u = h @ w_up
    up = jnp.pad(u, ((0, 0), (1, 1), (0, 0)))
    u = up[:, :-2] * conv_w[0] + up[:, 1:-1] * conv_w[1] + up[:, 2:] * conv_w[2] + conv_b
    gate, val = u[..., :D_FF], u[..., D_FF:]
    return (jax.nn.silu(gate) * val) @ w_down


def setup_inputs(seed: int = 0) -> dict:
    key = jax.random.key(seed)
    ks = jax.random.split(key, 24)
    f32 = jnp.float32

    def nrm(k, shape, scale):
        return jax.random.normal(k, shape, f32) * scale

    def gain(k, shape):
        return 1.0 + 0.05 * jax.random.normal(k, shape, f32)

    na, nb = N_ATTN_LAYERS, N_POOL_LAYERS
    return {
        "x": nrm(ks[0], (BATCH, SEQ, D_MODEL), 1.0),
        "mem": nrm(ks[1], (BATCH, MEM_LEN, D_MODEL), 1.0),
        "attn_norm": gain(ks[2], (na, D_MODEL)),
        "attn_w_qkv": nrm(ks[3], (na, D_MODEL, QKV_DIM), D_MODEL ** -0.5),
        "attn_q_gain": gain(ks[4], (na, HEAD_DIM)),
        "attn_k_gain": gain(ks[5], (na, HEAD_DIM)),
        "attn_w_o": nrm(ks[6], (na, N_HEADS * HEAD_DIM, D_MODEL), (N_HEADS * HEAD_DIM) ** -0.5),
        "pool_norm": gain(ks[7], (nb, D_MODEL)),
        "pool_w": nrm(ks[8], (nb, N_POOL_GROUPS, POOL_GROUP_W, POOL_GROUP_W), POOL_GROUP_W ** -0.5),
        "pool_scale": gain(ks[9], (nb, D_MODEL)),
        "xattn_norm": gain(ks[10], (DEPTH, D_MODEL)),
        "mem_norm": gain(ks[11], (DEPTH, D_MODEL)),
        "xattn_w_q": nrm(ks[12], (DEPTH, D_MODEL, D_MODEL), D_MODEL ** -0.5),
        "xattn_w_kv": nrm(ks[13], (DEPTH, D_MODEL, 2 * D_MODEL), D_MODEL ** -0.5),
        "xattn_w_o": nrm(ks[14], (DEPTH, D_MODEL, D_MODEL), D_MODEL ** -0.5),
        "ffn_norm": gain(ks[15], (DEPTH, D_MODEL)),
        "ffn_w_up": nrm(ks[16], (DEPTH, D_MODEL, 2 * D_FF), D_MODEL ** -0.5),
        "ffn_conv_w": nrm(ks[17], (DEPTH, CONV_W, 2 * D_FF), CONV_W ** -0.5),
        "ffn_conv_b": nrm(ks[18], (DEPTH, 2 * D_FF), 0.02),
        "ffn_w_down": nrm(ks[19], (DEPTH, D_FF, D_MODEL), D_FF ** -0.5),
        "final_norm": gain(ks[20], (D_MODEL,)),
    }


def reference(x, mem, attn_norm, attn_w_qkv, attn_q_gain, attn_k_gain, attn_w_o,
              pool_norm, pool_w, pool_scale,
              xattn_norm, mem_norm, xattn_w_q, xattn_w_kv, xattn_w_o,
              ffn_norm, ffn_w_up, ffn_conv_w, ffn_conv_b, ffn_w_down,
              final_norm):
    ia = 0
    ib = 0
    for i in range(DEPTH):
        if i % N_MIXERS == 0:
            x = x + axial_gqa_attention(rmsnorm(x, attn_norm[ia]), attn_w_qkv[ia],
                                        attn_q_gain[ia], attn_k_gain[ia], attn_w_o[ia])
            ia += 1
        else:
            x = x + multiscale_pool_mixer(rmsnorm(x, pool_norm[ib]), pool_w[ib], pool_scale[ib])
            ib += 1
        x = x + memory_cross_attention(rmsnorm(x, xattn_norm[i]), rmsnorm(mem, mem_norm[i]),
                                       xattn_w_q[i], xattn_w_kv[i], xattn_w_o[i])
        x = x + conv_gated_ffn(rmsnorm(x, ffn_norm[i]), ffn_w_up[i], ffn_conv_w[i],
                               ffn_conv_b[i], ffn_w_down[i])
    return rmsnorm(x, final_norm)
```

```python
import numpy as np
from contextlib import ExitStack
import concourse.bass as bass
import concourse.mybir as mybir
from concourse.bass_utils import run_bass_kernel_spmd

F32 = mybir.dt.float32
BF16 = mybir.dt.bfloat16
AF = mybir.ActivationFunctionType
ALU = mybir.AluOpType
AX = mybir.AxisListType

D = 1024
DFF = 2816
NCH = 44
EPS = 1e-6
MEM = 256


class Buf:
    __slots__ = ("name", "w", "r")

    def __init__(self, name):
        self.name = name
        self.w = {}
        self.r = {}


class T:
    __slots__ = ("t", "b")

    def __init__(self, t, name):
        self.t = t
        self.b = Buf(name)

    def __getitem__(self, k):
        return self.t[k]


class Sched:
    def __init__(self, nc, stack):
        self.nc = nc
        self.stack = stack
        self.engs = {"pe": nc.tensor, "act": nc.scalar, "dve": nc.vector, "pool": nc.gpsimd, "sp": nc.sync}
        self.sems = {}
        self.cnt = {}
        self.seen = {}
        self.nops = 0
        self.nwaits = 0
        for k in ("pe", "act", "dve", "pool"):
            self._sem(k)

    def _sem(self, name):
        if name not in self.sems:
            self.sems[name] = self.stack.enter_context(self.nc.semaphore("s_" + name))
            self.cnt[name] = 0
        return self.sems[name]

    def _deps(self, eng, reads, writes):
        deps = {}
        for t in reads:
            for s, v in t.b.w.items():
                if deps.get(s, 0) < v:
                    deps[s] = v
        for t in writes:
            for d in (t.b.w, t.b.r):
                for s, v in d.items():
                    if s == eng:
                        continue
                    if deps.get(s, 0) < v:
                        deps[s] = v
        return deps

    def _wait(self, eng, deps):
        e = self.engs[eng]
        for s, v in deps.items():
            if self.seen.get((eng, s), 0) < v:
                e.wait_ge(self.sems[s], v)
                self.seen[(eng, s)] = v
                self.nwaits += 1

    def op(self, eng, fn, reads=(), writes=()):
        self._wait(eng, self._deps(eng, reads, writes))
        ins = fn(self.engs[eng])
        if isinstance(ins, (list, tuple)):
            ins = ins[-1]
        self.cnt[eng] += 1
        v = self.cnt[eng]
        ins.then_inc(self.sems[eng], 1)
        for t in reads:
            t.b.r[eng] = v
        for t in writes:
            t.b.w[eng] = v
        self.nops += 1

    def dma(self, queue, out, in_, reads=(), writes=(), sem=None, **kw):
        self._sem(sem)
        self._wait(queue, self._deps("__dma__", reads, writes))
        ins = self.engs[queue].dma_start(out=out, in_=in_, **kw)
        self.cnt[sem] += 16
        v = self.cnt[sem]
        ins.then_inc(self.sems[sem], 16)
        for t in reads:
            t.b.r[sem] = v
        for t in writes:
            t.b.w[sem] = v
        self.nops += 1

    def barrier(self):
        deps = {s: v for s, v in self.cnt.items() if v > 0}
        for eng in ("sp", "pe", "act", "dve", "pool"):
            self._wait(eng, deps)


def bc_last(ap, n):
    return bass.AP(ap.tensor, ap.offset, [list(ap.ap[0]), list(ap.ap[1]), [0, n]])


def bc_mid(ap, n):
    return bass.AP(ap.tensor, ap.offset, [list(ap.ap[0]), [0, n]] + [list(a) for a in ap.ap[1:]])


def bc_part(ap, n=128):
    return bass.AP(ap.tensor, ap.offset, [[0, n]] + [list(a) for a in ap.ap[1:]])


def build(S=4096, nlayers_dbg=None, stop_after=None):
    NT = S // 128
    NB = S // 512
    nc = bass.Bass("TRN2", target_bir_lowering=False)

    def din(name, shape):
        return nc.dram_tensor(name, list(shape), F32, kind="ExternalInput").ap()

    x_in = din("x", [S, D])
    mem_in = din("mem", [MEM, D])
    attn_norm = din("attn_norm", [1, D])
    attn_w_qkv = din("attn_w_qkv", [1, D, 1536])
    attn_q_gain = din("attn_q_gain", [1, 64])
    attn_k_gain = din("attn_k_gain", [1, 64])
    attn_w_o = din("attn_w_o", [1, D, D])
    pool_norm = din("pool_norm", [1, D])
    pool_w = din("pool_w", [1, 4, 256, 256])
    pool_scale = din("pool_scale", [1, D])
    xattn_norm = din("xattn_norm", [2, D])
    mem_norm = din("mem_norm", [2, D])
    xattn_w_q = din("xattn_w_q", [2, D, D])
    xattn_w_kv = din("xattn_w_kv", [2, D, 2 * D])
    xattn_w_o = din("xattn_w_o", [2, D, D])
    ffn_norm = din("ffn_norm", [2, D])
    ffn_w_up = din("ffn_w_up", [2, D, 2 * DFF])
    ffn_conv_w = din("ffn_conv_w", [2, 3, 2 * DFF])
    ffn_conv_b = din("ffn_conv_b", [2, 2 * DFF])
    ffn_w_down = din("ffn_w_down", [2, DFF, D])
    final_norm = din("final_norm", [1, D])
    rope_in = din("rope", [S, 128])
    invc_in = din("invc", [1, 64])
    out = nc.dram_tensor("out", [S, D], F32, kind="ExternalOutput").ap()
    rA = nc.dram_tensor("resA", [S, D], F32, kind="Internal").ap()
    rB = nc.dram_tensor("resB", [S, D], F32, kind="Internal").ap()
    dA = T(None, "resA")
    dB = T(None, "resB")
    dX = T(None, "x")
    dO = T(None, "out")

    with ExitStack() as st:
        SC = Sched(nc, st)
        op = SC.op
        dma = SC.dma

        uid = [0]

        def sbuf(stack, name, shape, dt=F32):
            uid[0] += 1
            name = f"{name}_{uid[0]}"
            return T(stack.enter_context(nc.sbuf_tensor(name, list(shape), dt)), name)

        def psum(stack, name, shape, dt=F32):
            return T(stack.enter_context(nc.psum_tensor(name, list(shape), dt)), name)

        ident = sbuf(st, "ident", [128, 128], BF16)
        identf = sbuf(st, "identf", [128, 128], F32)
        ones_f = sbuf(st, "ones_f", [128, 128], F32)
        ones_b = sbuf(st, "ones_b", [128, 128], BF16)
        for idt in (ident, identf):
            op("pool", lambda e, idt=idt: e.memset(idt[:], 0.0), writes=[idt])
            op("pool", lambda e, idt=idt: e.affine_select(out=idt[:], in_=idt[:], pattern=[[-1, 128]],
                                                         compare_op=ALU.not_equal, fill=1.0, base=0,
                                                         channel_multiplier=1), reads=[idt], writes=[idt])
        op("dve", lambda e: e.memset(ones_f[:], 1.0), writes=[ones_f])
        op("dve", lambda e: e.memset(ones_b[:], 1.0), writes=[ones_b])

        PB = [psum(st, f"pb{i}", [128, 512], F32) for i in range(6)]
        PT = [psum(st, f"pt{i}", [128, 1024], BF16) for i in range(2)]

        def load_weight_bf16(w_t, src2d, nchunk, ncols, sem, col_split=1, p=128):
            dst3 = w_t[:].rearrange("p (c n) -> p c n", c=nchunk)
            src3 = src2d.rearrange("(c p) n -> p c n", p=p)
            step = ncols // col_split
            for i in range(col_split):
                dma("pool", dst3[:, :, i * step:(i + 1) * step], src3[:, :, i * step:(i + 1) * step],
                    writes=[w_t], sem=sem)

        def rms_stats(xt_ap, xt, npart, sq, ss, lnv, rstd, width=D, out_bias=0.0):
            op("dve", lambda e: e.tensor_tensor(out=sq[0:npart, 0:width], in0=xt_ap, in1=xt_ap, op=ALU.mult),
               reads=[xt], writes=[sq])
            op("dve", lambda e: e.tensor_reduce(out=ss[0:npart, 0:1], in_=sq[0:npart, 0:width], axis=AX.X, op=ALU.add),
               reads=[sq], writes=[ss])
            op("act", lambda e: e.activation(out=lnv[0:npart, 0:1], in_=ss[0:npart, 0:1], func=AF.Ln,
                                             scale=1.0 / width, bias=EPS), reads=[ss], writes=[lnv])
            op("act", lambda e: e.activation(out=rstd[0:npart, 0:1], in_=lnv[0:npart, 0:1], func=AF.Exp,
                                             scale=-0.5, bias=out_bias), reads=[lnv], writes=[rstd])

        def norm_to_bf16(xt, gain, hb, sq, ss, lnv, rstd, npart=128):
            rms_stats(xt[0:npart, :], xt, npart, sq, ss, lnv, rstd)
            op("dve", lambda e: e.scalar_tensor_tensor(out=hb[0:npart, :], in0=xt[0:npart, :], scalar=rstd[0:npart, 0:1],
                                                      in1=gain[0:npart, :], op0=ALU.mult, op1=ALU.mult),
               reads=[xt, rstd, gain], writes=[hb])

        def transpose_h(hb, pt, npart=128):
            op("pe", lambda e: [e.transpose(out=pt[:, c * npart:(c + 1) * npart], in_=hb[0:npart, c * 128:(c + 1) * 128],
                                            identity=ident[0:npart, 0:npart]) for c in range(8)],
               reads=[hb, ident], writes=[pt])

        def qk_norm_rope(src, H, gain, rope, dst, tmp, ss, lnv, rstd, out_bias):
            W = H * 64
            sq, ta, tb = tmp
            s3 = src[:, 0:W].rearrange("p (h d) -> p h d", d=64)
            op("dve", lambda e: e.tensor_tensor(out=sq[:, 0:W], in0=src[:, 0:W], in1=src[:, 0:W], op=ALU.mult),
               reads=[src], writes=[sq])
            op("dve", lambda e: e.tensor_reduce(out=ss[:, 0:H], in_=sq[:, 0:W].rearrange("p (h d) -> p h d", d=64),
                                                axis=AX.X, op=ALU.add), reads=[sq], writes=[ss])
            op("act", lambda e: e.activation(out=lnv[:, 0:H], in_=ss[:, 0:H], func=AF.Ln, scale=1.0 / 64, bias=EPS),
               reads=[ss], writes=[lnv])
            op("act", lambda e: e.activation(out=rstd[:, 0:H], in_=lnv[:, 0:H], func=AF.Exp, scale=-0.5, bias=out_bias),
               reads=[lnv], writes=[rstd])
            a3 = ta[:, 0:W].rearrange("p (h d) -> p h d", d=64)
            op("dve", lambda e: e.tensor_tensor(out=a3, in0=s3, in1=bc_last(rstd[:, 0:H], 64), op=ALU.mult),
               reads=[src, rstd], writes=[ta])
            op("dve", lambda e: e.tensor_tensor(out=a3, in0=a3, in1=bc_mid(gain[:, 0:64], H), op=ALU.mult),
               reads=[ta, gain], writes=[ta])
            b3 = tb[:, 0:W].rearrange("p (h d) -> p h d", d=64)
            op("dve", lambda e: e.tensor_tensor(out=b3, in0=a3, in1=bc_mid(rope[:, 0:64], H), op=ALU.mult),
               reads=[ta, rope], writes=[tb])
            a5 = ta[:, 0:W].rearrange("p (h a f q) -> p h a f q", a=2, f=2, q=16)
            s5 = sq[:, 0:W].rearrange("p (h a f q) -> p h a f q", a=2, f=2, q=16)
            nsin = bc_mid(rope[:, 64:96].rearrange("p (a q) -> p a q", a=2), H)
            psin = bc_mid(rope[:, 96:128].rearrange("p (a q) -> p a q", a=2), H)
            op("dve", lambda e: e.tensor_tensor(out=s5[:, :, :, 0, :], in0=a5[:, :, :, 1, :], in1=nsin, op=ALU.mult),
               reads=[ta, rope], writes=[sq])
            op("dve", lambda e: e.tensor_tensor(out=s5[:, :, :, 1, :], in0=a5[:, :, :, 0, :], in1=psin, op=ALU.mult),
               reads=[ta, rope], writes=[sq])
            op("dve", lambda e: e.tensor_tensor(out=dst[:, 0:W], in0=tb[:, 0:W], in1=sq[:, 0:W], op=ALU.add),
               reads=[tb, sq], writes=[dst])

        def phase_attention(src, dsrc, dst, ddst):
            with ExitStack() as ph:
                wqkv = sbuf(ph, "wqkv", [128, 8 * 1536], BF16)
                wo = sbuf(ph, "wo_a", [64, 16 * D], BF16)
                KT = sbuf(ph, "KT", [128, 4 * S], BF16)
                Vs = sbuf(ph, "Vs", [128, NT * 4 * 65], BF16)
                gain = sbuf(ph, "gain_a", [128, D])
                qg = sbuf(ph, "qg", [128, 64])
                kg = sbuf(ph, "kg", [128, 64])
                xs = [sbuf(ph, f"xs{i}", [128, D]) for i in range(3)]
                rp = [sbuf(ph, f"rp{i}", [128, 128]) for i in range(3)]
                hb = [sbuf(ph, f"hb{i}", [128, D], BF16) for i in range(2)]
                hT = [sbuf(ph, f"hT{i}", [128, 8 * 128], BF16) for i in range(2)]
                sq = sbuf(ph, "sq", [128, D])
                ta = sbuf(ph, "ta", [128, D])
                tb = sbuf(ph, "tb", [128, D])
                qs = sbuf(ph, "qs", [128, D])
                qr = sbuf(ph, "qr", [128, D], BF16)
                QT = [sbuf(ph, f"QT{i}", [128, 16 * 128], BF16) for i in range(2)]
                Pb = [sbuf(ph, f"Pb{i}", [128, 512], BF16) for i in range(3)]
                OT = [sbuf(ph, f"OT{i}", [64, 16 * 128], BF16) for i in range(2)]
                rden = sbuf(ph, "rden", [128, 512])
                bcs = sbuf(ph, "bcs", [64, 512])
                xn = [sbuf(ph, f"xn{i}", [128, D]) for i in range(2)]
                ss = sbuf(ph, "ss", [128, 16]); lnv = sbuf(ph, "lnv", [128, 16]); rstd = sbuf(ph, "rstd", [128, 16])
                ss1 = sbuf(ph, "ss1", [128, 1]); lnv1 = sbuf(ph, "lnv1", [128, 1]); rstd1 = sbuf(ph, "rstd1", [128, 1])

                load_weight_bf16(wqkv, attn_w_qkv[0], 8, 1536, "w0")
                dma("pool", wo[:].rearrange("d (h n) -> d h n", h=16), attn_w_o[0].rearrange("(h d) n -> d h n", d=64),
                    writes=[wo], sem="w1")
                dma("sp", gain[:], bc_part(attn_norm), writes=[gain], sem="c0")
                dma("sp", qg[:], bc_part(attn_q_gain), writes=[qg], sem="c0")
                dma("sp", kg[:], bc_part(attn_k_gain), writes=[kg], sem="c0")
                op("pool", lambda e: e.memset(KT[64:128, :], 0.0), writes=[KT])
                for q in QT:
                    op("pool", lambda e, q=q: e.memset(q[64:128, :], 0.0), writes=[q])
                op("dve", lambda e: e.memset(Vs[:].rearrange("p (t e) -> p t e", e=65)[:, :, 64:65], 1.0), writes=[Vs])

                def load_tile(t, i):
                    dma("sp", xs[i][:], src[t * 128:(t + 1) * 128, :], reads=[dsrc], writes=[xs[i]], sem=f"lx{i}")
                    dma("sp", rp[i][:], rope_in[t * 128:(t + 1) * 128, :], writes=[rp[i]], sem=f"lr{i}")

                load_tile(0, 0)
                for t in range(NT):
                    i = t % 3
                    if t + 1 < NT:
                        load_tile(t + 1, (t + 1) % 3)
                    j = t % 2
                    norm_to_bf16(xs[i], gain, hb[j], sq, ss1, lnv1, rstd1)
                    transpose_h(hb[j], PT[j])
                    op("dve", lambda e, j=j: e.tensor_copy(out=hT[j][:], in_=PT[j][:]), reads=[PT[j]], writes=[hT[j]])
                    pk = PB[j]
                    op("pe", lambda e, j=j, pk=pk: [e.matmul(pk[:], lhsT=hT[j][:, c * 128:(c + 1) * 128],
                                                             rhs=wqkv[:, c * 1536 + 1024:c * 1536 + 1536],
                                                             start=(c == 0), stop=(c == 7)) for c in range(8)],
                       reads=[hT[j], wqkv], writes=[pk])
                    op("act", lambda e, pk=pk: e.activation(out=qs[:, 0:512], in_=pk[:], func=AF.Copy),
                       reads=[pk], writes=[qs])
                    qk_norm_rope(qs, 4, kg, rp[i], qr, (sq, ta, tb), ss, lnv, rstd, 0.0)
                    pt = PT[j]
                    op("pe", lambda e, pt=pt: [e.transpose(out=pt[0:64, g * 128:(g + 1) * 128], in_=qr[:, g * 64:(g + 1) * 64],
                                                           identity=ident[:]) for g in range(4)],
                       reads=[qr, ident], writes=[pt])
                    op("dve", lambda e, pt=pt, t=t: e.tensor_copy(
                        out=KT[0:64, :].rearrange("p (g s) -> p g s", g=4)[:, :, t * 128:(t + 1) * 128],
                        in_=pt[0:64, 0:512].rearrange("p (g s) -> p g s", g=4)), reads=[pt], writes=[KT])
                    op("act", lambda e, t=t: e.activation(
                        out=Vs[:, t * 260:(t + 1) * 260].rearrange("p (g e) -> p g e", e=65)[:, :, 0:64],
                        in_=qs[:, 256:512].rearrange("p (g d) -> p g d", d=64), func=AF.Copy),
                       reads=[qs], writes=[Vs])

                def prep_a1(qb):
                    i = qb % 3
                    load_tile(qb, i)
                    norm_to_bf16(xs[i], gain, hb[qb % 2], sq, ss1, lnv1, rstd1)

                def prep_a2(qb):
                    j = qb % 2
                    transpose_h(hb[j], PT[0])
                    op("dve", lambda e: e.tensor_copy(out=hT[j][:], in_=PT[0][:]), reads=[PT[0]], writes=[hT[j]])

                def prep_a3(qb):
                    j = qb % 2
                    i = qb % 3
                    for half in range(2):
                        pk = PB[half]
                        op("pe", lambda e, pk=pk, half=half: [e.matmul(pk[:], lhsT=hT[j][:, c * 128:(c + 1) * 128],
                                                                       rhs=wqkv[:, c * 1536 + half * 512:c * 1536 + half * 512 + 512],
                                                                       start=(c == 0), stop=(c == 7)) for c in range(8)],
                           reads=[hT[j], wqkv], writes=[pk])
                        op("act", lambda e, pk=pk, half=half: e.activation(out=qs[:, half * 512:(half + 1) * 512], in_=pk[:],
                                                                           func=AF.Copy), reads=[pk], writes=[qs])
                    qk_norm_rope(qs, 16, qg, rp[i], qr, (sq, ta, tb), ss, lnv, rstd, float(np.log(0.125)))

                def prep_b(qb):
                    j = qb % 2
                    for half in range(2):
                        pt = PT[half]
                        op("pe", lambda e, pt=pt, half=half: [e.transpose(out=pt[0:64, h * 128:(h + 1) * 128],
                                                                          in_=qr[:, (half * 8 + h) * 64:(half * 8 + h + 1) * 64],
                                                                          identity=ident[:]) for h in range(8)],
                           reads=[qr, ident], writes=[pt])
                        op("dve", lambda e, pt=pt, half=half: e.tensor_copy(out=QT[j][0:64, half * 1024:(half + 1) * 1024],
                                                                            in_=pt[0:64, :]), reads=[pt], writes=[QT[j]])

                prep_a1(0); prep_a2(0); prep_a3(0); prep_b(0)
                SB_ = [PB[2], PB[3]]
                OB_ = [PB[4], PB[5]]
                for qb in range(NT):
                    j = qb % 2
                    nxt = qb + 1 < NT
                    for g in range(4):
                        if nxt:
                            (prep_a1, prep_a2, prep_a3, prep_b)[g](qb + 1)
                        ob = OB_[g % 2]

                        def s_mm(kt):
                            sbk = SB_[kt % 2]
                            op("pe", lambda e: e.matmul(sbk[:], lhsT=KT[:, g * S + kt * 128:g * S + (kt + 1) * 128],
                                                        rhs=QT[j][:, g * 512:(g + 1) * 512], start=True, stop=True),
                               reads=[KT, QT[j]], writes=[sbk])

                        s_mm(0)
                        if NT > 1:
                            s_mm(1)
                        for kt in range(NT):
                            sbk = SB_[kt % 2]
                            pb = Pb[kt % 3]
                            op("act", lambda e: e.activation(out=pb[:], in_=sbk[:], func=AF.Exp), reads=[sbk], writes=[pb])
                            op("pe", lambda e: e.matmul(ob[0:65, :], lhsT=Vs[:, (kt * 4 + g) * 65:(kt * 4 + g + 1) * 65],
                                                        rhs=pb[:], start=(kt == 0), stop=(kt == NT - 1)),
                               reads=[Vs, pb], writes=[ob])
                            if kt + 2 < NT:
                                s_mm(kt + 2)
                        op("dve", lambda e: e.reciprocal(out=rden[64:65, :], in_=ob[64:65, :]), reads=[ob], writes=[rden])
                        bcb = PB[g % 2]
                        op("pe", lambda e: e.matmul(bcb[0:64, :], lhsT=ones_f[64:65, 0:64], rhs=rden[64:65, :],
                                                    start=True, stop=True), reads=[ones_f, rden], writes=[bcb])
                        op("dve", lambda e: e.tensor_copy(out=bcs[:], in_=bcb[0:64, :]), reads=[bcb], writes=[bcs])
                        op("dve", lambda e: e.tensor_tensor(out=OT[j][:, g * 512:(g + 1) * 512], in0=ob[0:64, :], in1=bcs[:],
                                                            op=ALU.mult), reads=[ob, bcs], writes=[OT[j]])
                    xi = xs[qb % 3]
                    xo = xn[j]
                    for half in range(2):
                        pk = PB[half]
                        op("pe", lambda e: [e.matmul(pk[:], lhsT=OT[j][:, h * 128:(h + 1) * 128],
                                                     rhs=wo[:, h * D + half * 512:h * D + half * 512 + 512],
                                                     start=(h == 0), stop=(h == 15)) for h in range(16)],
                           reads=[OT[j], wo], writes=[pk])
                        op("dve", lambda e: e.tensor_tensor(out=xo[:, half * 512:(half + 1) * 512], in0=pk[:],
                                                            in1=xi[:, half * 512:(half + 1) * 512], op=ALU.add),
                           reads=[pk, xi], writes=[xo])
                    dma("sp", dst[qb * 128:(qb + 1) * 128, :], xo[:], reads=[xo], writes=[ddst], sem=f"so{j}")
                SC.barrier()

        def phase_xattn(l, src, dsrc, dst, ddst):
            with ExitStack() as ph:
                wq = sbuf(ph, "wq", [128, 8 * D], BF16)
                wkv = sbuf(ph, "wkv", [128, 8 * 2 * D], BF16)
                wo = sbuf(ph, "wo_x", [128, 8 * D], BF16)
                KmT = sbuf(ph, "KmT", [128, 8 * MEM], BF16)
                Vm = sbuf(ph, "Vm", [128, 2 * D], BF16)
                gain = sbuf(ph, "gain_x", [128, D])
                mgain = sbuf(ph, "mgain", [128, D])
                xs = [sbuf(ph, f"xs{i}", [128, D]) for i in range(8)]
                hb = [sbuf(ph, f"hb{i}", [128, D], BF16) for i in range(2)]
                hT = sbuf(ph, "hT", [128, 8 * 512], BF16)
                memT = sbuf(ph, "memT", [128, 8 * MEM], BF16)
                qT = sbuf(ph, "qT", [128, 8 * 512], BF16)
                Pb = [sbuf(ph, f"Pb{i}", [128, 512], BF16) for i in range(4)]
                OT = sbuf(ph, "OT", [128, 8 * 512], BF16)
                rden = [sbuf(ph, f"rden{i}", [128, 512]) for i in range(2)]
                xn = [sbuf(ph, f"xn{i}", [128, D]) for i in range(2)]
                sq = sbuf(ph, "sq", [128, D])
                ss1 = sbuf(ph, "ss1", [128, 1]); lnv1 = sbuf(ph, "lnv1", [128, 1]); rstd1 = sbuf(ph, "rstd1", [128, 1])

                load_weight_bf16(wkv, xattn_w_kv[l], 8, 2 * D, "w0", col_split=2)
                load_weight_bf16(wq, xattn_w_q[l], 8, D, "w1")
                load_weight_bf16(wo, xattn_w_o[l], 8, D, "w2")
                dma("sp", gain[:], bc_part(xattn_norm[l:l + 1, :]), writes=[gain], sem="c0")
                dma("sp", mgain[:], bc_part(mem_norm[l:l + 1, :]), writes=[mgain], sem="c0")

                for mt in range(2):
                    dma("sp", xs[mt][:], mem_in[mt * 128:(mt + 1) * 128, :], writes=[xs[mt]], sem=f"lx{mt}")
                for mt in range(2):
                    norm_to_bf16(xs[mt], mgain, hb[mt], sq, ss1, lnv1, rstd1)
                    transpose_h(hb[mt], PT[mt])
                    op("dve", lambda e, mt=mt: e.tensor_copy(
                        out=memT[:].rearrange("p (c m) -> p c m", c=8)[:, :, mt * 128:(mt + 1) * 128],
                        in_=PT[mt][:].rearrange("p (c m) -> p c m", c=8)), reads=[PT[mt]], writes=[memT])
                for dc in range(8):
                    pk = PB[dc % 2]
                    op("pe", lambda e, pk=pk, dc=dc: [e.matmul(pk[:, 0:MEM], lhsT=wkv[:, c * 2048 + dc * 128:c * 2048 + (dc + 1) * 128],
                                                               rhs=memT[:, c * MEM:(c + 1) * MEM], start=(c == 0), stop=(c == 7))
                                                      for c in range(8)], reads=[wkv, memT], writes=[pk])
                    op("dve", lambda e, pk=pk, dc=dc: e.tensor_copy(out=KmT[:, dc * MEM:(dc + 1) * MEM], in_=pk[:, 0:MEM]),
                       reads=[pk], writes=[KmT])
                for mt in range(2):
                    for half in range(2):
                        pk = PB[2 + (mt * 2 + half) % 2]
                        op("pe", lambda e, pk=pk, mt=mt, half=half: [
                            e.matmul(pk[:], lhsT=memT[:, c * MEM + mt * 128:c * MEM + (mt + 1) * 128],
                                     rhs=wkv[:, c * 2048 + 1024 + half * 512:c * 2048 + 1024 + (half + 1) * 512],
                                     start=(c == 0), stop=(c == 7)) for c in range(8)], reads=[wkv, memT], writes=[pk])
                        op("act", lambda e, pk=pk, mt=mt, half=half: e.activation(
                            out=Vm[:, mt * D + half * 512:mt * D + (half + 1) * 512], in_=pk[:], func=AF.Copy),
                           reads=[pk], writes=[Vm])

                def load_blk(blk):
                    for jj in range(4):
                        t = blk * 4 + jj
                        i = (blk % 2) * 4 + jj
                        dma("sp", xs[i][:], src[t * 128:(t + 1) * 128, :], reads=[dsrc], writes=[xs[i]], sem=f"lx{i}")

                load_blk(0)
                for blk in range(NB):
                    if blk + 1 < NB:
                        load_blk(blk + 1)
                    for jj in range(4):
                        i = (blk % 2) * 4 + jj
                        k = jj % 2
                        norm_to_bf16(xs[i], gain, hb[k], sq, ss1, lnv1, rstd1)
                        transpose_h(hb[k], PT[k])
                        op("dve", lambda e, k=k, jj=jj: e.tensor_copy(
                            out=hT[:].rearrange("p (c m) -> p c m", c=8)[:, :, jj * 128:(jj + 1) * 128],
                            in_=PT[k][:].rearrange("p (c m) -> p c m", c=8)), reads=[PT[k]], writes=[hT])
                    for dc in range(8):
                        pk = PB[dc % 2]
                        op("pe", lambda e, pk=pk, dc=dc: [e.matmul(pk[:], lhsT=wq[:, c * D + dc * 128:c * D + (dc + 1) * 128],
                                                                   rhs=hT[:, c * 512:(c + 1) * 512], start=(c == 0), stop=(c == 7))
                                                          for c in range(8)], reads=[wq, hT], writes=[pk])
                        op("act", lambda e, pk=pk, dc=dc: e.activation(out=qT[:, dc * 512:(dc + 1) * 512], in_=pk[:],
                                                                       func=AF.Copy, scale=1.0 / 16.0), reads=[pk], writes=[qT])
                    for h in range(4):
                        pbs = [Pb[(h % 2) * 2 + m] for m in range(2)]
                        for m in range(2):
                            sbk = PB[2 + m]
                            op("pe", lambda e, sbk=sbk, m=m: [
                                e.matmul(sbk[:], lhsT=KmT[:, (2 * h + jd) * MEM + m * 128:(2 * h + jd) * MEM + (m + 1) * 128],
                                         rhs=qT[:, (2 * h + jd) * 512:(2 * h + jd + 1) * 512], start=(jd == 0), stop=(jd == 1))
                                for jd in range(2)], reads=[KmT, qT], writes=[sbk])
                            op("act", lambda e, sbk=sbk, m=m: e.activation(out=pbs[m][:], in_=sbk[:], func=AF.Exp),
                               reads=[sbk], writes=[pbs[m]])
                        dbk = PB[4]
                        rd = rden[h % 2]
                        op("pe", lambda e: [e.matmul(dbk[:], lhsT=ones_b[:], rhs=pbs[m][:], start=(m == 0), stop=(m == 1))
                                            for m in range(2)], reads=[ones_b, pbs[0], pbs[1]], writes=[dbk])
                        op("dve", lambda e: e.reciprocal(out=rd[:], in_=dbk[:]), reads=[dbk], writes=[rd])
                        for jd in range(2):
                            obk = PB[5] if jd else PB[0]
                            op("pe", lambda e, obk=obk, jd=jd: [
                                e.matmul(obk[:], lhsT=Vm[:, m * D + h * 256 + jd * 128:m * D + h * 256 + (jd + 1) * 128],
                                         rhs=pbs[m][:], start=(m == 0), stop=(m == 1)) for m in range(2)],
                               reads=[Vm, pbs[0], pbs[1]], writes=[obk])
                            op("dve", lambda e, obk=obk, jd=jd: e.tensor_tensor(
                                out=OT[:, (2 * h + jd) * 512:(2 * h + jd + 1) * 512], in0=obk[:], in1=rd[:], op=ALU.mult),
                               reads=[obk, rd], writes=[OT])
                    for jj in range(4):
                        t = blk * 4 + jj
                        xi = xs[(blk % 2) * 4 + jj]
                        xo = xn[jj % 2]
                        for half in range(2):
                            pk = PB[half + 1]
                            op("pe", lambda e, pk=pk, half=half: [
                                e.matmul(pk[:], lhsT=OT[:, dc * 512 + jj * 128:dc * 512 + (jj + 1) * 128],
                                         rhs=wo[:, dc * D + half * 512:dc * D + (half + 1) * 512], start=(dc == 0), stop=(dc == 7))
                                for dc in range(8)], reads=[OT, wo], writes=[pk])
                            op("dve", lambda e, pk=pk, half=half: e.tensor_tensor(
                                out=xo[:, half * 512:(half + 1) * 512], in0=pk[:], in1=xi[:, half * 512:(half + 1) * 512],
                                op=ALU.add), reads=[pk, xi], writes=[xo])
                        dma("sp", dst[t * 128:(t + 1) * 128, :], xo[:], reads=[xo], writes=[ddst], sem=f"so{jj % 2}")
                SC.barrier()

        def phase_ffn(l, src, dsrc, dst, ddst, final):
            with ExitStack() as ph:
                wup = sbuf(ph, "wup", [128, 8 * 2 * DFF], BF16)
                wd = sbuf(ph, "wd", [128, 22 * D], BF16)
                gain = sbuf(ph, "gain_f", [128, D])
                fgain = sbuf(ph, "fgain", [128, D]) if final else None
                cp = sbuf(ph, "cp", [128, 4 * NCH])
                xs = [sbuf(ph, f"xs{i}", [128, D]) for i in range(2)]
                xr = [sbuf(ph, f"xr{i}", [128, D]) for i in range(2)]
                hb = sbuf(ph, "hb", [128, D], BF16)
                hT = sbuf(ph, "hT", [128, 8 * 512], BF16)
                hTh = sbuf(ph, "hTh", [128, 8 * 16], BF16)
                uh = sbuf(ph, "uh", [128, NCH * 16])
                G = sbuf(ph, "G", [128, 22 * 512], BF16)
                accg = [sbuf(ph, f"accg{i}", [128, 512]) for i in range(2)]
                accv = [sbuf(ph, f"accv{i}", [128, 512]) for i in range(2)]
                sq = sbuf(ph, "sq", [128, D])
                cpr = sq
                xh = xr[0]
                ss1 = sbuf(ph, "ss1", [128, 1]); lnv1 = sbuf(ph, "lnv1", [128, 1]); rstd1 = sbuf(ph, "rstd1", [128, 1])

                load_weight_bf16(wup, ffn_w_up[l], 8, 2 * DFF, "w0", col_split=4)
                load_weight_bf16(wd, ffn_w_down[l], 22, D, "w1", col_split=2)
                dma("sp", gain[:], bc_part(ffn_norm[l:l + 1, :]), writes=[gain], sem="c0")
                if final:
                    dma("sp", fgain[:], bc_part(final_norm), writes=[fgain], sem="c0")
                for k in range(3):
                    dma("sp", cpr[0:NCH, k * 128:(k + 1) * 128], ffn_conv_w[l, k].rearrange("(c p) -> c p", p=128),
                        writes=[cpr], sem="c1")
                dma("sp", cpr[0:NCH, 384:512], ffn_conv_b[l].rearrange("(c p) -> c p", p=128), writes=[cpr], sem="c1")
                op("pe", lambda e: [e.transpose(out=PB[0][:, k * NCH:(k + 1) * NCH], in_=cpr[0:NCH, k * 128:(k + 1) * 128],
                                                identity=identf[0:NCH, 0:NCH]) for k in range(4)],
                   reads=[cpr, identf], writes=[PB[0]])
                op("dve", lambda e: e.tensor_copy(out=cp[:], in_=PB[0][:, 0:4 * NCH]), reads=[PB[0]], writes=[cp])

                def cw(k, ch):
                    return cp[:, k * NCH + ch:k * NCH + ch + 1]

                op("dve", lambda e: e.memset(xh[0:16, :], 0.0), writes=[xh])
                for m in range(NB - 1):
                    dma("sp", xh[2 * m:2 * m + 2, :], src[512 * (m + 1) - 1:512 * (m + 1) + 1, :], reads=[dsrc], writes=[xh],
                        sem="lh")
                op("dve", lambda e: e.memset(uh[:], 0.0), writes=[uh])
                if NB > 1:
                    norm_to_bf16(xh, gain, hb, sq, ss1, lnv1, rstd1, npart=16)
                    transpose_h(hb, PT[0], npart=16)
                    op("dve", lambda e: e.tensor_copy(out=hTh[:], in_=PT[0][:, 0:128]), reads=[PT[0]], writes=[hTh])
                    for hh in range(2):
                        pk = PB[1 + hh]
                        op("pe", lambda e, pk=pk, hh=hh: [
                            e.matmul(pk[:, cc * 16:(cc + 1) * 16], lhsT=wup[:, c * 2 * DFF + (hh * 22 + cc) * 128:c * 2 * DFF + (hh * 22 + cc + 1) * 128],
                                     rhs=hTh[:, c * 16:(c + 1) * 16], start=(c == 0), stop=(c == 7))
                            for cc in range(22) for c in range(8)], reads=[wup, hTh], writes=[pk])
                        op("dve", lambda e, pk=pk, hh=hh: e.tensor_copy(out=uh[:, hh * 352:(hh + 1) * 352], in_=pk[:, 0:352]),
                           reads=[pk], writes=[uh])
                    uh4 = uh[:].rearrange("p (c m two) -> p c m two", c=NCH, two=2)
                    op("dve", lambda e: e.tensor_tensor(out=uh4[:, :, :, 0], in0=uh4[:, :, :, 0],
                                                        in1=bc_last(cp[:, 0:NCH], 8), op=ALU.mult), reads=[uh, cp], writes=[uh])
                    op("dve", lambda e: e.tensor_tensor(out=uh4[:, :, :, 1], in0=uh4[:, :, :, 1],
                                                        in1=bc_last(cp[:, 2 * NCH:3 * NCH], 8), op=ALU.mult), reads=[uh, cp], writes=[uh])

                def conv_chunk(blk, ch, ub, acc):
                    op("act", lambda e: e.activation(out=acc[:], in_=ub[:], func=AF.Identity, scale=cw(1, ch), bias=cw(3, ch)),
                       reads=[ub, cp], writes=[acc])
                    op("dve", lambda e: e.scalar_tensor_tensor(out=acc[:, 1:512], in0=ub[:, 0:511], scalar=cw(0, ch),
                                                               in1=acc[:, 1:512], op0=ALU.mult, op1=ALU.add),
                       reads=[ub, cp, acc], writes=[acc])
                    op("dve", lambda e: e.scalar_tensor_tensor(out=acc[:, 0:511], in0=ub[:, 1:512], scalar=cw(2, ch),
                                                               in1=acc[:, 0:511], op0=ALU.mult, op1=ALU.add),
                       reads=[ub, cp, acc], writes=[acc])
                    if blk > 0:
                        li = ch * 16 + 2 * (blk - 1)
                        op("dve", lambda e: e.tensor_tensor(out=acc[:, 0:1], in0=acc[:, 0:1], in1=uh[:, li:li + 1], op=ALU.add),
                           reads=[acc, uh], writes=[acc])
                    if blk < NB - 1:
                        ri = ch * 16 + 2 * blk + 1
                        op("dve", lambda e: e.tensor_tensor(out=acc[:, 511:512], in0=acc[:, 511:512], in1=uh[:, ri:ri + 1], op=ALU.add),
                           reads=[acc, uh], writes=[acc])

                def load_x(t, i):
                    dma("sp", xs[i][:], src[t * 128:(t + 1) * 128, :], reads=[dsrc], writes=[xs[i]], sem=f"lx{i}")

                load_x(0, 0)
                for blk in range(NB):
                    for jj in range(4):
                        t = blk * 4 + jj
                        if t + 1 < NT:
                            load_x(t + 1, (t + 1) % 2)
                        norm_to_bf16(xs[t % 2], gain, hb, sq, ss1, lnv1, rstd1)
                        transpose_h(hb, PT[jj % 2])
                        op("dve", lambda e, jj=jj: e.tensor_copy(
                            out=hT[:].rearrange("p (c m) -> p c m", c=8)[:, :, jj * 128:(jj + 1) * 128],
                            in_=PT[jj % 2][:].rearrange("p (c m) -> p c m", c=8)), reads=[PT[jj % 2]], writes=[hT])
                    for g in range(22):
                        k = g % 2
                        ug, uv = PB[2 * k], PB[2 * k + 1]
                        for (ub, ch) in ((ug, g), (uv, 22 + g)):
                            op("pe", lambda e, ub=ub, ch=ch: [
                                e.matmul(ub[:], lhsT=wup[:, c * 2 * DFF + ch * 128:c * 2 * DFF + (ch + 1) * 128],
                                         rhs=hT[:, c * 512:(c + 1) * 512], start=(c == 0), stop=(c == 7)) for c in range(8)],
                               reads=[wup, hT], writes=[ub])
                        conv_chunk(blk, g, ug, accg[k])
                        conv_chunk(blk, 22 + g, uv, accv[k])
                        op("act", lambda e, k=k: e.activation(out=accg[k][:], in_=accg[k][:], func=AF.Silu),
                           reads=[accg[k]], writes=[accg[k]])
                        op("pool", lambda e, k=k, g=g: e.tensor_tensor(out=G[:, g * 512:(g + 1) * 512], in0=accg[k][:], in1=accv[k][:],
                                                                       op=ALU.mult), reads=[accg[k], accv[k]], writes=[G])
                    for jj in range(4):
                        t = blk * 4 + jj
                        xi = xr[jj % 2]
                        dma("sp", xi[:], src[t * 128:(t + 1) * 128, :], reads=[dsrc], writes=[xi], sem=f"lr{jj % 2}")
                        for half in range(2):
                            pk = PB[4 + half]
                            op("pe", lambda e, pk=pk, half=half, jj=jj: [
                                e.matmul(pk[:], lhsT=G[:, g * 512 + jj * 128:g * 512 + (jj + 1) * 128],
                                         rhs=wd[:, g * D + half * 512:g * D + (half + 1) * 512], start=(g == 0), stop=(g == 21))
                                for g in range(22)], reads=[G, wd], writes=[pk])
                            op("dve", lambda e, pk=pk, half=half, xi=xi: e.tensor_tensor(
                                out=xi[:, half * 512:(half + 1) * 512], in0=pk[:], in1=xi[:, half * 512:(half + 1) * 512],
                                op=ALU.add), reads=[pk, xi], writes=[xi])
                        if final:
                            rms_stats(xi[:, :], xi, 128, sq, ss1, lnv1, rstd1)
                            op("dve", lambda e, xi=xi: e.scalar_tensor_tensor(out=xi[:], in0=xi[:], scalar=rstd1[:, 0:1],
                                                                              in1=fgain[:], op0=ALU.mult, op1=ALU.mult),
                               reads=[xi, rstd1, fgain], writes=[xi])
                        dma("sp", dst[t * 128:(t + 1) * 128, :], xi[:], reads=[xi], writes=[ddst], sem=f"so{jj % 2}")
                SC.barrier()

        def phase_pool(src, dsrc, dst, ddst):
            PADW = S + 16
            with ExitStack() as ph:
                HT = sbuf(ph, "HT", [128, 8 * PADW], BF16)
                wp = sbuf(ph, "wp", [128, 8 * 256], BF16)
                gain = sbuf(ph, "gain_p", [128, D])
                pscale = sbuf(ph, "pscale", [128, D])
                invc = sbuf(ph, "invc", [128, 64])
                xs = [sbuf(ph, f"xs{i}", [128, D]) for i in range(3)]
                hb = [sbuf(ph, f"hb{i}", [128, D], BF16) for i in range(2)]
                Wa = [sbuf(ph, f"Wa{i}", [128, PADW]) for i in range(2)]
                Wb = [sbuf(ph, f"Wb{i}", [128, PADW]) for i in range(2)]
                edge = [sbuf(ph, f"edge{i}", [128, 16]) for i in range(2)]
                edgeb = [sbuf(ph, f"edgeb{i}", [128, 16], BF16) for i in range(2)]
                xn = [sbuf(ph, f"xn{i}", [128, D]) for i in range(2)]
                sq = sbuf(ph, "sq", [128, D])
                ss1 = sbuf(ph, "ss1", [128, 1]); lnv1 = sbuf(ph, "lnv1", [128, 1]); rstd1 = sbuf(ph, "rstd1", [128, 1])

                dma("pool", wp[:].rearrange("p (c n) -> p c n", c=8), pool_w[0].rearrange("g (j p) n -> p (g j) n", p=128),
                    writes=[wp], sem="w0")
                dma("sp", gain[:], bc_part(pool_norm), writes=[gain], sem="c0")
                dma("sp", pscale[:], bc_part(pool_scale), writes=[pscale], sem="c0")
                dma("sp", invc[:], bc_part(invc_in), writes=[invc], sem="c0")
                H3 = HT[:].rearrange("p (c w) -> p c w", c=8)
                op("pool", lambda e: e.memset(H3[:, :, 0:8], 0.0), writes=[HT])
                op("pool", lambda e: e.memset(H3[:, :, 8 + S:16 + S], 0.0), writes=[HT])

                def load_x(t, i):
                    dma("sp", xs[i][:], src[t * 128:(t + 1) * 128, :], reads=[dsrc], writes=[xs[i]], sem=f"lx{i}")

                load_x(0, 0)
                for t in range(NT):
                    if t + 1 < NT:
                        load_x(t + 1, (t + 1) % 3)
                    k = t % 2
                    norm_to_bf16(xs[t % 3], gain, hb[k], sq, ss1, lnv1, rstd1)
                    transpose_h(hb[k], PT[k])
                    op("act", lambda e, k=k, t=t: e.activation(out=H3[:, :, 8 + t * 128:8 + (t + 1) * 128],
                                                               in_=PT[k][:].rearrange("p (c m) -> p c m", c=8), func=AF.Copy),
                       reads=[PT[k]], writes=[HT])
                for c in range(8):
                    grp = c // 2
                    eng = "dve" if c % 2 == 0 else "pool"
                    Hc = HT[:, c * PADW:(c + 1) * PADW]
                    A_, B_ = Wa[c % 2], Wb[c % 2]
                    ed, edb = edge[c % 2], edgeb[c % 2]
                    op(eng, lambda e: e.tensor_tensor(out=A_[:, 1:PADW], in0=Hc[:, 0:PADW - 1], in1=Hc[:, 1:PADW], op=ALU.add),
                       reads=[HT], writes=[A_])
                    cur, other = A_, B_
                    lo, hi = 1, PADW
                    sh = 1
                    for lvl in range(grp):
                        nlo, nhi = lo + sh, hi - sh
                        op(eng, lambda e, cur=cur, other=other, nlo=nlo, nhi=nhi, sh=sh: e.tensor_tensor(
                            out=other[:, nlo:nhi], in0=cur[:, nlo - sh:nhi - sh], in1=cur[:, nlo + sh:nhi + sh], op=ALU.add),
                           reads=[cur], writes=[other])
                        cur, other = other, cur
                        lo, hi = nlo, nhi
                        sh *= 2
                    w = 2 ** (grp + 1)
                    cur_e = bass.AP(cur.t[:, 8:16].tensor, cur.t[:, 8:16].offset, [list(cur.t[:, 8:16].ap[0]), [S - 8, 2], [1, 8]])
                    h_e = bass.AP(Hc[:, 8:16].tensor, Hc[:, 8:16].offset, [list(Hc[:, 8:16].ap[0]), [S - 8, 2], [1, 8]])
                    op(eng, lambda e: e.tensor_tensor(out=ed[:].rearrange("p (a b) -> p a b", a=2), in0=cur_e,
                                                      in1=invc[:, grp * 16:(grp + 1) * 16].rearrange("p (a b) -> p a b", a=2),
                                                      op=ALU.mult), reads=[cur, invc], writes=[ed])
                    op(eng, lambda e: e.tensor_tensor(out=edb[:].rearrange("p (a b) -> p a b", a=2),
                                                      in0=ed[:].rearrange("p (a b) -> p a b", a=2), in1=h_e, op=ALU.subtract),
                       reads=[ed, HT], writes=[edb])
                    if eng == "dve":
                        op(eng, lambda e: e.scalar_tensor_tensor(out=Hc[:, 8:8 + S], in0=cur[:, 8:8 + S], scalar=1.0 / w,
                                                                 in1=Hc[:, 8:8 + S], op0=ALU.mult, op1=ALU.subtract),
                           reads=[cur, HT], writes=[HT])
                    else:
                        op(eng, lambda e: e.tensor_scalar(out=cur[:, 8:8 + S], in0=cur[:, 8:8 + S], scalar1=1.0 / w, scalar2=None,
                                                          op0=ALU.mult), reads=[cur], writes=[cur])
                        op(eng, lambda e: e.tensor_tensor(out=Hc[:, 8:8 + S], in0=cur[:, 8:8 + S], in1=Hc[:, 8:8 + S],
                                                          op=ALU.subtract), reads=[cur, HT], writes=[HT])
                    op(eng, lambda e: e.tensor_copy(out=h_e, in_=edb[:].rearrange("p (a b) -> p a b", a=2)),
                       reads=[edb], writes=[HT])
                load_x(0, 0)
                for t in range(NT):
                    if t + 1 < NT:
                        load_x(t + 1, (t + 1) % 3)
                    xi = xs[t % 3]
                    xo = xn[t % 2]
                    for half in range(2):
                        pk = PB[(t % 2) * 2 + half]
                        op("pe", lambda e, pk=pk, half=half, t=t: [
                            e.matmul(pk[:, gg * 256:(gg + 1) * 256],
                                     lhsT=HT[:, (4 * half + 2 * gg + jc) * PADW + 8 + t * 128:(4 * half + 2 * gg + jc) * PADW + 8 + (t + 1) * 128],
                                     rhs=wp[:, (4 * half + 2 * gg + jc) * 256:(4 * half + 2 * gg + jc + 1) * 256],
                                     start=(jc == 0), stop=(jc == 1)) for gg in range(2) for jc in range(2)],
                           reads=[HT, wp], writes=[pk])
                        op("dve", lambda e, pk=pk, half=half: e.tensor_tensor(
                            out=xo[:, half * 512:(half + 1) * 512], in0=pk[:], in1=pscale[:, half * 512:(half + 1) * 512],
                            op=ALU.mult), reads=[pk, pscale], writes=[xo])
                        op("dve", lambda e, half=half: e.tensor_tensor(
                            out=xo[:, half * 512:(half + 1) * 512], in0=xo[:, half * 512:(half + 1) * 512],
                            in1=xi[:, half * 512:(half + 1) * 512], op=ALU.add), reads=[xo, xi], writes=[xo])
                    dma("sp", dst[t * 128:(t + 1) * 128, :], xo[:], reads=[xo], writes=[ddst], sem=f"so{t % 2}")
                SC.barrier()

        SC.barrier()
        plan = [
            ("attn", lambda s, ds, d, dd: phase_attention(s, ds, d, dd)),
            ("xattn0", lambda s, ds, d, dd: phase_xattn(0, s, ds, d, dd)),
            ("ffn0", lambda s, ds, d, dd: phase_ffn(0, s, ds, d, dd, False)),
            ("pool", lambda s, ds, d, dd: phase_pool(s, ds, d, dd)),
            ("xattn1", lambda s, ds, d, dd: phase_xattn(1, s, ds, d, dd)),
            ("ffn1", lambda s, ds, d, dd: phase_ffn(1, s, ds, d, dd, True)),
        ]
        if stop_after is not None:
            plan = plan[:stop_after]
        cur, dcur = x_in, dX
        for i, (name, fn) in enumerate(plan):
            last = i == len(plan) - 1
            if last:
                nxt, dnxt = out, dO
            else:
                nxt, dnxt = (rA, dA) if i % 2 == 0 else (rB, dB)
            fn(cur, dcur, nxt, dnxt)
            cur, dcur = nxt, dnxt
        SC.barrier()
        build.stats = (SC.nops, SC.nwaits)
    return nc


def rope_table(S):
    t = np.arange(S)
    row = (t // 64).astype(np.float32)
    col = (t % 64).astype(np.float32)
    inv_freq = (10000.0 ** (-np.arange(16, dtype=np.float32) / 16)).astype(np.float32)
    ang = np.stack([row[:, None] * inv_freq, col[:, None] * inv_freq], axis=1).astype(np.float32)
    c, s = np.cos(ang).astype(np.float32), np.sin(ang).astype(np.float32)
    cos64 = np.concatenate([c[:, 0], c[:, 0], c[:, 1], c[:, 1]], axis=1)
    return np.ascontiguousarray(np.concatenate([cos64, -s.reshape(S, 32), s.reshape(S, 32)], axis=1).astype(np.float32))


def invc_table(S):
    tab = np.zeros((4, 16), np.float32)
    toks = np.concatenate([np.arange(8), np.arange(S - 8, S)])
    for g, w in enumerate((2, 4, 8, 16)):
        lo = np.clip(toks - w // 2, 0, S)
        hi = np.clip(toks + w - w // 2, 0, S)
        tab[g] = 1.0 / (hi - lo).astype(np.float32)
    return tab.reshape(1, 64)


_NC_CACHE = {}


def kernel(**inputs):
    S = inputs["x"].shape[1]
    B = inputs["x"].shape[0]
    if S not in _NC_CACHE:
        _NC_CACHE[S] = build(S)
    nc = _NC_CACHE[S]
    shared = {k: np.ascontiguousarray(np.asarray(v, dtype=np.float32)) for k, v in inputs.items() if k not in ("x", "mem")}
    shared["final_norm"] = shared["final_norm"].reshape(1, D)
    shared["rope"] = rope_table(S)
    shared["invc"] = invc_table(S)
    in_maps = []
    for b in range(B):
        m = dict(shared)
        m["x"] = np.ascontiguousarray(np.asarray(inputs["x"][b], dtype=np.float32))
        m["mem"] = np.ascontiguousarray(np.asarray(inputs["mem"][b], dtype=np.float32))
        in_maps.append(m)
    res = run_bass_kernel_spmd(nc, in_maps, core_ids=list(range(B)))
    return np.stack([np.asarray(r["out"], dtype=np.float32) for r in res.results], axis=0)
```

```python
import numpy as np
from contextlib import ExitStack
import concourse.bass as bass
import concourse.mybir as mybir
from concourse.bass_utils import run_bass_kernel_spmd

F32 = mybir.dt.float32
BF16 = mybir.dt.bfloat16
AF = mybir.ActivationFunctionType
ALU = mybir.AluOpType
AX = mybir.AxisListType

D = 1024
DFF = 2816
NCH = 44
EPS = 1e-6
MEM = 256


class Buf:
    __slots__ = ("name", "w", "r")

    def __init__(self, name):
        self.name = name
        self.w = {}
        self.r = {}


class T:
    __slots__ = ("t", "b")

    def __init__(self, t, name):
        self.t = t
        self.b = Buf(name)

    def __getitem__(self, k):
        return self.t[k]


class Sched:
    def __init__(self, nc, stack):
        self.nc = nc
        self.stack = stack
        self.engs = {"pe": nc.tensor, "act": nc.scalar, "dve": nc.vector, "pool": nc.gpsimd, "sp": nc.sync}
        self.sems = {}
        self.cnt = {}
        self.seen = {}
        self.nops = 0
        self.nwaits = 0
        for k in ("pe", "act", "dve", "pool"):
            self._sem(k)

    def _sem(self, name):
        if name not in self.sems:
            self.sems[name] = self.stack.enter_context(self.nc.semaphore("s_" + name))
            self.cnt[name] = 0
        return self.sems[name]

    def _deps(self, eng, reads, writes):
        deps = {}
        for t in reads:
            for s, v in t.b.w.items():
                if deps.get(s, 0) < v:
                    deps[s] = v
        for t in writes:
            for d in (t.b.w, t.b.r):
                for s, v in d.items():
                    if s == eng:
                        continue
                    if deps.get(s, 0) < v:
                        deps[s] = v
        return deps

    def _wait(self, eng, deps):
        e = self.engs[eng]
        for s, v in deps.items():
            if s not in ("pe", "act", "dve", "pool"):
                v = self.cnt[s]
            if self.seen.get((eng, s), 0) < v:
                e.wait_ge(self.sems[s], v)
                self.seen[(eng, s)] = v
                self.nwaits += 1

    def op(self, eng, fn, reads=(), writes=()):
        self._wait(eng, self._deps(eng, reads, writes))
        ins = fn(self.engs[eng])
        if isinstance(ins, (list, tuple)):
            ins = ins[-1]
        self.cnt[eng] += 1
        v = self.cnt[eng]
        ins.then_inc(self.sems[eng], 1)
        for t in reads:
            t.b.r[eng] = v
        for t in writes:
            t.b.w[eng] = v
        self.nops += 1

    def dma(self, queue, out, in_, reads=(), writes=(), sem=None, **kw):
        self._sem(sem)
        self._wait(queue, self._deps("__dma__", reads, writes))
        ins = self.engs[queue].dma_start(out=out, in_=in_, **kw)
        self.cnt[sem] += 16
        v = self.cnt[sem]
        ins.then_inc(self.sems[sem], 16)
        for t in reads:
            t.b.r[sem] = v
        for t in writes:
            t.b.w[sem] = v
        self.nops += 1

    def barrier(self):
        deps = {s: v for s, v in self.cnt.items() if v > 0}
        for eng in ("sp", "pe", "act", "dve", "pool"):
            self._wait(eng, deps)


def bc_last(ap, n):
    return bass.AP(ap.tensor, ap.offset, [list(ap.ap[0]), list(ap.ap[1]), [0, n]])


def bc_mid(ap, n):
    return bass.AP(ap.tensor, ap.offset, [list(ap.ap[0]), [0, n]] + [list(a) for a in ap.ap[1:]])


def bc_part(ap, n=128):
    return bass.AP(ap.tensor, ap.offset, [[0, n]] + [list(a) for a in ap.ap[1:]])


def build(S=4096, nlayers_dbg=None, stop_after=None):
    NT = S // 128
    NB = S // 512
    nc = bass.Bass("TRN2", target_bir_lowering=False)

    def din(name, shape):
        return nc.dram_tensor(name, list(shape), F32, kind="ExternalInput").ap()

    x_in = din("x", [S, D])
    mem_in = din("mem", [MEM, D])
    attn_norm = din("attn_norm", [1, D])
    attn_w_qkv = din("attn_w_qkv", [1, D, 1536])
    attn_q_gain = din("attn_q_gain", [1, 64])
    attn_k_gain = din("attn_k_gain", [1, 64])
    attn_w_o = din("attn_w_o", [1, D, D])
    pool_norm = din("pool_norm", [1, D])
    pool_w = din("pool_w", [1, 4, 256, 256])
    pool_scale = din("pool_scale", [1, D])
    xattn_norm = din("xattn_norm", [2, D])
    mem_norm = din("mem_norm", [2, D])
    xattn_w_q = din("xattn_w_q", [2, D, D])
    xattn_w_kv = din("xattn_w_kv", [2, D, 2 * D])
    xattn_w_o = din("xattn_w_o", [2, D, D])
    ffn_norm = din("ffn_norm", [2, D])
    ffn_w_up = din("ffn_w_up", [2, D, 2 * DFF])
    ffn_conv_w = din("ffn_conv_w", [2, 3, 2 * DFF])
    ffn_conv_b = din("ffn_conv_b", [2, 2 * DFF])
    ffn_w_down = din("ffn_w_down", [2, DFF, D])
    final_norm = din("final_norm", [1, D])
    rope_in = din("rope", [S, 128])
    invc_in = din("invc", [1, 64])
    out = nc.dram_tensor("out", [S, D], F32, kind="ExternalOutput").ap()
    rA = nc.dram_tensor("resA", [S, D], F32, kind="Internal").ap()
    rB = nc.dram_tensor("resB", [S, D], F32, kind="Internal").ap()
    dA = T(None, "resA")
    dB = T(None, "resB")
    dX = T(None, "x")
    dO = T(None, "out")

    with ExitStack() as st:
        SC = Sched(nc, st)
        op = SC.op
        dma = SC.dma

        uid = [0]

        def sbuf(stack, name, shape, dt=F32):
            uid[0] += 1
            name = f"{name}_{uid[0]}"
            return T(stack.enter_context(nc.sbuf_tensor(name, list(shape), dt)), name)

        def psum(stack, name, shape, dt=F32):
            return T(stack.enter_context(nc.psum_tensor(name, list(shape), dt)), name)

        ident = sbuf(st, "ident", [128, 128], BF16)
        identf = sbuf(st, "identf", [128, 128], F32)
        ones_f = sbuf(st, "ones_f", [128, 128], F32)
        ones_b = sbuf(st, "ones_b", [128, 128], BF16)
        for idt in (ident, identf):
            op("pool", lambda e, idt=idt: e.memset(idt[:], 0.0), writes=[idt])
            op("pool", lambda e, idt=idt: e.affine_select(out=idt[:], in_=idt[:], pattern=[[-1, 128]],
                                                         compare_op=ALU.not_equal, fill=1.0, base=0,
                                                         channel_multiplier=1), reads=[idt], writes=[idt])
        op("dve", lambda e: e.memset(ones_f[:], 1.0), writes=[ones_f])
        op("dve", lambda e: e.memset(ones_b[:], 1.0), writes=[ones_b])

        def alloc_psum(stack, nf32):
            uid[0] += 1
            pb = [psum(stack, f"pb{i}_{uid[0]}", [128, 512], F32) for i in range(nf32)]
            pt = [psum(stack, f"pt{i}_{uid[0]}", [128, 1024], BF16) for i in range(8 - nf32)]
            return pb, pt

        def load_weight_bf16(w_t, src2d, nchunk, ncols, sem, col_split=1, p=128):
            dst3 = w_t[:].rearrange("p (c n) -> p c n", c=nchunk)
            src3 = src2d.rearrange("(c p) n -> p c n", p=p)
            step = ncols // col_split
            for i in range(col_split):
                dma("pool", dst3[:, :, i * step:(i + 1) * step], src3[:, :, i * step:(i + 1) * step],
                    writes=[w_t], sem=sem)

        def rms_stats(xt_ap, xt, npart, sq, ss, lnv, rstd, width=D, out_bias=0.0):
            op("dve", lambda e: e.tensor_tensor(out=sq[0:npart, 0:width], in0=xt_ap, in1=xt_ap, op=ALU.mult),
               reads=[xt], writes=[sq])
            op("dve", lambda e: e.tensor_reduce(out=ss[0:npart, 0:1], in_=sq[0:npart, 0:width], axis=AX.X, op=ALU.add),
               reads=[sq], writes=[ss])
            op("act", lambda e: e.activation(out=lnv[0:npart, 0:1], in_=ss[0:npart, 0:1], func=AF.Ln,
                                             scale=1.0 / width, bias=EPS), reads=[ss], writes=[lnv])
            op("act", lambda e: e.activation(out=rstd[0:npart, 0:1], in_=lnv[0:npart, 0:1], func=AF.Exp,
                                             scale=-0.5, bias=out_bias), reads=[lnv], writes=[rstd])

        def norm_to_bf16(xt, gain, hb, sq, ss, lnv, rstd, npart=128):
            rms_stats(xt[0:npart, :], xt, npart, sq, ss, lnv, rstd)
            op("dve", lambda e: e.scalar_tensor_tensor(out=hb[0:npart, :], in0=xt[0:npart, :], scalar=rstd[0:npart, 0:1],
                                                      in1=gain[0:npart, :], op0=ALU.mult, op1=ALU.mult),
               reads=[xt, rstd, gain], writes=[hb])

        def transpose_h(hb, pt, npart=128):
            op("pe", lambda e: [e.transpose(out=pt[:, c * npart:(c + 1) * npart], in_=hb[0:npart, c * 128:(c + 1) * 128],
                                            identity=ident[0:npart, 0:npart]) for c in range(8)],
               reads=[hb, ident], writes=[pt])

        def qk_norm_rope(src, H, gain, rope, dst, tmp, ss, lnv, rstd, out_bias):
            W = H * 64
            sq, ta, tb = tmp
            s3 = src[:, 0:W].rearrange("p (h d) -> p h d", d=64)
            op("dve", lambda e: e.tensor_tensor(out=sq[:, 0:W], in0=src[:, 0:W], in1=src[:, 0:W], op=ALU.mult),
               reads=[src], writes=[sq])
            op("dve", lambda e: e.tensor_reduce(out=ss[:, 0:H], in_=sq[:, 0:W].rearrange("p (h d) -> p h d", d=64),
                                                axis=AX.X, op=ALU.add), reads=[sq], writes=[ss])
            op("act", lambda e: e.activation(out=lnv[:, 0:H], in_=ss[:, 0:H], func=AF.Ln, scale=1.0 / 64, bias=EPS),
               reads=[ss], writes=[lnv])
            op("act", lambda e: e.activation(out=rstd[:, 0:H], in_=lnv[:, 0:H], func=AF.Exp, scale=-0.5, bias=out_bias),
               reads=[lnv], writes=[rstd])
            a3 = ta[:, 0:W].rearrange("p (h d) -> p h d", d=64)
            op("dve", lambda e: e.tensor_tensor(out=a3, in0=s3, in1=bc_last(rstd[:, 0:H], 64), op=ALU.mult),
               reads=[src, rstd], writes=[ta])
            op("dve", lambda e: e.tensor_tensor(out=a3, in0=a3, in1=bc_mid(gain[:, 0:64], H), op=ALU.mult),
               reads=[ta, gain], writes=[ta])
            b3 = tb[:, 0:W].rearrange("p (h d) -> p h d", d=64)
            op("dve", lambda e: e.tensor_tensor(out=b3, in0=a3, in1=bc_mid(rope[:, 0:64], H), op=ALU.mult),
               reads=[ta, rope], writes=[tb])
            a5 = ta[:, 0:W].rearrange("p (h a f q) -> p h a f q", a=2, f=2, q=16)
            s5 = sq[:, 0:W].rearrange("p (h a f q) -> p h a f q", a=2, f=2, q=16)
            nsin = bc_mid(rope[:, 64:96].rearrange("p (a q) -> p a q", a=2), H)
            psin = bc_mid(rope[:, 96:128].rearrange("p (a q) -> p a q", a=2), H)
            op("dve", lambda e: e.tensor_tensor(out=s5[:, :, :, 0, :], in0=a5[:, :, :, 1, :], in1=nsin, op=ALU.mult),
               reads=[ta, rope], writes=[sq])
            op("dve", lambda e: e.tensor_tensor(out=s5[:, :, :, 1, :], in0=a5[:, :, :, 0, :], in1=psin, op=ALU.mult),
               reads=[ta, rope], writes=[sq])
            op("dve", lambda e: e.tensor_tensor(out=dst[:, 0:W], in0=tb[:, 0:W], in1=sq[:, 0:W], op=ALU.add),
               reads=[tb, sq], writes=[dst])

        def phase_attention(src, dsrc, dst, ddst):
            with ExitStack() as ph:
                PB, PT1 = alloc_psum(ph, 7)
                PT = [PT1[0], PT1[0]]
                wqkv = sbuf(ph, "wqkv", [128, 8 * 1536], BF16)
                wo = sbuf(ph, "wo_a", [64, 16 * D], BF16)
                KT = sbuf(ph, "KT", [128, 4 * S], BF16)
                Vs = sbuf(ph, "Vs", [128, NT * 4 * 65], BF16)
                gain = sbuf(ph, "gain_a", [128, D])
                qg = sbuf(ph, "qg", [128, 64])
                kg = sbuf(ph, "kg", [128, 64])
                xs = [sbuf(ph, f"xs{i}", [128, D]) for i in range(3)]
                rp = [sbuf(ph, f"rp{i}", [128, 128]) for i in range(3)]
                hb = [sbuf(ph, f"hb{i}", [128, D], BF16) for i in range(2)]
                hT = [sbuf(ph, f"hT{i}", [128, 8 * 128], BF16) for i in range(2)]
                sq = sbuf(ph, "sq", [128, D])
                ta = sbuf(ph, "ta", [128, D])
                tb = sbuf(ph, "tb", [128, D])
                qs = sbuf(ph, "qs", [128, D])
                qr = sbuf(ph, "qr", [128, D], BF16)
                QT = [sbuf(ph, f"QT{i}", [128, 16 * 128], BF16) for i in range(2)]
                Pb = [sbuf(ph, f"Pb{i}", [128, 512], BF16) for i in range(3)]
                OT = [sbuf(ph, f"OT{i}", [64, 16 * 128], BF16) for i in range(2)]
                rden = sbuf(ph, "rden", [128, 512])
                bcs = sbuf(ph, "bcs", [64, 512])
                xn = [sbuf(ph, f"xn{i}", [128, D]) for i in range(2)]
                ss = sbuf(ph, "ss", [128, 16]); lnv = sbuf(ph, "lnv", [128, 16]); rstd = sbuf(ph, "rstd", [128, 16])
                ss1 = sbuf(ph, "ss1", [128, 1]); lnv1 = sbuf(ph, "lnv1", [128, 1]); rstd1 = sbuf(ph, "rstd1", [128, 1])

                load_weight_bf16(wqkv, attn_w_qkv[0], 8, 1536, "w0")
                dma("pool", wo[:].rearrange("d (h n) -> d h n", h=16), attn_w_o[0].rearrange("(h d) n -> d h n", d=64),
                    writes=[wo], sem="w1")
                dma("sp", gain[:], bc_part(attn_norm), writes=[gain], sem="c0")
                dma("sp", qg[:], bc_part(attn_q_gain), writes=[qg], sem="c0")
                dma("sp", kg[:], bc_part(attn_k_gain), writes=[kg], sem="c0")
                op("pool", lambda e: e.memset(KT[64:128, :], 0.0), writes=[KT])
                for q in QT:
                    op("pool", lambda e, q=q: e.memset(q[64:128, :], 0.0), writes=[q])
                op("dve", lambda e: e.memset(Vs[:].rearrange("p (t e) -> p t e", e=65)[:, :, 64:65], 1.0), writes=[Vs])

                def load_tile(t, i):
                    dma("sp", xs[i][:], src[t * 128:(t + 1) * 128, :], reads=[dsrc], writes=[xs[i]], sem=f"lx{i}")
                    dma("sp", rp[i][:], rope_in[t * 128:(t + 1) * 128, :], writes=[rp[i]], sem=f"lr{i}")

                load_tile(0, 0)
                for t in range(NT):
                    i = t % 3
                    if t + 1 < NT:
                        load_tile(t + 1, (t + 1) % 3)
                    j = t % 2
                    norm_to_bf16(xs[i], gain, hb[j], sq, ss1, lnv1, rstd1)
                    transpose_h(hb[j], PT[j])
                    op("dve", lambda e, j=j: e.tensor_copy(out=hT[j][:], in_=PT[j][:]), reads=[PT[j]], writes=[hT[j]])
                    pk = PB[j]
                    op("pe", lambda e, j=j, pk=pk: [e.matmul(pk[:], lhsT=hT[j][:, c * 128:(c + 1) * 128],
                                                             rhs=wqkv[:, c * 1536 + 1024:c * 1536 + 1536],
                                                             start=(c == 0), stop=(c == 7)) for c in range(8)],
                       reads=[hT[j], wqkv], writes=[pk])
                    op("act", lambda e, pk=pk: e.activation(out=qs[:, 0:512], in_=pk[:], func=AF.Copy),
                       reads=[pk], writes=[qs])
                    qk_norm_rope(qs, 4, kg, rp[i], qr, (sq, ta, tb), ss, lnv, rstd, 0.0)
                    pt = PT[j]
                    op("pe", lambda e, pt=pt: [e.transpose(out=pt[0:64, g * 128:(g + 1) * 128], in_=qr[:, g * 64:(g + 1) * 64],
                                                           identity=ident[:]) for g in range(4)],
                       reads=[qr, ident], writes=[pt])
                    op("dve", lambda e, pt=pt, t=t: e.tensor_copy(
                        out=KT[0:64, :].rearrange("p (g s) -> p g s", g=4)[:, :, t * 128:(t + 1) * 128],
                        in_=pt[0:64, 0:512].rearrange("p (g s) -> p g s", g=4)), reads=[pt], writes=[KT])
                    op("act", lambda e, t=t: e.activation(
                        out=Vs[:, t * 260:(t + 1) * 260].rearrange("p (g e) -> p g e", e=65)[:, :, 0:64],
                        in_=qs[:, 256:512].rearrange("p (g d) -> p g d", d=64), func=AF.Copy),
                       reads=[qs], writes=[Vs])

                def prep_a1(qb):
                    i = qb % 3
                    load_tile(qb, i)
                    norm_to_bf16(xs[i], gain, hb[qb % 2], sq, ss1, lnv1, rstd1)

                def prep_a2(qb):
                    j = qb % 2
                    transpose_h(hb[j], PT[1])
                    op("dve", lambda e: e.tensor_copy(out=hT[j][:], in_=PT[1][:]), reads=[PT[1]], writes=[hT[j]])

                def prep_a3(qb):
                    j = qb % 2
                    i = qb % 3
                    for half in range(2):
                        pk = PB[half]
                        op("pe", lambda e, pk=pk, half=half: [e.matmul(pk[:], lhsT=hT[j][:, c * 128:(c + 1) * 128],
                                                                       rhs=wqkv[:, c * 1536 + half * 512:c * 1536 + half * 512 + 512],
                                                                       start=(c == 0), stop=(c == 7)) for c in range(8)],
                           reads=[hT[j], wqkv], writes=[pk])
                        op("act", lambda e, pk=pk, half=half: e.activation(out=qs[:, half * 512:(half + 1) * 512], in_=pk[:],
                                                                           func=AF.Copy), reads=[pk], writes=[qs])
                    qk_norm_rope(qs, 16, qg, rp[i], qr, (sq, ta, tb), ss, lnv, rstd, float(np.log(0.125)))

                def prep_b(qb):
                    j = qb % 2
                    for half in range(2):
                        pt = PT[1]
                        op("pe", lambda e, pt=pt, half=half: [e.transpose(out=pt[0:64, h * 128:(h + 1) * 128],
                                                                          in_=qr[:, (half * 8 + h) * 64:(half * 8 + h + 1) * 64],
                                                                          identity=ident[:]) for h in range(8)],
                           reads=[qr, ident], writes=[pt])
                        op("dve", lambda e, pt=pt, half=half: e.tensor_copy(out=QT[j][0:64, half * 1024:(half + 1) * 1024],
                                                                            in_=pt[0:64, :]), reads=[pt], writes=[QT[j]])

                prep_a1(0); prep_a2(0); prep_a3(0); prep_b(0)
                SB_ = [PB[2], PB[3], PB[4]]
                OB_ = [PB[5], PB[6]]
                steps = [(qb, g, kt) for qb in range(NT) for g in range(4) for kt in range(NT)]
                nsteps = len(steps)
                DEPTH = 3
                deferred = {}

                def defer(i, fn):
                    deferred.setdefault(i, []).append(fn)

                def s_mm(i):
                    qb, g, kt = steps[i]
                    sbk = SB_[i % DEPTH]
                    jq = qb % 2
                    op("pe", lambda e: e.matmul(sbk[:], lhsT=KT[:, g * S + kt * 128:g * S + (kt + 1) * 128],
                                                rhs=QT[jq][:, g * 512:(g + 1) * 512], start=True, stop=True),
                       reads=[KT, QT[jq]], writes=[sbk])

                def normalize(qb, g, ob):
                    jq = qb % 2
                    bcb = PB[g % 2]
                    op("pe", lambda e: e.matmul(bcb[0:64, :], lhsT=ones_f[64:65, 0:64], rhs=rden[64:65, :],
                                                start=True, stop=True), reads=[ones_f, rden], writes=[bcb])
                    op("dve", lambda e: e.tensor_copy(out=bcs[:], in_=bcb[0:64, :]), reads=[bcb], writes=[bcs])
                    op("dve", lambda e: e.tensor_tensor(out=OT[jq][:, g * 512:(g + 1) * 512], in0=ob[0:64, :], in1=bcs[:],
                                                        op=ALU.mult), reads=[ob, bcs], writes=[OT[jq]])

                def wo_store(qb):
                    jq = qb % 2
                    xi = xs[qb % 3]
                    xo = xn[jq]
                    for half in range(2):
                        pk = PB[half]
                        op("pe", lambda e: [e.matmul(pk[:], lhsT=OT[jq][:, h * 128:(h + 1) * 128],
                                                     rhs=wo[:, h * D + half * 512:h * D + half * 512 + 512],
                                                     start=(h == 0), stop=(h == 15)) for h in range(16)],
                           reads=[OT[jq], wo], writes=[pk])
                        op("dve", lambda e: e.tensor_tensor(out=xo[:, half * 512:(half + 1) * 512], in0=pk[:],
                                                            in1=xi[:, half * 512:(half + 1) * 512], op=ALU.add),
                           reads=[pk, xi], writes=[xo])
                    dma("sp", dst[qb * 128:(qb + 1) * 128, :], xo[:], reads=[xo], writes=[ddst], sem=f"so{jq}")

                for i in range(min(DEPTH, nsteps)):
                    s_mm(i)
                for i, (qb, g, kt) in enumerate(steps):
                    if kt == 0 and qb + 1 < NT:
                        (prep_a1, prep_a2, prep_a3, prep_b)[g](qb + 1)
                    ob = OB_[g % 2]
                    sbk = SB_[i % DEPTH]
                    pb = Pb[i % 3]
                    op("act", lambda e: e.activation(out=pb[:], in_=sbk[:], func=AF.Exp), reads=[sbk], writes=[pb])
                    op("pe", lambda e: e.matmul(ob[0:65, :], lhsT=Vs[:, (kt * 4 + g) * 65:(kt * 4 + g + 1) * 65],
                                                rhs=pb[:], start=(kt == 0), stop=(kt == NT - 1)),
                       reads=[Vs, pb], writes=[ob])
                    if i + DEPTH < nsteps:
                        s_mm(i + DEPTH)
                    for fn in deferred.pop(i, []):
                        fn()
                    if kt == NT - 1:
                        op("dve", lambda e: e.reciprocal(out=rden[64:65, :], in_=ob[64:65, :]), reads=[ob], writes=[rden])
                        dd_ = max(1, min(8, NT // 2))
                        if i + dd_ >= nsteps:
                            normalize(qb, g, ob)
                            if g == 3:
                                wo_store(qb)
                        else:
                            defer(i + dd_, lambda qb=qb, g=g, ob=ob: normalize(qb, g, ob))
                            if g == 3:
                                defer(i + dd_, lambda qb=qb: wo_store(qb))
                SC.barrier()

        def phase_xattn(l, src, dsrc, dst, ddst):
            with ExitStack() as ph:
                PB, PT = alloc_psum(ph, 6)
                wq = sbuf(ph, "wq", [128, 8 * D], BF16)
                wkv = sbuf(ph, "wkv", [128, 8 * 2 * D], BF16)
                wo = sbuf(ph, "wo_x", [128, 8 * D], BF16)
                KmT = sbuf(ph, "KmT", [128, 8 * MEM], BF16)
                Vm = sbuf(ph, "Vm", [128, 2 * D], BF16)
                gain = sbuf(ph, "gain_x", [128, D])
                mgain = sbuf(ph, "mgain", [128, D])
                xs = [sbuf(ph, f"xs{i}", [128, D]) for i in range(8)]
                hb = [sbuf(ph, f"hb{i}", [128, D], BF16) for i in range(2)]
                hT = sbuf(ph, "hT", [128, 8 * 512], BF16)
                memT = sbuf(ph, "memT", [128, 8 * MEM], BF16)
                qT = sbuf(ph, "qT", [128, 8 * 512], BF16)
                Pb = [sbuf(ph, f"Pb{i}", [128, 512], BF16) for i in range(4)]
                OT = sbuf(ph, "OT", [128, 8 * 512], BF16)
                rden = [sbuf(ph, f"rden{i}", [128, 512]) for i in range(2)]
                xn = [sbuf(ph, f"xn{i}", [128, D]) for i in range(2)]
                sq = sbuf(ph, "sq", [128, D])
                ss1 = sbuf(ph, "ss1", [128, 1]); lnv1 = sbuf(ph, "lnv1", [128, 1]); rstd1 = sbuf(ph, "rstd1", [128, 1])

                load_weight_bf16(wkv, xattn_w_kv[l], 8, 2 * D, "w0", col_split=2)
                load_weight_bf16(wq, xattn_w_q[l], 8, D, "w1")
                load_weight_bf16(wo, xattn_w_o[l], 8, D, "w2")
                dma("sp", gain[:], bc_part(xattn_norm[l:l + 1, :]), writes=[gain], sem="c0")
                dma("sp", mgain[:], bc_part(mem_norm[l:l + 1, :]), writes=[mgain], sem="c0")

                for mt in range(2):
                    dma("sp", xs[mt][:], mem_in[mt * 128:(mt + 1) * 128, :], writes=[xs[mt]], sem=f"lx{mt}")
                for mt in range(2):
                    norm_to_bf16(xs[mt], mgain, hb[mt], sq, ss1, lnv1, rstd1)
                    transpose_h(hb[mt], PT[mt])
                    op("dve", lambda e, mt=mt: e.tensor_copy(
                        out=memT[:].rearrange("p (c m) -> p c m", c=8)[:, :, mt * 128:(mt + 1) * 128],
                        in_=PT[mt][:].rearrange("p (c m) -> p c m", c=8)), reads=[PT[mt]], writes=[memT])
                for dc in range(8):
                    pk = PB[dc % 2]
                    op("pe", lambda e, pk=pk, dc=dc: [e.matmul(pk[:, 0:MEM], lhsT=wkv[:, c * 2048 + dc * 128:c * 2048 + (dc + 1) * 128],
                                                               rhs=memT[:, c * MEM:(c + 1) * MEM], start=(c == 0), stop=(c == 7))
                                                      for c in range(8)], reads=[wkv, memT], writes=[pk])
                    op("dve", lambda e, pk=pk, dc=dc: e.tensor_copy(out=KmT[:, dc * MEM:(dc + 1) * MEM], in_=pk[:, 0:MEM]),
                       reads=[pk], writes=[KmT])
                for mt in range(2):
                    for half in range(2):
                        pk = PB[2 + (mt * 2 + half) % 2]
                        op("pe", lambda e, pk=pk, mt=mt, half=half: [
                            e.matmul(pk[:], lhsT=memT[:, c * MEM + mt * 128:c * MEM + (mt + 1) * 128],
                                     rhs=wkv[:, c * 2048 + 1024 + half * 512:c * 2048 + 1024 + (half + 1) * 512],
                                     start=(c == 0), stop=(c == 7)) for c in range(8)], reads=[wkv, memT], writes=[pk])
                        op("act", lambda e, pk=pk, mt=mt, half=half: e.activation(
                            out=Vm[:, mt * D + half * 512:mt * D + (half + 1) * 512], in_=pk[:], func=AF.Copy),
                           reads=[pk], writes=[Vm])

                def load_blk(blk):
                    for jj in range(4):
                        t = blk * 4 + jj
                        i = (blk % 2) * 4 + jj
                        dma("sp", xs[i][:], src[t * 128:(t + 1) * 128, :], reads=[dsrc], writes=[xs[i]], sem=f"lx{i}")

                load_blk(0)
                for blk in range(NB):
                    if blk + 1 < NB:
                        load_blk(blk + 1)
                    for jj in range(4):
                        i = (blk % 2) * 4 + jj
                        k = jj % 2
                        norm_to_bf16(xs[i], gain, hb[k], sq, ss1, lnv1, rstd1)
                        transpose_h(hb[k], PT[k])
                        op("dve", lambda e, k=k, jj=jj: e.tensor_copy(
                            out=hT[:].rearrange("p (c m) -> p c m", c=8)[:, :, jj * 128:(jj + 1) * 128],
                            in_=PT[k][:].rearrange("p (c m) -> p c m", c=8)), reads=[PT[k]], writes=[hT])
                    for dc in range(8):
                        pk = PB[dc % 2]
                        op("pe", lambda e, pk=pk, dc=dc: [e.matmul(pk[:], lhsT=wq[:, c * D + dc * 128:c * D + (dc + 1) * 128],
                                                                   rhs=hT[:, c * 512:(c + 1) * 512], start=(c == 0), stop=(c == 7))
                                                          for c in range(8)], reads=[wq, hT], writes=[pk])
                        op("act", lambda e, pk=pk, dc=dc: e.activation(out=qT[:, dc * 512:(dc + 1) * 512], in_=pk[:],
                                                                       func=AF.Copy, scale=1.0 / 16.0), reads=[pk], writes=[qT])
                    for h in range(4):
                        pbs = [Pb[(h % 2) * 2 + m] for m in range(2)]
                        for m in range(2):
                            sbk = PB[2 + m]
                            op("pe", lambda e, sbk=sbk, m=m: [
                                e.matmul(sbk[:], lhsT=KmT[:, (2 * h + jd) * MEM + m * 128:(2 * h + jd) * MEM + (m + 1) * 128],
                                         rhs=qT[:, (2 * h + jd) * 512:(2 * h + jd + 1) * 512], start=(jd == 0), stop=(jd == 1))
                                for jd in range(2)], reads=[KmT, qT], writes=[sbk])
                            op("act", lambda e, sbk=sbk, m=m: e.activation(out=pbs[m][:], in_=sbk[:], func=AF.Exp),
                               reads=[sbk], writes=[pbs[m]])
                        dbk = PB[4]
                        rd = rden[h % 2]
                        op("pe", lambda e: [e.matmul(dbk[:], lhsT=ones_b[:], rhs=pbs[m][:], start=(m == 0), stop=(m == 1))
                                            for m in range(2)], reads=[ones_b, pbs[0], pbs[1]], writes=[dbk])
                        op("act", lambda e: e.activation(out=rd[:], in_=dbk[:], func=AF.Ln), reads=[dbk], writes=[rd])
                        op("act", lambda e: e.activation(out=rd[:], in_=rd[:], func=AF.Exp, scale=-1.0), reads=[rd], writes=[rd])
                        for jd in range(2):
                            obk = PB[5] if jd else PB[0]
                            op("pe", lambda e, obk=obk, jd=jd: [
                                e.matmul(obk[:], lhsT=Vm[:, m * D + h * 256 + jd * 128:m * D + h * 256 + (jd + 1) * 128],
                                         rhs=pbs[m][:], start=(m == 0), stop=(m == 1)) for m in range(2)],
                               reads=[Vm, pbs[0], pbs[1]], writes=[obk])
                            op("dve", lambda e, obk=obk, jd=jd: e.tensor_tensor(
                                out=OT[:, (2 * h + jd) * 512:(2 * h + jd + 1) * 512], in0=obk[:], in1=rd[:], op=ALU.mult),
                               reads=[obk, rd], writes=[OT])
                    for jj in range(4):
                        t = blk * 4 + jj
                        xi = xs[(blk % 2) * 4 + jj]
                        xo = xn[jj % 2]
                        for half in range(2):
                            pk = PB[half + 1]
                            op("pe", lambda e, pk=pk, half=half: [
                                e.matmul(pk[:], lhsT=OT[:, dc * 512 + jj * 128:dc * 512 + (jj + 1) * 128],
                                         rhs=wo[:, dc * D + half * 512:dc * D + (half + 1) * 512], start=(dc == 0), stop=(dc == 7))
                                for dc in range(8)], reads=[OT, wo], writes=[pk])
                            op("dve", lambda e, pk=pk, half=half: e.tensor_tensor(
                                out=xo[:, half * 512:(half + 1) * 512], in0=pk[:], in1=xi[:, half * 512:(half + 1) * 512],
                                op=ALU.add), reads=[pk, xi], writes=[xo])
                        dma("sp", dst[t * 128:(t + 1) * 128, :], xo[:], reads=[xo], writes=[ddst], sem=f"so{jj % 2}")
                SC.barrier()

        def phase_ffn(l, src, dsrc, dst, ddst, final):
            with ExitStack() as ph:
                PB, PT = alloc_psum(ph, 6)
                wup = sbuf(ph, "wup", [128, 8 * 2 * DFF], BF16)
                wd = sbuf(ph, "wd", [128, 22 * D], BF16)
                gain = sbuf(ph, "gain_f", [128, D])
                fgain = sbuf(ph, "fgain", [128, D]) if final else None
                cp = sbuf(ph, "cp", [128, 4 * NCH])
                xs = [sbuf(ph, f"xs{i}", [128, D]) for i in range(2)]
                xr = [sbuf(ph, f"xr{i}", [128, D]) for i in range(2)]
                hb = sbuf(ph, "hb", [128, D], BF16)
                hT = sbuf(ph, "hT", [128, 8 * 512], BF16)
                hTh = sbuf(ph, "hTh", [128, 8 * 16], BF16)
                uh = sbuf(ph, "uh", [128, NCH * 16])
                G = sbuf(ph, "G", [128, 22 * 512], BF16)
                accg = [sbuf(ph, f"accg{i}", [128, 512]) for i in range(2)]
                accv = [sbuf(ph, f"accv{i}", [128, 512]) for i in range(2)]
                sq = sbuf(ph, "sq", [128, D])
                cpr = sq
                xh = xr[0]
                ss1 = sbuf(ph, "ss1", [128, 1]); lnv1 = sbuf(ph, "lnv1", [128, 1]); rstd1 = sbuf(ph, "rstd1", [128, 1])

                load_weight_bf16(wup, ffn_w_up[l], 8, 2 * DFF, "w0", col_split=4)
                load_weight_bf16(wd, ffn_w_down[l], 22, D, "w1", col_split=2)
                dma("sp", gain[:], bc_part(ffn_norm[l:l + 1, :]), writes=[gain], sem="c0")
                if final:
                    dma("sp", fgain[:], bc_part(final_norm), writes=[fgain], sem="c0")
                for k in range(3):
                    dma("sp", cpr[0:NCH, k * 128:(k + 1) * 128], ffn_conv_w[l, k].rearrange("(c p) -> c p", p=128),
                        writes=[cpr], sem="c1")
                dma("sp", cpr[0:NCH, 384:512], ffn_conv_b[l].rearrange("(c p) -> c p", p=128), writes=[cpr], sem="c1")
                op("pe", lambda e: [e.transpose(out=PB[0][:, k * NCH:(k + 1) * NCH], in_=cpr[0:NCH, k * 128:(k + 1) * 128],
                                                identity=identf[0:NCH, 0:NCH]) for k in range(4)],
                   reads=[cpr, identf], writes=[PB[0]])
                op("dve", lambda e: e.tensor_copy(out=cp[:], in_=PB[0][:, 0:4 * NCH]), reads=[PB[0]], writes=[cp])

                def cw(k, ch):
                    return cp[:, k * NCH + ch:k * NCH + ch + 1]

                op("dve", lambda e: e.memset(xh[0:16, :], 0.0), writes=[xh])
                for m in range(NB - 1):
                    dma("sp", xh[2 * m:2 * m + 2, :], src[512 * (m + 1) - 1:512 * (m + 1) + 1, :], reads=[dsrc], writes=[xh],
                        sem="lh")
                op("dve", lambda e: e.memset(uh[:], 0.0), writes=[uh])
                if NB > 1:
                    norm_to_bf16(xh, gain, hb, sq, ss1, lnv1, rstd1, npart=16)
                    transpose_h(hb, PT[0], npart=16)
                    op("dve", lambda e: e.tensor_copy(out=hTh[:], in_=PT[0][:, 0:128]), reads=[PT[0]], writes=[hTh])
                    for hh in range(2):
                        pk = PB[1 + hh]
                        op("pe", lambda e, pk=pk, hh=hh: [
                            e.matmul(pk[:, cc * 16:(cc + 1) * 16], lhsT=wup[:, c * 2 * DFF + (hh * 22 + cc) * 128:c * 2 * DFF + (hh * 22 + cc + 1) * 128],
                                     rhs=hTh[:, c * 16:(c + 1) * 16], start=(c == 0), stop=(c == 7))
                            for cc in range(22) for c in range(8)], reads=[wup, hTh], writes=[pk])
                        op("dve", lambda e, pk=pk, hh=hh: e.tensor_copy(out=uh[:, hh * 352:(hh + 1) * 352], in_=pk[:, 0:352]),
                           reads=[pk], writes=[uh])
                    uh4 = uh[:].rearrange("p (c m two) -> p c m two", c=NCH, two=2)
                    op("dve", lambda e: e.tensor_tensor(out=uh4[:, :, :, 0], in0=uh4[:, :, :, 0],
                                                        in1=bc_last(cp[:, 0:NCH], 8), op=ALU.mult), reads=[uh, cp], writes=[uh])
                    op("dve", lambda e: e.tensor_tensor(out=uh4[:, :, :, 1], in0=uh4[:, :, :, 1],
                                                        in1=bc_last(cp[:, 2 * NCH:3 * NCH], 8), op=ALU.mult), reads=[uh, cp], writes=[uh])

                def conv_chunk(blk, ch, ub, acc):
                    op("act", lambda e: e.activation(out=acc[:], in_=ub[:], func=AF.Identity, scale=cw(1, ch), bias=cw(3, ch)),
                       reads=[ub, cp], writes=[acc])
                    op("dve", lambda e: e.scalar_tensor_tensor(out=acc[:, 1:512], in0=ub[:, 0:511], scalar=cw(0, ch),
                                                               in1=acc[:, 1:512], op0=ALU.mult, op1=ALU.add),
                       reads=[ub, cp, acc], writes=[acc])
                    op("dve", lambda e: e.scalar_tensor_tensor(out=acc[:, 0:511], in0=ub[:, 1:512], scalar=cw(2, ch),
                                                               in1=acc[:, 0:511], op0=ALU.mult, op1=ALU.add),
                       reads=[ub, cp, acc], writes=[acc])
                    if blk > 0:
                        li = ch * 16 + 2 * (blk - 1)
                        op("dve", lambda e: e.tensor_tensor(out=acc[:, 0:1], in0=acc[:, 0:1], in1=uh[:, li:li + 1], op=ALU.add),
                           reads=[acc, uh], writes=[acc])
                    if blk < NB - 1:
                        ri = ch * 16 + 2 * blk + 1
                        op("dve", lambda e: e.tensor_tensor(out=acc[:, 511:512], in0=acc[:, 511:512], in1=uh[:, ri:ri + 1], op=ALU.add),
                           reads=[acc, uh], writes=[acc])

                def load_x(t, i):
                    dma("sp", xs[i][:], src[t * 128:(t + 1) * 128, :], reads=[dsrc], writes=[xs[i]], sem=f"lx{i}")

                load_x(0, 0)
                for blk in range(NB):
                    for jj in range(4):
                        t = blk * 4 + jj
                        if t + 1 < NT:
                            load_x(t + 1, (t + 1) % 2)
                        norm_to_bf16(xs[t % 2], gain, hb, sq, ss1, lnv1, rstd1)
                        transpose_h(hb, PT[jj % 2])
                        op("dve", lambda e, jj=jj: e.tensor_copy(
                            out=hT[:].rearrange("p (c m) -> p c m", c=8)[:, :, jj * 128:(jj + 1) * 128],
                            in_=PT[jj % 2][:].rearrange("p (c m) -> p c m", c=8)), reads=[PT[jj % 2]], writes=[hT])
                    for g in range(22):
                        k = g % 2
                        ug, uv = PB[2 * k], PB[2 * k + 1]
                        for (ub, ch) in ((ug, g), (uv, 22 + g)):
                            op("pe", lambda e, ub=ub, ch=ch: [
                                e.matmul(ub[:], lhsT=wup[:, c * 2 * DFF + ch * 128:c * 2 * DFF + (ch + 1) * 128],
                                         rhs=hT[:, c * 512:(c + 1) * 512], start=(c == 0), stop=(c == 7)) for c in range(8)],
                               reads=[wup, hT], writes=[ub])
                        conv_chunk(blk, g, ug, accg[k])
                        conv_chunk(blk, 22 + g, uv, accv[k])
                        op("act", lambda e, k=k: e.activation(out=accg[k][:], in_=accg[k][:], func=AF.Silu),
                           reads=[accg[k]], writes=[accg[k]])
                        op("pool", lambda e, k=k, g=g: e.tensor_tensor(out=G[:, g * 512:(g + 1) * 512], in0=accg[k][:], in1=accv[k][:],
                                                                       op=ALU.mult), reads=[accg[k], accv[k]], writes=[G])
                    for jj in range(4):
                        t = blk * 4 + jj
                        xi = xr[jj % 2]
                        dma("sp", xi[:], src[t * 128:(t + 1) * 128, :], reads=[dsrc], writes=[xi], sem=f"lr{jj % 2}")
                        for half in range(2):
                            pk = PB[4 + half]
                            op("pe", lambda e, pk=pk, half=half, jj=jj: [
                                e.matmul(pk[:], lhsT=G[:, g * 512 + jj * 128:g * 512 + (jj + 1) * 128],
                                         rhs=wd[:, g * D + half * 512:g * D + (half + 1) * 512], start=(g == 0), stop=(g == 21))
                                for g in range(22)], reads=[G, wd], writes=[pk])
                            op("dve", lambda e, pk=pk, half=half, xi=xi: e.tensor_tensor(
                                out=xi[:, half * 512:(half + 1) * 512], in0=pk[:], in1=xi[:, half * 512:(half + 1) * 512],
                                op=ALU.add), reads=[pk, xi], writes=[xi])
                        if final:
                            rms_stats(xi[:, :], xi, 128, sq, ss1, lnv1, rstd1)
                            op("dve", lambda e, xi=xi: e.scalar_tensor_tensor(out=xi[:], in0=xi[:], scalar=rstd1[:, 0:1],
                                                                              in1=fgain[:], op0=ALU.mult, op1=ALU.mult),
                               reads=[xi, rstd1, fgain], writes=[xi])
                        dma("sp", dst[t * 128:(t + 1) * 128, :], xi[:], reads=[xi], writes=[ddst], sem=f"so{jj % 2}")
                SC.barrier()

        def phase_pool(src, dsrc, dst, ddst):
            PADW = S + 16
            with ExitStack() as ph:
                PB, PT = alloc_psum(ph, 6)
                HT = sbuf(ph, "HT", [128, 8 * PADW], BF16)
                wp = sbuf(ph, "wp", [128, 8 * 256], BF16)
                gain = sbuf(ph, "gain_p", [128, D])
                pscale = sbuf(ph, "pscale", [128, D])
                invc = sbuf(ph, "invc", [128, 64])
                xs = [sbuf(ph, f"xs{i}", [128, D]) for i in range(3)]
                hb = [sbuf(ph, f"hb{i}", [128, D], BF16) for i in range(2)]
                Wa = [sbuf(ph, f"Wa{i}", [128, PADW]) for i in range(2)]
                Wb = [sbuf(ph, f"Wb{i}", [128, PADW]) for i in range(2)]
                edge = [sbuf(ph, f"edge{i}", [128, 16]) for i in range(2)]
                edgeb = [sbuf(ph, f"edgeb{i}", [128, 16], BF16) for i in range(2)]
                xn = [sbuf(ph, f"xn{i}", [128, D]) for i in range(2)]
                sq = sbuf(ph, "sq", [128, D])
                ss1 = sbuf(ph, "ss1", [128, 1]); lnv1 = sbuf(ph, "lnv1", [128, 1]); rstd1 = sbuf(ph, "rstd1", [128, 1])

                dma("pool", wp[:].rearrange("p (c n) -> p c n", c=8), pool_w[0].rearrange("g (j p) n -> p (g j) n", p=128),
                    writes=[wp], sem="w0")
                dma("sp", gain[:], bc_part(pool_norm), writes=[gain], sem="c0")
                dma("sp", pscale[:], bc_part(pool_scale), writes=[pscale], sem="c0")
                dma("sp", invc[:], bc_part(invc_in), writes=[invc], sem="c0")
                H3 = HT[:].rearrange("p (c w) -> p c w", c=8)
                op("pool", lambda e: e.memset(H3[:, :, 0:8], 0.0), writes=[HT])
                op("pool", lambda e: e.memset(H3[:, :, 8 + S:16 + S], 0.0), writes=[HT])

                def load_x(t, i):
                    dma("sp", xs[i][:], src[t * 128:(t + 1) * 128, :], reads=[dsrc], writes=[xs[i]], sem=f"lx{i}")

                load_x(0, 0)
                for t in range(NT):
                    if t + 1 < NT:
                        load_x(t + 1, (t + 1) % 3)
                    k = t % 2
                    norm_to_bf16(xs[t % 3], gain, hb[k], sq, ss1, lnv1, rstd1)
                    transpose_h(hb[k], PT[k])
                    op("act", lambda e, k=k, t=t: e.activation(out=H3[:, :, 8 + t * 128:8 + (t + 1) * 128],
                                                               in_=PT[k][:].rearrange("p (c m) -> p c m", c=8), func=AF.Copy),
                       reads=[PT[k]], writes=[HT])
                for c in range(8):
                    grp = c // 2
                    eng = "pool" if c in (5, 7) else "dve"
                    Hc = HT[:, c * PADW:(c + 1) * PADW]
                    ks = 1 if eng == "pool" else 0
                    A_, B_ = Wa[ks], Wb[ks]
                    ed, edb = edge[ks], edgeb[ks]
                    op(eng, lambda e: e.tensor_tensor(out=A_[:, 1:PADW], in0=Hc[:, 0:PADW - 1], in1=Hc[:, 1:PADW], op=ALU.add),
                       reads=[HT], writes=[A_])
                    cur, other = A_, B_
                    lo, hi = 1, PADW
                    sh = 1
                    for lvl in range(grp):
                        nlo, nhi = lo + sh, hi - sh
                        op(eng, lambda e, cur=cur, other=other, nlo=nlo, nhi=nhi, sh=sh: e.tensor_tensor(
                            out=other[:, nlo:nhi], in0=cur[:, nlo - sh:nhi - sh], in1=cur[:, nlo + sh:nhi + sh], op=ALU.add),
                           reads=[cur], writes=[other])
                        cur, other = other, cur
                        lo, hi = nlo, nhi
                        sh *= 2
                    w = 2 ** (grp + 1)
                    cur_e = bass.AP(cur.t[:, 8:16].tensor, cur.t[:, 8:16].offset, [list(cur.t[:, 8:16].ap[0]), [S - 8, 2], [1, 8]])
                    h_e = bass.AP(Hc[:, 8:16].tensor, Hc[:, 8:16].offset, [list(Hc[:, 8:16].ap[0]), [S - 8, 2], [1, 8]])
                    op(eng, lambda e: e.tensor_tensor(out=ed[:].rearrange("p (a b) -> p a b", a=2), in0=cur_e,
                                                      in1=invc[:, grp * 16:(grp + 1) * 16].rearrange("p (a b) -> p a b", a=2),
                                                      op=ALU.mult), reads=[cur, invc], writes=[ed])
                    op(eng, lambda e: e.tensor_tensor(out=edb[:].rearrange("p (a b) -> p a b", a=2),
                                                      in0=ed[:].rearrange("p (a b) -> p a b", a=2), in1=h_e, op=ALU.subtract),
                       reads=[ed, HT], writes=[edb])
                    if eng == "dve":
                        op(eng, lambda e: e.scalar_tensor_tensor(out=Hc[:, 8:8 + S], in0=cur[:, 8:8 + S], scalar=1.0 / w,
                                                                 in1=Hc[:, 8:8 + S], op0=ALU.mult, op1=ALU.subtract),
                           reads=[cur, HT], writes=[HT])
                    else:
                        op(eng, lambda e: e.tensor_scalar(out=cur[:, 8:8 + S], in0=cur[:, 8:8 + S], scalar1=1.0 / w, scalar2=0.0,
                                                          op0=ALU.mult, op1=ALU.add), reads=[cur], writes=[cur])
                        op(eng, lambda e: e.tensor_tensor(out=Hc[:, 8:8 + S], in0=cur[:, 8:8 + S], in1=Hc[:, 8:8 + S],
                                                          op=ALU.subtract), reads=[cur, HT], writes=[HT])
                    op(eng, lambda e: e.tensor_copy(out=h_e, in_=edb[:].rearrange("p (a b) -> p a b", a=2)),
                       reads=[edb], writes=[HT])
                load_x(0, 0)
                for t in range(NT):
                    if t + 1 < NT:
                        load_x(t + 1, (t + 1) % 3)
                    xi = xs[t % 3]
                    xo = xn[t % 2]
                    for half in range(2):
                        pk = PB[(t % 2) * 2 + half]
                        op("pe", lambda e, pk=pk, half=half, t=t: [
                            e.matmul(pk[:, gg * 256:(gg + 1) * 256],
                                     lhsT=HT[:, (4 * half + 2 * gg + jc) * PADW + 8 + t * 128:(4 * half + 2 * gg + jc) * PADW + 8 + (t + 1) * 128],
                                     rhs=wp[:, (4 * half + 2 * gg + jc) * 256:(4 * half + 2 * gg + jc + 1) * 256],
                                     start=(jc == 0), stop=(jc == 1)) for gg in range(2) for jc in range(2)],
                           reads=[HT, wp], writes=[pk])
                        op("dve", lambda e, pk=pk, half=half: e.tensor_tensor(
                            out=xo[:, half * 512:(half + 1) * 512], in0=pk[:], in1=pscale[:, half * 512:(half + 1) * 512],
                            op=ALU.mult), reads=[pk, pscale], writes=[xo])
                        op("dve", lambda e, half=half: e.tensor_tensor(
                            out=xo[:, half * 512:(half + 1) * 512], in0=xo[:, half * 512:(half + 1) * 512],
                            in1=xi[:, half * 512:(half + 1) * 512], op=ALU.add), reads=[xo, xi], writes=[xo])
                    dma("sp", dst[t * 128:(t + 1) * 128, :], xo[:], reads=[xo], writes=[ddst], sem=f"so{t % 2}")
                SC.barrier()

        SC.barrier()
        plan = [
            ("attn", lambda s, ds, d, dd: phase_attention(s, ds, d, dd)),
            ("xattn0", lambda s, ds, d, dd: phase_xattn(0, s, ds, d, dd)),
            ("ffn0", lambda s, ds, d, dd: phase_ffn(0, s, ds, d, dd, False)),
            ("pool", lambda s, ds, d, dd: phase_pool(s, ds, d, dd)),
            ("xattn1", lambda s, ds, d, dd: phase_xattn(1, s, ds, d, dd)),
            ("ffn1", lambda s, ds, d, dd: phase_ffn(1, s, ds, d, dd, True)),
        ]
        if stop_after is not None:
            plan = plan[:stop_after]
        cur, dcur = x_in, dX
        for i, (name, fn) in enumerate(plan):
            last = i == len(plan) - 1
            if last:
                nxt, dnxt = out, dO
            else:
                nxt, dnxt = (rA, dA) if i % 2 == 0 else (rB, dB)
            fn(cur, dcur, nxt, dnxt)
            cur, dcur = nxt, dnxt
        SC.barrier()
        build.stats = (SC.nops, SC.nwaits)
    return nc


def rope_table(S):
    t = np.arange(S)
    row = (t // 64).astype(np.float32)
    col = (t % 64).astype(np.float32)
    inv_freq = (10000.0 ** (-np.arange(16, dtype=np.float32) / 16)).astype(np.float32)
    ang = np.stack([row[:, None] * inv_freq, col[:, None] * inv_freq], axis=1).astype(np.float32)
    c, s = np.cos(ang).astype(np.float32), np.sin(ang).astype(np.float32)
    cos64 = np.concatenate([c[:, 0], c[:, 0], c[:, 1], c[:, 1]], axis=1)
    return np.ascontiguousarray(np.concatenate([cos64, -s.reshape(S, 32), s.reshape(S, 32)], axis=1).astype(np.float32))


def invc_table(S):
    tab = np.zeros((4, 16), np.float32)
    toks = np.concatenate([np.arange(8), np.arange(S - 8, S)])
    for g, w in enumerate((2, 4, 8, 16)):
        lo = np.clip(toks - w // 2, 0, S)
        hi = np.clip(toks + w - w // 2, 0, S)
        tab[g] = 1.0 / (hi - lo).astype(np.float32)
    return tab.reshape(1, 64)


_NC_CACHE = {}


def kernel(**inputs):
    S = inputs["x"].shape[1]
    B = inputs["x"].shape[0]
    if S not in _NC_CACHE:
        _NC_CACHE[S] = build(S)
    nc = _NC_CACHE[S]
    shared = {k: np.ascontiguousarray(np.asarray(v, dtype=np.float32)) for k, v in inputs.items() if k not in ("x", "mem")}
    shared["final_norm"] = shared["final_norm"].reshape(1, D)
    shared["rope"] = rope_table(S)
    shared["invc"] = invc_table(S)
    in_maps = []
    for b in range(B):
        m = dict(shared)
        m["x"] = np.ascontiguousarray(np.asarray(inputs["x"][b], dtype=np.float32))
        m["mem"] = np.ascontiguousarray(np.asarray(inputs["mem"][b], dtype=np.float32))
        in_maps.append(m)
    res = run_bass_kernel_spmd(nc, in_maps, core_ids=list(range(B)))
    return np.stack([np.asarray(r["out"], dtype=np.float32) for r in res.results], axis=0)
```

```python
import numpy as np
from contextlib import ExitStack
import concourse.bass as bass
import concourse.mybir as mybir
from concourse.bass_utils import run_bass_kernel_spmd

F32 = mybir.dt.float32
BF16 = mybir.dt.bfloat16
AF = mybir.ActivationFunctionType
ALU = mybir.AluOpType
AX = mybir.AxisListType

D = 1024
DFF = 2816
NCH = 44
EPS = 1e-6
MEM = 256
import os
PRECAST = os.environ.get('K_PRECAST', '1') == '1'
NSPLIT = int(os.environ.get('K_NSPLIT', '1'))
SKIP_OWN = os.environ.get('K_SKIP_OWN', '0') == '1'
WO_OFF = int(os.environ.get('K_WO_OFF', '4'))
WO_STRIDE = int(os.environ.get('K_WO_STRIDE', '0'))


class Buf:
    __slots__ = ("name", "w", "r")

    def __init__(self, name):
        self.name = name
        self.w = {}
        self.r = {}


class T:
    __slots__ = ("t", "b")

    def __init__(self, t, name):
        self.t = t
        self.b = Buf(name)

    def __getitem__(self, k):
        return self.t[k]


class Sched:
    def __init__(self, nc, stack):
        self.nc = nc
        self.stack = stack
        self.engs = {"pe": nc.tensor, "act": nc.scalar, "dve": nc.vector, "pool": nc.gpsimd, "sp": nc.sync}
        self.sems = {}
        self.cnt = {}
        self.seen = {}
        self.nops = 0
        self.nwaits = 0
        for k in ("pe", "act", "dve", "pool"):
            self._sem(k)

    def _sem(self, name):
        if name not in self.sems:
            self.sems[name] = self.stack.enter_context(self.nc.semaphore("s_" + name))
            self.cnt[name] = 0
        return self.sems[name]

    def _deps(self, eng, reads, writes):
        deps = {}
        for t in reads:
            for s, v in t.b.w.items():
                if deps.get(s, 0) < v:
                    deps[s] = v
        for t in writes:
            for d in (t.b.w, t.b.r):
                for s, v in d.items():
                    if s == eng and SKIP_OWN:
                        continue
                    if deps.get(s, 0) < v:
                        deps[s] = v
        return deps

    def _wait(self, eng, deps):
        e = self.engs[eng]
        for s, v in deps.items():
            if s not in ("pe", "act", "dve", "pool"):
                v = self.cnt[s]
            if self.seen.get((eng, s), 0) < v:
                e.wait_ge(self.sems[s], v)
                self.seen[(eng, s)] = v
                self.nwaits += 1

    def op(self, eng, fn, reads=(), writes=()):
        self._wait(eng, self._deps(eng, reads, writes))
        ins = fn(self.engs[eng])
        if isinstance(ins, (list, tuple)):
            ins = ins[-1]
        self.cnt[eng] += 1
        v = self.cnt[eng]
        ins.then_inc(self.sems[eng], 1)
        for t in reads:
            t.b.r[eng] = v
        for t in writes:
            t.b.w[eng] = v
        self.nops += 1

    def dma(self, queue, out, in_, reads=(), writes=(), sem=None, **kw):
        sem = queue + "_" + sem
        self._sem(sem)
        self._wait(queue, self._deps("__dma__", reads, writes))
        ins = self.engs[queue].dma_start(out=out, in_=in_, **kw)
        self.cnt[sem] += 16
        v = self.cnt[sem]
        ins.then_inc(self.sems[sem], 16)
        for t in reads:
            t.b.r[sem] = v
        for t in writes:
            t.b.w[sem] = v
        self.nops += 1

    def barrier(self):
        deps = {s: v for s, v in self.cnt.items() if v > 0}
        for eng in ("sp", "pe", "act", "dve", "pool"):
            self._wait(eng, deps)


def bc_last(ap, n):
    return bass.AP(ap.tensor, ap.offset, [list(ap.ap[0]), list(ap.ap[1]), [0, n]])


def bc_mid(ap, n):
    return bass.AP(ap.tensor, ap.offset, [list(ap.ap[0]), [0, n]] + [list(a) for a in ap.ap[1:]])


def bc_part(ap, n=128):
    return bass.AP(ap.tensor, ap.offset, [[0, n]] + [list(a) for a in ap.ap[1:]])


def build(S=4096, nlayers_dbg=None, stop_after=None):
    NT = S // 128
    NB = S // 512
    nc = bass.Bass("TRN2", target_bir_lowering=False)

    def din(name, shape):
        return nc.dram_tensor(name, list(shape), F32, kind="ExternalInput").ap()

    x_in = din("x", [S, D])
    mem_in = din("mem", [MEM, D])
    attn_norm = din("attn_norm", [1, D])
    attn_w_qkv = din("attn_w_qkv", [1, D, 1536])
    attn_q_gain = din("attn_q_gain", [1, 64])
    attn_k_gain = din("attn_k_gain", [1, 64])
    attn_w_o = din("attn_w_o", [1, D, D])
    pool_norm = din("pool_norm", [1, D])
    pool_w = din("pool_w", [1, 4, 256, 256])
    pool_scale = din("pool_scale", [1, D])
    xattn_norm = din("xattn_norm", [2, D])
    mem_norm = din("mem_norm", [2, D])
    xattn_w_q = din("xattn_w_q", [2, D, D])
    xattn_w_kv = din("xattn_w_kv", [2, D, 2 * D])
    xattn_w_o = din("xattn_w_o", [2, D, D])
    ffn_norm = din("ffn_norm", [2, D])
    ffn_w_up = din("ffn_w_up", [2, D, 2 * DFF])
    ffn_conv_w = din("ffn_conv_w", [2, 3, 2 * DFF])
    ffn_conv_b = din("ffn_conv_b", [2, 2 * DFF])
    ffn_w_down = din("ffn_w_down", [2, DFF, D])
    final_norm = din("final_norm", [1, D])
    rope_in = din("rope", [S, 128])
    invc_in = din("invc", [1, 64])
    out = nc.dram_tensor("out", [S, D], F32, kind="ExternalOutput").ap()
    rA = nc.dram_tensor("resA", [S, D], F32, kind="Internal").ap()
    rB = nc.dram_tensor("resB", [S, D], F32, kind="Internal").ap()
    def dscr(name, shape):
        if not PRECAST:
            return None
        return nc.dram_tensor(name, list(shape), BF16, kind="Internal").ap()

    wq_b = dscr("wq_b", [2, D, D]); wkv_b = dscr("wkv_b", [2, D, 2 * D]); wo_b = dscr("wo_b", [2, D, D])
    wup_b = dscr("wup_b", [2, D, 2 * DFF]); wd_b = dscr("wd_b", [2, DFF, D]); wp_b = dscr("wp_b", [4, 256, 256])
    d_wq = [T(None, f"d_wq{l}") for l in range(2)]; d_wkv = [T(None, f"d_wkv{l}") for l in range(2)]
    d_wo = [T(None, f"d_wo{l}") for l in range(2)]; d_wup = [T(None, f"d_wup{l}") for l in range(2)]
    d_wd = [T(None, f"d_wd{l}") for l in range(2)]; d_wp = T(None, "d_wp")
    dA = T(None, "resA")
    dB = T(None, "resB")
    dX = T(None, "x")
    dO = T(None, "out")

    with ExitStack() as st:
        SC = Sched(nc, st)
        op = SC.op
        dma = SC.dma

        uid = [0]

        def sbuf(stack, name, shape, dt=F32):
            uid[0] += 1
            name = f"{name}_{uid[0]}"
            return T(stack.enter_context(nc.sbuf_tensor(name, list(shape), dt)), name)

        def psum(stack, name, shape, dt=F32):
            return T(stack.enter_context(nc.psum_tensor(name, list(shape), dt)), name)

        ident = sbuf(st, "ident", [128, 128], BF16)
        identf = sbuf(st, "identf", [128, 128], F32)
        ones_f = sbuf(st, "ones_f", [128, 128], F32)
        ones_b = sbuf(st, "ones_b", [128, 128], BF16)
        for idt in (ident, identf):
            op("pool", lambda e, idt=idt: e.memset(idt[:], 0.0), writes=[idt])
            op("pool", lambda e, idt=idt: e.affine_select(out=idt[:], in_=idt[:], pattern=[[-1, 128]],
                                                         compare_op=ALU.not_equal, fill=1.0, base=0,
                                                         channel_multiplier=1), reads=[idt], writes=[idt])
        op("dve", lambda e: e.memset(ones_f[:], 1.0), writes=[ones_f])
        op("dve", lambda e: e.memset(ones_b[:], 1.0), writes=[ones_b])

        def alloc_psum(stack, nf32):
            uid[0] += 1
            pb = [psum(stack, f"pb{i}_{uid[0]}", [128, 512], F32) for i in range(nf32)]
            pt = [psum(stack, f"pt{i}_{uid[0]}", [128, 1024], BF16) for i in range(8 - nf32)]
            return pb, pt

        def load_weight_bf16(w_t, src2d, nchunk, ncols, sem, col_split=1, p=128):
            dst3 = w_t[:].rearrange("p (c n) -> p c n", c=nchunk)
            src3 = src2d.rearrange("(c p) n -> p c n", p=p)
            step = ncols // col_split
            for i in range(col_split):
                dma("pool", dst3[:, :, i * step:(i + 1) * step], src3[:, :, i * step:(i + 1) * step],
                    writes=[w_t], sem=sem)

        def load_weight_sb(w_t, src2d, nchunk, ncols, sem, dsrc_t, col_lo=0, col_hi=None, deps_t=None, p=128, src32=None):
            col_hi = ncols if col_hi is None else col_hi
            dst3 = w_t[:].rearrange("p (c n) -> p c n", c=nchunk)
            if PRECAST:
                src3 = src2d.rearrange("(c p) n -> p c n", p=p)
                dma("sp", dst3[:, :, col_lo:col_hi], src3[:, :, col_lo:col_hi], reads=[dsrc_t],
                    writes=[deps_t if deps_t is not None else w_t], sem=sem)
            else:
                src3 = src32.rearrange("(c p) n -> p c n", p=p)
                step = 1024
                for lo in range(col_lo, col_hi, step):
                    hi = min(col_hi, lo + step)
                    dma("pool", dst3[:, :, lo:hi], src3[:, :, lo:hi], writes=[deps_t if deps_t is not None else w_t], sem=sem)

        def precast_all():
            if not PRECAST:
                return
            def cast2d(dst2d, src2d, t, ncols, split):
                step = ncols // split
                for i in range(split):
                    dma("pool", dst2d[:, i * step:(i + 1) * step], src2d[:, i * step:(i + 1) * step], writes=[t], sem="pc")
                SC._wait("pool", {"pool_pc": SC.cnt["pool_pc"]})
            for l in range(2):
                cast2d(wkv_b[l], xattn_w_kv[l], d_wkv[l], 2 * D, 2)
                cast2d(wq_b[l], xattn_w_q[l], d_wq[l], D, 1)
                cast2d(wo_b[l], xattn_w_o[l], d_wo[l], D, 1)
                cast2d(wup_b[l], ffn_w_up[l], d_wup[l], 2 * DFF, 4)
                cast2d(wd_b[l], ffn_w_down[l], d_wd[l], D, 1)
                if l == 0:
                    cast2d(wp_b.rearrange("g r n -> (g r) n"), pool_w[0].rearrange("g r n -> (g r) n"), d_wp, 256, 1)

        def rms_stats_st(xt_ap, xt, npart, sq, ss, lnv, rstd, width=D, out_bias=0.0):
            def a():
                op("dve", lambda e: e.tensor_tensor(out=sq[0:npart, 0:width], in0=xt_ap, in1=xt_ap, op=ALU.mult),
                   reads=[xt], writes=[sq])
                op("dve", lambda e: e.tensor_reduce(out=ss[0:npart, 0:1], in_=sq[0:npart, 0:width], axis=AX.X, op=ALU.add),
                   reads=[sq], writes=[ss])

            def b():
                op("act", lambda e: e.activation(out=lnv[0:npart, 0:1], in_=ss[0:npart, 0:1], func=AF.Ln,
                                                 scale=1.0 / width, bias=EPS), reads=[ss], writes=[lnv])
                op("act", lambda e: e.activation(out=rstd[0:npart, 0:1], in_=lnv[0:npart, 0:1], func=AF.Exp,
                                                 scale=-0.5, bias=out_bias), reads=[lnv], writes=[rstd])
            return [a, b]

        def rms_stats(*a, **k):
            for f in rms_stats_st(*a, **k):
                f()

        def norm_st(xt, gain, hb, sq, ss, lnv, rstd, npart=128):
            def c():
                op("dve", lambda e: e.scalar_tensor_tensor(out=hb[0:npart, :], in0=xt[0:npart, :], scalar=rstd[0:npart, 0:1],
                                                          in1=gain[0:npart, :], op0=ALU.mult, op1=ALU.mult),
                   reads=[xt, rstd, gain], writes=[hb])
            return rms_stats_st(xt[0:npart, :], xt, npart, sq, ss, lnv, rstd) + [c]

        def norm_to_bf16(*a, **k):
            for f in norm_st(*a, **k):
                f()

        def transpose_h(hb, pt, npart=128):
            op("pe", lambda e: [e.transpose(out=pt[:, c * npart:(c + 1) * npart], in_=hb[0:npart, c * 128:(c + 1) * 128],
                                            identity=ident[0:npart, 0:npart]) for c in range(8)],
               reads=[hb, ident], writes=[pt])

        def qk_norm_rope_st(src, H, gain, rope, dst, tmp, ss, lnv, rstd, out_bias):
            W = H * 64
            sq, ta, tb = tmp
            s3 = src[:, 0:W].rearrange("p (h d) -> p h d", d=64)
            a3 = ta[:, 0:W].rearrange("p (h d) -> p h d", d=64)
            b3 = tb[:, 0:W].rearrange("p (h d) -> p h d", d=64)
            a5 = ta[:, 0:W].rearrange("p (h a f q) -> p h a f q", a=2, f=2, q=16)
            s5 = sq[:, 0:W].rearrange("p (h a f q) -> p h a f q", a=2, f=2, q=16)
            nsin = bc_mid(rope[:, 64:96].rearrange("p (a q) -> p a q", a=2), H)
            psin = bc_mid(rope[:, 96:128].rearrange("p (a q) -> p a q", a=2), H)

            def a():
                op("dve", lambda e: e.tensor_tensor(out=sq[:, 0:W], in0=src[:, 0:W], in1=src[:, 0:W], op=ALU.mult),
                   reads=[src], writes=[sq])
                op("dve", lambda e: e.tensor_reduce(out=ss[:, 0:H], in_=sq[:, 0:W].rearrange("p (h d) -> p h d", d=64),
                                                    axis=AX.X, op=ALU.add), reads=[sq], writes=[ss])

            def b():
                op("act", lambda e: e.activation(out=lnv[:, 0:H], in_=ss[:, 0:H], func=AF.Ln, scale=1.0 / 64, bias=EPS),
                   reads=[ss], writes=[lnv])
                op("act", lambda e: e.activation(out=rstd[:, 0:H], in_=lnv[:, 0:H], func=AF.Exp, scale=-0.5, bias=out_bias),
                   reads=[lnv], writes=[rstd])

            def c():
                op("dve", lambda e: e.tensor_tensor(out=a3, in0=s3, in1=bc_last(rstd[:, 0:H], 64), op=ALU.mult),
                   reads=[src, rstd], writes=[ta])
                op("dve", lambda e: e.tensor_tensor(out=a3, in0=a3, in1=bc_mid(gain[:, 0:64], H), op=ALU.mult),
                   reads=[ta, gain], writes=[ta])
                op("dve", lambda e: e.tensor_tensor(out=b3, in0=a3, in1=bc_mid(rope[:, 0:64], H), op=ALU.mult),
                   reads=[ta, rope], writes=[tb])
                op("dve", lambda e: e.tensor_tensor(out=s5[:, :, :, 0, :], in0=a5[:, :, :, 1, :], in1=nsin, op=ALU.mult),
                   reads=[ta, rope], writes=[sq])
                op("dve", lambda e: e.tensor_tensor(out=s5[:, :, :, 1, :], in0=a5[:, :, :, 0, :], in1=psin, op=ALU.mult),
                   reads=[ta, rope], writes=[sq])
                op("dve", lambda e: e.tensor_tensor(out=dst[:, 0:W], in0=tb[:, 0:W], in1=sq[:, 0:W], op=ALU.add),
                   reads=[tb, sq], writes=[dst])
            return [a, b, c]

        def phase_attention(src, dsrc, dst, ddst):
            with ExitStack() as ph:
                PB, PT1 = alloc_psum(ph, 7)
                PT = [PT1[0], PT1[0]]
                wqkv = sbuf(ph, "wqkv", [128, 8 * 1536], BF16)
                wo = sbuf(ph, "wo_a", [64, 16 * D], BF16)
                KT = sbuf(ph, "KT", [128, 4 * S], BF16)
                Vs = sbuf(ph, "Vs", [128, NT * 4 * 65], BF16)
                gain = sbuf(ph, "gain_a", [128, D])
                qg = sbuf(ph, "qg", [128, 64])
                kg = sbuf(ph, "kg", [128, 64])
                xs = [sbuf(ph, f"xs{i}", [128, D]) for i in range(3)]
                rp = [sbuf(ph, f"rp{i}", [128, 128]) for i in range(3)]
                hb = [sbuf(ph, f"hb{i}", [128, D], BF16) for i in range(2)]
                hT = [sbuf(ph, f"hT{i}", [128, 8 * 128], BF16) for i in range(2)]
                sq = sbuf(ph, "sq", [128, D])
                ta = sbuf(ph, "ta", [128, D])
                tb = sbuf(ph, "tb", [128, D])
                qs = sbuf(ph, "qs", [128, D])
                qr = sbuf(ph, "qr", [128, D], BF16)
                QT = [sbuf(ph, f"QT{i}", [128, 16 * 128], BF16) for i in range(2)]
                Pb = [sbuf(ph, f"Pb{i}", [128, 512], BF16) for i in range(3)]
                OT = [sbuf(ph, f"OT{i}", [64, 16 * 128], BF16) for i in range(2)]
                rden = sbuf(ph, "rden", [128, 512])
                bcs = sbuf(ph, "bcs", [64, 512])
                xn = [sbuf(ph, f"xn{i}", [128, D]) for i in range(2)]
                ss = sbuf(ph, "ss", [128, 16]); lnv = sbuf(ph, "lnv", [128, 16]); rstd = sbuf(ph, "rstd", [128, 16])
                ss1 = sbuf(ph, "ss1", [128, 1]); lnv1 = sbuf(ph, "lnv1", [128, 1]); rstd1 = sbuf(ph, "rstd1", [128, 1])

                load_weight_bf16(wqkv, attn_w_qkv[0], 8, 1536, "w0")
                dma("pool", wo[:].rearrange("d (h n) -> d h n", h=16), attn_w_o[0].rearrange("(h d) n -> d h n", d=64),
                    writes=[wo], sem="w1")
                dma("sp", gain[:], bc_part(attn_norm), writes=[gain], sem="c0")
                dma("sp", qg[:], bc_part(attn_q_gain), writes=[qg], sem="c0")
                dma("sp", kg[:], bc_part(attn_k_gain), writes=[kg], sem="c0")
                op("pool", lambda e: e.memset(KT[64:128, :], 0.0), writes=[KT])
                for q in QT:
                    op("pool", lambda e, q=q: e.memset(q[64:128, :], 0.0), writes=[q])
                op("dve", lambda e: e.memset(Vs[:].rearrange("p (t e) -> p t e", e=65)[:, :, 64:65], 1.0), writes=[Vs])
                precast_all()

                def load_tile(t, i):
                    dma("sp", xs[i][:], src[t * 128:(t + 1) * 128, :], reads=[dsrc], writes=[xs[i]], sem=f"lx{i}")
                    dma("sp", rp[i][:], rope_in[t * 128:(t + 1) * 128, :], writes=[rp[i]], sem=f"lr{i}")

                sqn = sbuf(ph, "sqn", [128, D])

                def kv_stA(t):
                    i, j = t % 3, t % 2
                    sts = [lambda: load_tile(t + 1, (t + 1) % 3) if t + 1 < NT else None]
                    sts += norm_st(xs[i], gain, hb[j], sqn, ss1, lnv1, rstd1)
                    sts.append(lambda: transpose_h(hb[j], PT[0]))
                    sts.append(lambda: op("dve", lambda e: e.tensor_copy(out=hT[j][:], in_=PT[0][:]), reads=[PT[0]], writes=[hT[j]]))
                    return sts

                def kv_stB(t):
                    i, j = t % 3, t % 2
                    pk = PB[j]
                    pt = PB7b
                    sts = [lambda: op("pe", lambda e: [e.matmul(pk[:], lhsT=hT[j][:, c * 128:(c + 1) * 128],
                                                                rhs=wqkv[:, c * 1536 + 1024:c * 1536 + 1536],
                                                                start=(c == 0), stop=(c == 7)) for c in range(8)],
                                      reads=[hT[j], wqkv], writes=[pk]),
                           lambda: op("act", lambda e: e.activation(out=qs[:, 0:512], in_=pk[:], func=AF.Copy),
                                      reads=[pk], writes=[qs])]
                    sts += qk_norm_rope_st(qs, 4, kg, rp[i], qr, (sq, ta, tb), ss, lnv, rstd, 0.0)
                    sts.append(lambda: op("act", lambda e: e.activation(
                        out=Vs[:, t * 260:(t + 1) * 260].rearrange("p (g e) -> p g e", e=65)[:, :, 0:64],
                        in_=qs[:, 256:512].rearrange("p (g d) -> p g d", d=64), func=AF.Copy), reads=[qs], writes=[Vs]))
                    sts.append(lambda: op("pe", lambda e: [e.transpose(out=pt[0:64, g * 128:(g + 1) * 128],
                                                                       in_=qr[:, g * 64:(g + 1) * 64], identity=ident[:])
                                                           for g in range(4)], reads=[qr, ident], writes=[pt]))
                    sts.append(lambda: op("dve", lambda e: e.tensor_copy(
                        out=KT[0:64, :].rearrange("p (g s) -> p g s", g=4)[:, :, t * 128:(t + 1) * 128],
                        in_=pt[0:64, 0:512].rearrange("p (g s) -> p g s", g=4)), reads=[pt], writes=[KT]))
                    return sts

                PB7b = PT[0]
                load_tile(0, 0)
                for f in kv_stA(0):
                    f()
                for t in range(NT):
                    sa = kv_stA(t + 1) if t + 1 < NT else []
                    sb_ = kv_stB(t)
                    for k in range(max(len(sa), len(sb_))):
                        if k < len(sb_):
                            sb_[k]()
                        if k < len(sa):
                            sa[k]()

                def prep_stages(qb):
                    i, j = qb % 3, qb % 2
                    sts = [lambda: load_tile(qb, i)]
                    sts += norm_st(xs[i], gain, hb[j], sqn, ss1, lnv1, rstd1)
                    sts.append(lambda: transpose_h(hb[j], PT[0]))
                    sts.append(lambda: op("dve", lambda e: e.tensor_copy(out=hT[j][:], in_=PT[0][:]), reads=[PT[0]], writes=[hT[j]]))

                    def qproj(half):
                        pk = PB[half]
                        op("pe", lambda e: [e.matmul(pk[:], lhsT=hT[j][:, c * 128:(c + 1) * 128],
                                                     rhs=wqkv[:, c * 1536 + half * 512:c * 1536 + half * 512 + 512],
                                                     start=(c == 0), stop=(c == 7)) for c in range(8)],
                           reads=[hT[j], wqkv], writes=[pk])

                    def qcopy(half):
                        pk = PB[half]
                        op("act", lambda e: e.activation(out=qs[:, half * 512:(half + 1) * 512], in_=pk[:], func=AF.Copy),
                           reads=[pk], writes=[qs])
                    def qproj4(half, part):
                        pk = PB[half]
                        op("pe", lambda e: [e.matmul(pk[:], lhsT=hT[j][:, c * 128:(c + 1) * 128],
                                                     rhs=wqkv[:, c * 1536 + half * 512:c * 1536 + half * 512 + 512],
                                                     start=(c == 0), stop=(c == 7)) for c in range(part * 4, part * 4 + 4)],
                           reads=[hT[j], wqkv], writes=[pk])
                    sts.append(lambda: (qproj(0), qproj(1)))
                    sts.append(lambda: (qcopy(0), qcopy(1)))
                    sts += qk_norm_rope_st(qs, 16, qg, rp[i], qr, (sq, ta, tb), ss, lnv, rstd, float(np.log(0.125)))

                    def qtr(half):
                        pt = PT[0]
                        op("pe", lambda e: [e.transpose(out=pt[0:64, h * 128:(h + 1) * 128],
                                                        in_=qr[:, (half * 8 + h) * 64:(half * 8 + h + 1) * 64],
                                                        identity=ident[:]) for h in range(8)],
                           reads=[qr, ident], writes=[pt])

                    def qev(half):
                        pt = PT[0]
                        op("dve", lambda e: e.tensor_copy(out=QT[j][0:64, half * 1024:(half + 1) * 1024], in_=pt[0:64, :]),
                           reads=[pt], writes=[QT[j]])
                    sts.append(lambda: qtr(0))
                    sts.append(lambda: qev(0))
                    sts.append(lambda: qtr(1))
                    sts.append(lambda: qev(1))
                    return sts

                for f in prep_stages(0):
                    f()
                SB_ = [PB[2], PB[3], PB[4]]
                OB_ = [PB[5], PB[6]]
                steps = [(qb, g, kt) for qb in range(NT) for g in range(4) for kt in range(NT)]
                nsteps = len(steps)
                DEPTH = 3
                deferred = {}

                def defer(i, fn):
                    deferred.setdefault(i, []).append(fn)

                def s_mm(i):
                    qb, g, kt = steps[i]
                    sbk = SB_[i % DEPTH]
                    jq = qb % 2
                    op("pe", lambda e: e.matmul(sbk[:], lhsT=KT[:, g * S + kt * 128:g * S + (kt + 1) * 128],
                                                rhs=QT[jq][:, g * 512:(g + 1) * 512], start=True, stop=True),
                       reads=[KT, QT[jq]], writes=[sbk])

                def normalize(qb, g, ob):
                    jq = qb % 2
                    bcb = PB[g % 2]
                    op("pe", lambda e: e.matmul(bcb[0:64, :], lhsT=ones_f[64:65, 0:64], rhs=rden[64:65, :],
                                                start=True, stop=True), reads=[ones_f, rden], writes=[bcb])
                    op("dve", lambda e: e.tensor_copy(out=bcs[:], in_=bcb[0:64, :]), reads=[bcb], writes=[bcs])
                    op("dve", lambda e: e.tensor_tensor(out=OT[jq][:, g * 512:(g + 1) * 512], in0=ob[0:64, :], in1=bcs[:],
                                                        op=ALU.mult), reads=[ob, bcs], writes=[OT[jq]])

                def wo_stages(qb):
                    jq = qb % 2
                    xi = xs[qb % 3]
                    xo = xn[jq]
                    sts = []

                    def mm16(half):
                        pk = PB[half]
                        op("pe", lambda e: [e.matmul(pk[:], lhsT=OT[jq][:, h * 128:(h + 1) * 128],
                                                     rhs=wo[:, h * D + half * 512:h * D + half * 512 + 512],
                                                     start=(h == 0), stop=(h == 15)) for h in range(16)],
                           reads=[OT[jq], wo], writes=[pk])

                    def resid(half):
                        pk = PB[half]
                        op("dve", lambda e: e.tensor_tensor(out=xo[:, half * 512:(half + 1) * 512], in0=pk[:],
                                                            in1=xi[:, half * 512:(half + 1) * 512], op=ALU.add),
                           reads=[pk, xi], writes=[xo])
                    for half in range(2):
                        sts.append(lambda half=half: mm16(half))
                        sts.append(lambda half=half: resid(half))
                    sts.append(lambda: dma("sp", dst[qb * 128:(qb + 1) * 128, :], xo[:], reads=[xo], writes=[ddst], sem=f"so{jq}"))
                    return sts

                for i in range(min(DEPTH, nsteps)):
                    s_mm(i)
                for i, (qb, g, kt) in enumerate(steps):
                    if qb + 1 < NT:
                        loc = g * NT + kt
                        if loc == 0:
                            cur_prep = prep_stages(qb + 1)
                            spacing = max(1, (4 * NT - 8) // len(cur_prep))
                        if loc % spacing == 0 and loc // spacing < len(cur_prep):
                            cur_prep[loc // spacing]()
                    ob = OB_[g % 2]
                    sbk = SB_[i % DEPTH]
                    pb = Pb[i % 3]
                    op("act", lambda e: e.activation(out=pb[:], in_=sbk[:], func=AF.Exp), reads=[sbk], writes=[pb])
                    op("pe", lambda e: e.matmul(ob[0:65, :], lhsT=Vs[:, (kt * 4 + g) * 65:(kt * 4 + g + 1) * 65],
                                                rhs=pb[:], start=(kt == 0), stop=(kt == NT - 1)),
                       reads=[Vs, pb], writes=[ob])
                    if i + DEPTH < nsteps:
                        s_mm(i + DEPTH)
                    for fn in deferred.pop(i, []):
                        fn()
                    if kt == NT - 1:
                        op("dve", lambda e: e.reciprocal(out=rden[64:65, :], in_=ob[64:65, :]), reads=[ob], writes=[rden])
                        dd_ = max(1, min(8, NT // 2))
                        if i + dd_ + 16 >= nsteps:
                            normalize(qb, g, ob)
                            if g == 3:
                                for f in wo_stages(qb):
                                    f()
                        else:
                            defer(i + dd_, lambda qb=qb, g=g, ob=ob: normalize(qb, g, ob))
                            if g == 3:
                                for k_, f in enumerate(wo_stages(qb)):
                                    defer(i + dd_ + WO_OFF + k_ * WO_STRIDE, f)
                SC.barrier()

        def phase_xattn(l, src, dsrc, dst, ddst):
            with ExitStack() as ph:
                PB, PT = alloc_psum(ph, 6)
                wq = sbuf(ph, "wq", [128, 8 * D], BF16)
                wkv = sbuf(ph, "wkv", [128, 8 * 2 * D], BF16)
                wo = sbuf(ph, "wo_x", [128, 8 * D], BF16)
                KmT = sbuf(ph, "KmT", [128, 8 * MEM], BF16)
                Vm = sbuf(ph, "Vm", [128, 2 * D], BF16)
                gain = sbuf(ph, "gain_x", [128, D])
                mgain = sbuf(ph, "mgain", [128, D])
                xs = [sbuf(ph, f"xs{i}", [128, D]) for i in range(8)]
                hb = [sbuf(ph, f"hb{i}", [128, D], BF16) for i in range(2)]
                hT = sbuf(ph, "hT", [128, 8 * 512], BF16)
                memT = sbuf(ph, "memT", [128, 8 * MEM], BF16)
                qT = sbuf(ph, "qT", [128, 8 * 512], BF16)
                Pb = [sbuf(ph, f"Pb{i}", [128, 512], BF16) for i in range(4)]
                OT = sbuf(ph, "OT", [128, 8 * 512], BF16)
                rden = [sbuf(ph, f"rden{i}", [128, 512]) for i in range(2)]
                xn = [sbuf(ph, f"xn{i}", [128, D]) for i in range(2)]
                sq = sbuf(ph, "sq", [128, D])
                ss1 = sbuf(ph, "ss1", [128, 1]); lnv1 = sbuf(ph, "lnv1", [128, 1]); rstd1 = sbuf(ph, "rstd1", [128, 1])

                load_weight_sb(wkv, wkv_b[l] if PRECAST else None, 8, 2 * D, "w0", d_wkv[l], src32=xattn_w_kv[l])
                load_weight_sb(wq, wq_b[l] if PRECAST else None, 8, D, "w1", d_wq[l], src32=xattn_w_q[l])
                load_weight_sb(wo, wo_b[l] if PRECAST else None, 8, D, "w2", d_wo[l], src32=xattn_w_o[l])
                dma("sp", gain[:], bc_part(xattn_norm[l:l + 1, :]), writes=[gain], sem="c0")
                dma("sp", mgain[:], bc_part(mem_norm[l:l + 1, :]), writes=[mgain], sem="c0")

                for mt in range(2):
                    dma("sp", xs[mt][:], mem_in[mt * 128:(mt + 1) * 128, :], writes=[xs[mt]], sem=f"lx{mt}")
                for mt in range(2):
                    norm_to_bf16(xs[mt], mgain, hb[mt], sq, ss1, lnv1, rstd1)
                    transpose_h(hb[mt], PT[mt])
                    op("dve", lambda e, mt=mt: e.tensor_copy(
                        out=memT[:].rearrange("p (c m) -> p c m", c=8)[:, :, mt * 128:(mt + 1) * 128],
                        in_=PT[mt][:].rearrange("p (c m) -> p c m", c=8)), reads=[PT[mt]], writes=[memT])
                for dc in range(8):
                    pk = PB[dc % 2]
                    op("pe", lambda e, pk=pk, dc=dc: [e.matmul(pk[:, 0:MEM], lhsT=wkv[:, c * 2048 + dc * 128:c * 2048 + (dc + 1) * 128],
                                                               rhs=memT[:, c * MEM:(c + 1) * MEM], start=(c == 0), stop=(c == 7))
                                                      for c in range(8)], reads=[wkv, memT], writes=[pk])
                    op("dve", lambda e, pk=pk, dc=dc: e.tensor_copy(out=KmT[:, dc * MEM:(dc + 1) * MEM], in_=pk[:, 0:MEM]),
                       reads=[pk], writes=[KmT])
                for mt in range(2):
                    for half in range(2):
                        pk = PB[2 + (mt * 2 + half) % 2]
                        op("pe", lambda e, pk=pk, mt=mt, half=half: [
                            e.matmul(pk[:], lhsT=memT[:, c * MEM + mt * 128:c * MEM + (mt + 1) * 128],
                                     rhs=wkv[:, c * 2048 + 1024 + half * 512:c * 2048 + 1024 + (half + 1) * 512],
                                     start=(c == 0), stop=(c == 7)) for c in range(8)], reads=[wkv, memT], writes=[pk])
                        op("act", lambda e, pk=pk, mt=mt, half=half: e.activation(
                            out=Vm[:, mt * D + half * 512:mt * D + (half + 1) * 512], in_=pk[:], func=AF.Copy),
                           reads=[pk], writes=[Vm])

                def load_blk(blk):
                    for jj in range(4):
                        t = blk * 4 + jj
                        i = (blk % 2) * 4 + jj
                        dma("sp", xs[i][:], src[t * 128:(t + 1) * 128, :], reads=[dsrc], writes=[xs[i]], sem=f"lx{i}")

                def norm_tile(blk, jj):
                    i = (blk % 2) * 4 + jj
                    norm_to_bf16(xs[i], gain, hb[jj % 2], sq, ss1, lnv1, rstd1)

                def tr_evac(jj):
                    k = jj % 2
                    transpose_h(hb[k], PT[k])
                    op("dve", lambda e: e.tensor_copy(
                        out=hT[:].rearrange("p (c m) -> p c m", c=8)[:, :, jj * 128:(jj + 1) * 128],
                        in_=PT[k][:].rearrange("p (c m) -> p c m", c=8)), reads=[PT[k]], writes=[hT])

                load_blk(0)
                for jj in range(4):
                    norm_tile(0, jj)
                    tr_evac(jj)
                for blk in range(NB):
                    if blk + 1 < NB:
                        load_blk(blk + 1)
                    for dc in range(8):
                        pk = PB[dc % 2]
                        op("pe", lambda e, pk=pk, dc=dc: [e.matmul(pk[:], lhsT=wq[:, c * D + dc * 128:c * D + (dc + 1) * 128],
                                                                   rhs=hT[:, c * 512:(c + 1) * 512], start=(c == 0), stop=(c == 7))
                                                          for c in range(8)], reads=[wq, hT], writes=[pk])
                        op("act", lambda e, pk=pk, dc=dc: e.activation(out=qT[:, dc * 512:(dc + 1) * 512], in_=pk[:],
                                                                       func=AF.Copy, scale=1.0 / 16.0), reads=[pk], writes=[qT])
                    for h in range(4):
                        pbs = [Pb[(h % 2) * 2 + m] for m in range(2)]
                        for m in range(2):
                            sbk = PB[2 + m]
                            op("pe", lambda e, sbk=sbk, m=m: [
                                e.matmul(sbk[:], lhsT=KmT[:, (2 * h + jd) * MEM + m * 128:(2 * h + jd) * MEM + (m + 1) * 128],
                                         rhs=qT[:, (2 * h + jd) * 512:(2 * h + jd + 1) * 512], start=(jd == 0), stop=(jd == 1))
                                for jd in range(2)], reads=[KmT, qT], writes=[sbk])
                            op("act", lambda e, sbk=sbk, m=m: e.activation(out=pbs[m][:], in_=sbk[:], func=AF.Exp),
                               reads=[sbk], writes=[pbs[m]])
                        dbk = PB[4]
                        rd = rden[h % 2]
                        op("pe", lambda e: [e.matmul(dbk[:], lhsT=ones_b[:], rhs=pbs[m][:], start=(m == 0), stop=(m == 1))
                                            for m in range(2)], reads=[ones_b, pbs[0], pbs[1]], writes=[dbk])
                        op("act", lambda e: e.activation(out=rd[:], in_=dbk[:], func=AF.Ln), reads=[dbk], writes=[rd])
                        op("act", lambda e: e.activation(out=rd[:], in_=rd[:], func=AF.Exp, scale=-1.0), reads=[rd], writes=[rd])
                        for jd in range(2):
                            obk = PB[5] if jd else PB[0]
                            op("pe", lambda e, obk=obk, jd=jd: [
                                e.matmul(obk[:], lhsT=Vm[:, m * D + h * 256 + jd * 128:m * D + h * 256 + (jd + 1) * 128],
                                         rhs=pbs[m][:], start=(m == 0), stop=(m == 1)) for m in range(2)],
                               reads=[Vm, pbs[0], pbs[1]], writes=[obk])
                            op("dve", lambda e, obk=obk, jd=jd: e.tensor_tensor(
                                out=OT[:, (2 * h + jd) * 512:(2 * h + jd + 1) * 512], in0=obk[:], in1=rd[:], op=ALU.mult),
                               reads=[obk, rd], writes=[OT])
                    for jj in range(4):
                        t = blk * 4 + jj
                        xi = xs[(blk % 2) * 4 + jj]
                        xo = xn[jj % 2]
                        if blk + 1 < NB:
                            norm_tile(blk + 1, jj)
                        for half in range(2):
                            pk = PB[half + 1]
                            op("pe", lambda e, pk=pk, half=half: [
                                e.matmul(pk[:], lhsT=OT[:, dc * 512 + jj * 128:dc * 512 + (jj + 1) * 128],
                                         rhs=wo[:, dc * D + half * 512:dc * D + (half + 1) * 512], start=(dc == 0), stop=(dc == 7))
                                for dc in range(8)], reads=[OT, wo], writes=[pk])
                            op("dve", lambda e, pk=pk, half=half: e.tensor_tensor(
                                out=xo[:, half * 512:(half + 1) * 512], in0=pk[:], in1=xi[:, half * 512:(half + 1) * 512],
                                op=ALU.add), reads=[pk, xi], writes=[xo])
                        dma("sp", dst[t * 128:(t + 1) * 128, :], xo[:], reads=[xo], writes=[ddst], sem=f"so{jj % 2}")
                        if blk + 1 < NB:
                            tr_evac(jj)
                SC.barrier()

        def phase_ffn(l, src, dsrc, dst, ddst, final):
            with ExitStack() as ph:
                PB, PT = alloc_psum(ph, 6)
                wup = sbuf(ph, "wup", [128, 8 * 2 * DFF], BF16)
                wd = sbuf(ph, "wd", [128, 22 * D], BF16)
                gain = sbuf(ph, "gain_f", [128, D])
                fgain = sbuf(ph, "fgain", [128, D]) if final else None
                cp = sbuf(ph, "cp", [128, 4 * NCH])
                xs = [sbuf(ph, f"xs{i}", [128, D]) for i in range(2)]
                xr = [sbuf(ph, f"xr{i}", [128, D]) for i in range(2)]
                hb = sbuf(ph, "hb", [128, D], BF16)
                hT = sbuf(ph, "hT", [128, 8 * 512], BF16)
                hTh = sbuf(ph, "hTh", [128, 8 * 16], BF16)
                uh = sbuf(ph, "uh", [128, NCH * 16])
                G = sbuf(ph, "G", [128, 22 * 512], BF16)
                accg = [sbuf(ph, f"accg{i}", [128, 512]) for i in range(2)]
                accv = [sbuf(ph, f"accv{i}", [128, 512]) for i in range(2)]
                sq = sbuf(ph, "sq", [128, D])
                ss2 = sbuf(ph, "ss2", [128, 1]); lnv2 = sbuf(ph, "lnv2", [128, 1]); rstd2 = sbuf(ph, "rstd2", [128, 1])
                sqf = sq
                cpr = sq
                xh = xr[0]
                ss1 = sbuf(ph, "ss1", [128, 1]); lnv1 = sbuf(ph, "lnv1", [128, 1]); rstd1 = sbuf(ph, "rstd1", [128, 1])

                wup_p = [T(wup.t, f"wup_p{p_}") for p_ in range(4)]
                for p_ in range(4):
                    g_lo, g_hi = p_ * 6, min(22, (p_ + 1) * 6)
                    for hh in range(2):
                        load_weight_sb(wup, wup_b[l] if PRECAST else None, 8, 2 * DFF, f"wu{p_}", d_wup[l], col_lo=(hh * 22 + g_lo) * 128,
                                       col_hi=(hh * 22 + g_hi) * 128, deps_t=wup_p[p_], src32=ffn_w_up[l])
                load_weight_sb(wd, wd_b[l] if PRECAST else None, 22, D, "w1", d_wd[l], src32=ffn_w_down[l])
                dma("sp", gain[:], bc_part(ffn_norm[l:l + 1, :]), writes=[gain], sem="c0")
                if final:
                    dma("sp", fgain[:], bc_part(final_norm), writes=[fgain], sem="c0")
                for k in range(3):
                    dma("sp", cpr[0:NCH, k * 128:(k + 1) * 128], ffn_conv_w[l, k].rearrange("(c p) -> c p", p=128),
                        writes=[cpr], sem="c1")
                dma("sp", cpr[0:NCH, 384:512], ffn_conv_b[l].rearrange("(c p) -> c p", p=128), writes=[cpr], sem="c1")
                op("pe", lambda e: [e.transpose(out=PB[0][:, k * NCH:(k + 1) * NCH], in_=cpr[0:NCH, k * 128:(k + 1) * 128],
                                                identity=identf[0:NCH, 0:NCH]) for k in range(4)],
                   reads=[cpr, identf], writes=[PB[0]])
                op("dve", lambda e: e.tensor_copy(out=cp[:], in_=PB[0][:, 0:4 * NCH]), reads=[PB[0]], writes=[cp])

                def cw(k, ch):
                    return cp[:, k * NCH + ch:k * NCH + ch + 1]

                op("dve", lambda e: e.memset(xh[0:16, :], 0.0), writes=[xh])
                for m in range(NB - 1):
                    dma("sp", xh[2 * m:2 * m + 2, :], src[512 * (m + 1) - 1:512 * (m + 1) + 1, :], reads=[dsrc], writes=[xh],
                        sem="lh")
                op("dve", lambda e: e.memset(uh[:], 0.0), writes=[uh])
                if NB > 1:
                    norm_to_bf16(xh, gain, hb, sq, ss1, lnv1, rstd1, npart=16)
                    transpose_h(hb, PT[0], npart=16)
                    op("dve", lambda e: e.tensor_copy(out=hTh[:], in_=PT[0][:, 0:128]), reads=[PT[0]], writes=[hTh])
                    for hh in range(2):
                        pk = PB[1 + hh]
                        op("pe", lambda e, pk=pk, hh=hh: [
                            e.matmul(pk[:, cc * 16:(cc + 1) * 16], lhsT=wup[:, c * 2 * DFF + (hh * 22 + cc) * 128:c * 2 * DFF + (hh * 22 + cc + 1) * 128],
                                     rhs=hTh[:, c * 16:(c + 1) * 16], start=(c == 0), stop=(c == 7))
                            for cc in range(22) for c in range(8)], reads=wup_p + [hTh], writes=[pk])
                        op("dve", lambda e, pk=pk, hh=hh: e.tensor_copy(out=uh[:, hh * 352:(hh + 1) * 352], in_=pk[:, 0:352]),
                           reads=[pk], writes=[uh])
                    uh4 = uh[:].rearrange("p (c m two) -> p c m two", c=NCH, two=2)
                    op("dve", lambda e: e.tensor_tensor(out=uh4[:, :, :, 0], in0=uh4[:, :, :, 0],
                                                        in1=bc_last(cp[:, 0:NCH], 8), op=ALU.mult), reads=[uh, cp], writes=[uh])
                    op("dve", lambda e: e.tensor_tensor(out=uh4[:, :, :, 1], in0=uh4[:, :, :, 1],
                                                        in1=bc_last(cp[:, 2 * NCH:3 * NCH], 8), op=ALU.mult), reads=[uh, cp], writes=[uh])

                def conv_chunk(blk, ch, ub, acc):
                    op("act", lambda e: e.activation(out=acc[:], in_=ub[:], func=AF.Identity, scale=cw(1, ch), bias=cw(3, ch)),
                       reads=[ub, cp], writes=[acc])
                    op("dve", lambda e: e.scalar_tensor_tensor(out=acc[:, 1:512], in0=ub[:, 0:511], scalar=cw(0, ch),
                                                               in1=acc[:, 1:512], op0=ALU.mult, op1=ALU.add),
                       reads=[ub, cp, acc], writes=[acc])
                    op("dve", lambda e: e.scalar_tensor_tensor(out=acc[:, 0:511], in0=ub[:, 1:512], scalar=cw(2, ch),
                                                               in1=acc[:, 0:511], op0=ALU.mult, op1=ALU.add),
                       reads=[ub, cp, acc], writes=[acc])
                    li = ch * 16 + 2 * (blk - 1)
                    ri = ch * 16 + 2 * blk + 1
                    if 0 < blk < NB - 1:
                        op("dve", lambda e: e.tensor_tensor(out=acc[:, 0:512:511], in0=acc[:, 0:512:511], in1=uh[:, li:li + 4:3],
                                                            op=ALU.add), reads=[acc, uh], writes=[acc])
                    elif blk > 0:
                        op("dve", lambda e: e.tensor_tensor(out=acc[:, 0:1], in0=acc[:, 0:1], in1=uh[:, li:li + 1], op=ALU.add),
                           reads=[acc, uh], writes=[acc])
                    elif blk < NB - 1:
                        op("dve", lambda e: e.tensor_tensor(out=acc[:, 511:512], in0=acc[:, 511:512], in1=uh[:, ri:ri + 1], op=ALU.add),
                           reads=[acc, uh], writes=[acc])

                def load_x(t, i):
                    dma("sp", xs[i][:], src[t * 128:(t + 1) * 128, :], reads=[dsrc], writes=[xs[i]], sem=f"lx{i}")

                def norm_stages(t):
                    sts = [lambda: load_x(t + 1, (t + 1) % 2) if t + 1 < NT else None]
                    sts += norm_st(xs[t % 2], gain, hb, sq, ss1, lnv1, rstd1)
                    return sts

                def tr_evac(jj):
                    transpose_h(hb, PT[jj % 2])
                    op("dve", lambda e: e.tensor_copy(
                        out=hT[:].rearrange("p (c m) -> p c m", c=8)[:, :, jj * 128:(jj + 1) * 128],
                        in_=PT[jj % 2][:].rearrange("p (c m) -> p c m", c=8)), reads=[PT[jj % 2]], writes=[hT])

                load_x(0, 0)
                for jj in range(4):
                    for f in norm_stages(jj):
                        f()
                    tr_evac(jj)
                for blk in range(NB):
                    for g in range(22):
                        k = g % 2
                        ug, uv = PB[2 * k], PB[2 * k + 1]
                        for (ub, ch) in ((ug, g), (uv, 22 + g)):
                            op("pe", lambda e, ub=ub, ch=ch: [
                                e.matmul(ub[:], lhsT=wup[:, c * 2 * DFF + ch * 128:c * 2 * DFF + (ch + 1) * 128],
                                         rhs=hT[:, c * 512:(c + 1) * 512], start=(c == 0), stop=(c == 7)) for c in range(8)],
                               reads=[wup_p[g // 6], hT], writes=[ub])
                        conv_chunk(blk, g, ug, accg[k])
                        conv_chunk(blk, 22 + g, uv, accv[k])
                        op("act", lambda e, k=k: e.activation(out=accg[k][:], in_=accg[k][:], func=AF.Silu),
                           reads=[accg[k]], writes=[accg[k]])
                        op("pool", lambda e, k=k, g=g: e.tensor_tensor(out=G[:, g * 512:(g + 1) * 512], in0=accg[k][:], in1=accv[k][:],
                                                                       op=ALU.mult), reads=[accg[k], accv[k]], writes=[G])
                    for jj in range(4):
                        t = blk * 4 + jj
                        nxt = blk + 1 < NB
                        xi = xr[jj % 2]
                        dma("sp", xi[:], src[t * 128:(t + 1) * 128, :], reads=[dsrc], writes=[xi], sem=f"lr{jj % 2}")
                        if nxt:
                            for f in norm_stages(t + 4):
                                f()
                        for half in range(2):
                            pk = PB[4 + half]
                            op("pe", lambda e, pk=pk, half=half, jj=jj: [
                                e.matmul(pk[:], lhsT=G[:, g * 512 + jj * 128:g * 512 + (jj + 1) * 128],
                                         rhs=wd[:, g * D + half * 512:g * D + (half + 1) * 512], start=(g == 0), stop=(g == 21))
                                for g in range(22)], reads=[G, wd], writes=[pk])
                        if nxt:
                            tr_evac(jj)
                        for half in range(2):
                            pk = PB[4 + half]
                            op("dve", lambda e, pk=pk, half=half, xi=xi: e.tensor_tensor(
                                out=xi[:, half * 512:(half + 1) * 512], in0=pk[:], in1=xi[:, half * 512:(half + 1) * 512],
                                op=ALU.add), reads=[pk, xi], writes=[xi])
                        if final:
                            rms_stats(xi[:, :], xi, 128, sqf, ss2, lnv2, rstd2)
                            op("dve", lambda e, xi=xi: e.scalar_tensor_tensor(out=xi[:], in0=xi[:], scalar=rstd2[:, 0:1],
                                                                              in1=fgain[:], op0=ALU.mult, op1=ALU.mult),
                               reads=[xi, rstd2, fgain], writes=[xi])
                        dma("sp", dst[t * 128:(t + 1) * 128, :], xi[:], reads=[xi], writes=[ddst], sem=f"so{jj % 2}")
                SC.barrier()

        def phase_pool(src, dsrc, dst, ddst):
            PADW = S + 16
            with ExitStack() as ph:
                PB, PT = alloc_psum(ph, 6)
                HT = sbuf(ph, "HT", [128, 8 * PADW], BF16)
                wp = sbuf(ph, "wp", [128, 8 * 256], BF16)
                gain = sbuf(ph, "gain_p", [128, D])
                pscale = sbuf(ph, "pscale", [128, D])
                invc = sbuf(ph, "invc", [128, 64])
                xs = [sbuf(ph, f"xs{i}", [128, D]) for i in range(3)]
                hb = [sbuf(ph, f"hb{i}", [128, D], BF16) for i in range(2)]
                Wa = [sbuf(ph, f"Wa{i}", [128, PADW]) for i in range(2)]
                Wb = [sbuf(ph, f"Wb{i}", [128, PADW]) for i in range(2)]
                edge = [sbuf(ph, f"edge{i}", [128, 16]) for i in range(2)]
                edgeb = [sbuf(ph, f"edgeb{i}", [128, 16], BF16) for i in range(2)]
                xn = [sbuf(ph, f"xn{i}", [128, D]) for i in range(2)]
                sq = sbuf(ph, "sq", [128, D])
                ss1 = sbuf(ph, "ss1", [128, 1]); lnv1 = sbuf(ph, "lnv1", [128, 1]); rstd1 = sbuf(ph, "rstd1", [128, 1])

                if PRECAST:
                    dma("sp", wp[:].rearrange("p (c n) -> p c n", c=8), wp_b.rearrange("g (j p) n -> p (g j) n", p=128),
                        reads=[d_wp], writes=[wp], sem="w0")
                else:
                    dma("pool", wp[:].rearrange("p (c n) -> p c n", c=8), pool_w[0].rearrange("g (j p) n -> p (g j) n", p=128),
                        writes=[wp], sem="w0")
                dma("sp", gain[:], bc_part(pool_norm), writes=[gain], sem="c0")
                dma("sp", pscale[:], bc_part(pool_scale), writes=[pscale], sem="c0")
                dma("sp", invc[:], bc_part(invc_in), writes=[invc], sem="c0")
                H3 = HT[:].rearrange("p (c w) -> p c w", c=8)
                op("pool", lambda e: e.memset(H3[:, :, 0:8], 0.0), writes=[HT])
                op("pool", lambda e: e.memset(H3[:, :, 8 + S:16 + S], 0.0), writes=[HT])

                def load_x(t, i):
                    dma("sp", xs[i][:], src[t * 128:(t + 1) * 128, :], reads=[dsrc], writes=[xs[i]], sem=f"lx{i}")

                load_x(0, 0)
                for t in range(NT):
                    if t + 1 < NT:
                        load_x(t + 1, (t + 1) % 3)
                    k = t % 2
                    norm_to_bf16(xs[t % 3], gain, hb[k], sq, ss1, lnv1, rstd1)
                    transpose_h(hb[k], PT[k])
                    op("act", lambda e, k=k, t=t: e.activation(out=H3[:, :, 8 + t * 128:8 + (t + 1) * 128],
                                                               in_=PT[k][:].rearrange("p (c m) -> p c m", c=8), func=AF.Copy),
                       reads=[PT[k]], writes=[HT])
                for c in range(8):
                    grp = c // 2
                    eng = "pool" if c in (5, 7) else "dve"
                    Hc = HT[:, c * PADW:(c + 1) * PADW]
                    ks = 1 if eng == "pool" else 0
                    A_, B_ = Wa[ks], Wb[ks]
                    ed, edb = edge[ks], edgeb[ks]
                    op(eng, lambda e: e.tensor_tensor(out=A_[:, 1:PADW], in0=Hc[:, 0:PADW - 1], in1=Hc[:, 1:PADW], op=ALU.add),
                       reads=[HT], writes=[A_])
                    cur, other = A_, B_
                    lo, hi = 1, PADW
                    sh = 1
                    for lvl in range(grp):
                        nlo, nhi = lo + sh, hi - sh
                        op(eng, lambda e, cur=cur, other=other, nlo=nlo, nhi=nhi, sh=sh: e.tensor_tensor(
                            out=other[:, nlo:nhi], in0=cur[:, nlo - sh:nhi - sh], in1=cur[:, nlo + sh:nhi + sh], op=ALU.add),
                           reads=[cur], writes=[other])
                        cur, other = other, cur
                        lo, hi = nlo, nhi
                        sh *= 2
                    w = 2 ** (grp + 1)
                    cur_e = bass.AP(cur.t[:, 8:16].tensor, cur.t[:, 8:16].offset, [list(cur.t[:, 8:16].ap[0]), [S - 8, 2], [1, 8]])
                    h_e = bass.AP(Hc[:, 8:16].tensor, Hc[:, 8:16].offset, [list(Hc[:, 8:16].ap[0]), [S - 8, 2], [1, 8]])
                    op(eng, lambda e: e.tensor_tensor(out=ed[:].rearrange("p (a b) -> p a b", a=2), in0=cur_e,
                                                      in1=invc[:, grp * 16:(grp + 1) * 16].rearrange("p (a b) -> p a b", a=2),
                                                      op=ALU.mult), reads=[cur, invc], writes=[ed])
                    op(eng, lambda e: e.tensor_tensor(out=edb[:].rearrange("p (a b) -> p a b", a=2),
                                                      in0=ed[:].rearrange("p (a b) -> p a b", a=2), in1=h_e, op=ALU.subtract),
                       reads=[ed, HT], writes=[edb])
                    if eng == "dve":
                        op(eng, lambda e: e.scalar_tensor_tensor(out=Hc[:, 8:8 + S], in0=cur[:, 8:8 + S], scalar=1.0 / w,
                                                                 in1=Hc[:, 8:8 + S], op0=ALU.mult, op1=ALU.subtract),
                           reads=[cur, HT], writes=[HT])
                    else:
                        op(eng, lambda e: e.tensor_scalar(out=cur[:, 8:8 + S], in0=cur[:, 8:8 + S], scalar1=1.0 / w, scalar2=0.0,
                                                          op0=ALU.mult, op1=ALU.add), reads=[cur], writes=[cur])
                        op(eng, lambda e: e.tensor_tensor(out=Hc[:, 8:8 + S], in0=cur[:, 8:8 + S], in1=Hc[:, 8:8 + S],
                                                          op=ALU.subtract), reads=[cur, HT], writes=[HT])
                    op(eng, lambda e: e.tensor_copy(out=h_e, in_=edb[:].rearrange("p (a b) -> p a b", a=2)),
                       reads=[edb], writes=[HT])
                load_x(0, 0)
                for t in range(NT):
                    if t + 1 < NT:
                        load_x(t + 1, (t + 1) % 3)
                    xi = xs[t % 3]
                    xo = xn[t % 2]
                    for half in range(2):
                        pk = PB[(t % 2) * 2 + half]
                        op("pe", lambda e, pk=pk, half=half, t=t: [
                            e.matmul(pk[:, gg * 256:(gg + 1) * 256],
                                     lhsT=HT[:, (4 * half + 2 * gg + jc) * PADW + 8 + t * 128:(4 * half + 2 * gg + jc) * PADW + 8 + (t + 1) * 128],
                                     rhs=wp[:, (4 * half + 2 * gg + jc) * 256:(4 * half + 2 * gg + jc + 1) * 256],
                                     start=(jc == 0), stop=(jc == 1)) for gg in range(2) for jc in range(2)],
                           reads=[HT, wp], writes=[pk])
                        op("dve", lambda e, pk=pk, half=half: e.tensor_tensor(
                            out=xo[:, half * 512:(half + 1) * 512], in0=pk[:], in1=pscale[:, half * 512:(half + 1) * 512],
                            op=ALU.mult), reads=[pk, pscale], writes=[xo])
                        op("dve", lambda e, half=half: e.tensor_tensor(
                            out=xo[:, half * 512:(half + 1) * 512], in0=xo[:, half * 512:(half + 1) * 512],
                            in1=xi[:, half * 512:(half + 1) * 512], op=ALU.add), reads=[xo, xi], writes=[xo])
                    dma("sp", dst[t * 128:(t + 1) * 128, :], xo[:], reads=[xo], writes=[ddst], sem=f"so{t % 2}")
                SC.barrier()

        SC.barrier()
        plan = [
            ("attn", lambda s, ds, d, dd: phase_attention(s, ds, d, dd)),
            ("xattn0", lambda s, ds, d, dd: phase_xattn(0, s, ds, d, dd)),
            ("ffn0", lambda s, ds, d, dd: phase_ffn(0, s, ds, d, dd, False)),
            ("pool", lambda s, ds, d, dd: phase_pool(s, ds, d, dd)),
            ("xattn1", lambda s, ds, d, dd: phase_xattn(1, s, ds, d, dd)),
            ("ffn1", lambda s, ds, d, dd: phase_ffn(1, s, ds, d, dd, True)),
        ]
        if stop_after is not None:
            plan = plan[:stop_after]
        cur, dcur = x_in, dX
        for i, (name, fn) in enumerate(plan):
            last = i == len(plan) - 1
            if last:
                nxt, dnxt = out, dO
            else:
                nxt, dnxt = (rA, dA) if i % 2 == 0 else (rB, dB)
            fn(cur, dcur, nxt, dnxt)
            cur, dcur = nxt, dnxt
        SC.barrier()
        build.stats = (SC.nops, SC.nwaits)
    return nc


def rope_table(S):
    t = np.arange(S)
    row = (t // 64).astype(np.float32)
    col = (t % 64).astype(np.float32)
    inv_freq = (10000.0 ** (-np.arange(16, dtype=np.float32) / 16)).astype(np.float32)
    ang = np.stack([row[:, None] * inv_freq, col[:, None] * inv_freq], axis=1).astype(np.float32)
    c, s = np.cos(ang).astype(np.float32), np.sin(ang).astype(np.float32)
    cos64 = np.concatenate([c[:, 0], c[:, 0], c[:, 1], c[:, 1]], axis=1)
    return np.ascontiguousarray(np.concatenate([cos64, -s.reshape(S, 32), s.reshape(S, 32)], axis=1).astype(np.float32))


def invc_table(S):
    tab = np.zeros((4, 16), np.float32)
    toks = np.concatenate([np.arange(8), np.arange(S - 8, S)])
    for g, w in enumerate((2, 4, 8, 16)):
        lo = np.clip(toks - w // 2, 0, S)
        hi = np.clip(toks + w - w // 2, 0, S)
        tab[g] = 1.0 / (hi - lo).astype(np.float32)
    return tab.reshape(1, 64)


_NC_CACHE = {}


def kernel(**inputs):
    S = inputs["x"].shape[1]
    B = inputs["x"].shape[0]
    if S not in _NC_CACHE:
        _NC_CACHE[S] = build(S)
    nc = _NC_CACHE[S]
    shared = {k: np.ascontiguousarray(np.asarray(v, dtype=np.float32)) for k, v in inputs.items() if k not in ("x", "mem")}
    shared["final_norm"] = shared["final_norm"].reshape(1, D)
    shared["rope"] = rope_table(S)
    shared["invc"] = invc_table(S)
    in_maps = []
    for b in range(B):
        m = dict(shared)
        m["x"] = np.ascontiguousarray(np.asarray(inputs["x"][b], dtype=np.float32))
        m["mem"] = np.ascontiguousarray(np.asarray(inputs["mem"][b], dtype=np.float32))
        in_maps.append(m)
    res = run_bass_kernel_spmd(nc, in_maps, core_ids=list(range(B)))
    return np.stack([np.asarray(r["out"], dtype=np.float32) for r in res.results], axis=0)
```

```python
import numpy as np
from contextlib import ExitStack
import concourse.bass as bass
import concourse.mybir as mybir
from concourse.bass_utils import run_bass_kernel_spmd

F32 = mybir.dt.float32
BF16 = mybir.dt.bfloat16
AF = mybir.ActivationFunctionType
ALU = mybir.AluOpType
AX = mybir.AxisListType

D = 1024
DFF = 2816
NCH = 44
EPS = 1e-6
MEM = 256
import os
PRECAST = os.environ.get('K_PRECAST', '1') == '1'
NSPLIT = int(os.environ.get('K_NSPLIT', '1'))
SKIP_OWN = os.environ.get('K_SKIP_OWN', '1') == '1'
WO_OFF = int(os.environ.get('K_WO_OFF', '4'))
WO_STRIDE = int(os.environ.get('K_WO_STRIDE', '0'))


class Buf:
    __slots__ = ("name", "w", "r")

    def __init__(self, name):
        self.name = name
        self.w = {}
        self.r = {}


class T:
    __slots__ = ("t", "b")

    def __init__(self, t, name):
        self.t = t
        self.b = Buf(name)

    def __getitem__(self, k):
        return self.t[k]


class Sched:
    def __init__(self, nc, stack):
        self.nc = nc
        self.stack = stack
        self.engs = {"pe": nc.tensor, "act": nc.scalar, "dve": nc.vector, "pool": nc.gpsimd, "sp": nc.sync}
        self.sems = {}
        self.cnt = {}
        self.seen = {}
        self.nops = 0
        self.nwaits = 0
        for k in ("pe", "act", "dve", "pool"):
            self._sem(k)

    def _sem(self, name):
        if name not in self.sems:
            self.sems[name] = self.stack.enter_context(self.nc.semaphore("s_" + name))
            self.cnt[name] = 0
        return self.sems[name]

    def _deps(self, eng, reads, writes):
        deps = {}
        for t in reads:
            for s, v in t.b.w.items():
                if deps.get(s, 0) < v:
                    deps[s] = v
        for t in writes:
            for d in (t.b.w, t.b.r):
                for s, v in d.items():
                    if s == eng and SKIP_OWN and eng == "pe":
                        continue
                    if deps.get(s, 0) < v:
                        deps[s] = v
        return deps

    def _wait(self, eng, deps):
        e = self.engs[eng]
        for s, v in deps.items():
            if s not in ("pe", "act", "dve", "pool"):
                v = self.cnt[s]
            if self.seen.get((eng, s), 0) < v:
                e.wait_ge(self.sems[s], v)
                self.seen[(eng, s)] = v
                self.nwaits += 1

    def op(self, eng, fn, reads=(), writes=()):
        self._wait(eng, self._deps(eng, reads, writes))
        ins = fn(self.engs[eng])
        if isinstance(ins, (list, tuple)):
            ins = ins[-1]
        self.cnt[eng] += 1
        v = self.cnt[eng]
        ins.then_inc(self.sems[eng], 1)
        for t in reads:
            t.b.r[eng] = v
        for t in writes:
            t.b.w[eng] = v
        self.nops += 1

    def dma(self, queue, out, in_, reads=(), writes=(), sem=None, **kw):
        sem = queue + "_" + sem
        self._sem(sem)
        self._wait(queue, self._deps("__dma__", reads, writes))
        ins = self.engs[queue].dma_start(out=out, in_=in_, **kw)
        self.cnt[sem] += 16
        v = self.cnt[sem]
        ins.then_inc(self.sems[sem], 16)
        for t in reads:
            t.b.r[sem] = v
        for t in writes:
            t.b.w[sem] = v
        self.nops += 1

    def barrier(self):
        deps = {s: v for s, v in self.cnt.items() if v > 0}
        for eng in ("sp", "pe", "act", "dve", "pool"):
            self._wait(eng, deps)


def bc_last(ap, n):
    return bass.AP(ap.tensor, ap.offset, [list(ap.ap[0]), list(ap.ap[1]), [0, n]])


def bc_mid(ap, n):
    return bass.AP(ap.tensor, ap.offset, [list(ap.ap[0]), [0, n]] + [list(a) for a in ap.ap[1:]])


def bc_part(ap, n=128):
    return bass.AP(ap.tensor, ap.offset, [[0, n]] + [list(a) for a in ap.ap[1:]])


def build(S=4096, nlayers_dbg=None, stop_after=None):
    NT = S // 128
    NB = S // 512
    nc = bass.Bass("TRN2", target_bir_lowering=False)

    def din(name, shape):
        return nc.dram_tensor(name, list(shape), F32, kind="ExternalInput").ap()

    x_in = din("x", [S, D])
    mem_in = din("mem", [MEM, D])
    attn_norm = din("attn_norm", [1, D])
    attn_w_qkv = din("attn_w_qkv", [1, D, 1536])
    attn_q_gain = din("attn_q_gain", [1, 64])
    attn_k_gain = din("attn_k_gain", [1, 64])
    attn_w_o = din("attn_w_o", [1, D, D])
    pool_norm = din("pool_norm", [1, D])
    pool_w = din("pool_w", [1, 4, 256, 256])
    pool_scale = din("pool_scale", [1, D])
    xattn_norm = din("xattn_norm", [2, D])
    mem_norm = din("mem_norm", [2, D])
    xattn_w_q = din("xattn_w_q", [2, D, D])
    xattn_w_kv = din("xattn_w_kv", [2, D, 2 * D])
    xattn_w_o = din("xattn_w_o", [2, D, D])
    ffn_norm = din("ffn_norm", [2, D])
    ffn_w_up = din("ffn_w_up", [2, D, 2 * DFF])
    ffn_conv_w = din("ffn_conv_w", [2, 3, 2 * DFF])
    ffn_conv_b = din("ffn_conv_b", [2, 2 * DFF])
    ffn_w_down = din("ffn_w_down", [2, DFF, D])
    final_norm = din("final_norm", [1, D])
    rope_in = din("rope", [S, 128])
    invc_in = din("invc", [1, 64])
    out = nc.dram_tensor("out", [S, D], F32, kind="ExternalOutput").ap()
    rA = nc.dram_tensor("resA", [S, D], F32, kind="Internal").ap()
    rB = nc.dram_tensor("resB", [S, D], F32, kind="Internal").ap()
    def dscr(name, shape):
        if not PRECAST:
            return None
        return nc.dram_tensor(name, list(shape), BF16, kind="Internal").ap()

    wq_b = dscr("wq_b", [2, D, D]); wkv_b = dscr("wkv_b", [2, D, 2 * D]); wo_b = dscr("wo_b", [2, D, D])
    wup_b = dscr("wup_b", [2, D, 2 * DFF]); wd_b = dscr("wd_b", [2, DFF, D]); wp_b = dscr("wp_b", [4, 256, 256])
    d_wq = [T(None, f"d_wq{l}") for l in range(2)]; d_wkv = [T(None, f"d_wkv{l}") for l in range(2)]
    d_wo = [T(None, f"d_wo{l}") for l in range(2)]; d_wup = [T(None, f"d_wup{l}") for l in range(2)]
    d_wd = [T(None, f"d_wd{l}") for l in range(2)]; d_wp = T(None, "d_wp")
    dA = T(None, "resA")
    dB = T(None, "resB")
    dX = T(None, "x")
    dO = T(None, "out")

    with ExitStack() as st:
        SC = Sched(nc, st)
        op = SC.op
        dma = SC.dma

        uid = [0]

        def sbuf(stack, name, shape, dt=F32):
            uid[0] += 1
            name = f"{name}_{uid[0]}"
            return T(stack.enter_context(nc.sbuf_tensor(name, list(shape), dt)), name)

        def psum(stack, name, shape, dt=F32):
            return T(stack.enter_context(nc.psum_tensor(name, list(shape), dt)), name)

        ident = sbuf(st, "ident", [128, 128], BF16)
        identf = sbuf(st, "identf", [128, 128], F32)
        ones_f = sbuf(st, "ones_f", [128, 128], F32)
        ones_b = sbuf(st, "ones_b", [128, 128], BF16)
        for idt in (ident, identf):
            op("pool", lambda e, idt=idt: e.memset(idt[:], 0.0), writes=[idt])
            op("pool", lambda e, idt=idt: e.affine_select(out=idt[:], in_=idt[:], pattern=[[-1, 128]],
                                                         compare_op=ALU.not_equal, fill=1.0, base=0,
                                                         channel_multiplier=1), reads=[idt], writes=[idt])
        op("dve", lambda e: e.memset(ones_f[:], 1.0), writes=[ones_f])
        op("dve", lambda e: e.memset(ones_b[:], 1.0), writes=[ones_b])

        def alloc_psum(stack, nf32):
            uid[0] += 1
            pb = [psum(stack, f"pb{i}_{uid[0]}", [128, 512], F32) for i in range(nf32)]
            pt = [psum(stack, f"pt{i}_{uid[0]}", [128, 1024], BF16) for i in range(8 - nf32)]
            return pb, pt

        def load_weight_bf16(w_t, src2d, nchunk, ncols, sem, col_split=1, p=128):
            dst3 = w_t[:].rearrange("p (c n) -> p c n", c=nchunk)
            src3 = src2d.rearrange("(c p) n -> p c n", p=p)
            step = ncols // col_split
            for i in range(col_split):
                dma("pool", dst3[:, :, i * step:(i + 1) * step], src3[:, :, i * step:(i + 1) * step],
                    writes=[w_t], sem=sem)

        def load_weight_sb(w_t, src2d, nchunk, ncols, sem, dsrc_t, col_lo=0, col_hi=None, deps_t=None, p=128, src32=None):
            col_hi = ncols if col_hi is None else col_hi
            dst3 = w_t[:].rearrange("p (c n) -> p c n", c=nchunk)
            if PRECAST:
                src3 = src2d.rearrange("(c p) n -> p c n", p=p)
                dma("sp", dst3[:, :, col_lo:col_hi], src3[:, :, col_lo:col_hi], reads=[dsrc_t],
                    writes=[deps_t if deps_t is not None else w_t], sem=sem)
            else:
                src3 = src32.rearrange("(c p) n -> p c n", p=p)
                step = 1024
                for lo in range(col_lo, col_hi, step):
                    hi = min(col_hi, lo + step)
                    dma("pool", dst3[:, :, lo:hi], src3[:, :, lo:hi], writes=[deps_t if deps_t is not None else w_t], sem=sem)

        def precast_all():
            if not PRECAST:
                return
            def cast2d(dst2d, src2d, t, ncols, split):
                step = ncols // split
                for i in range(split):
                    dma("pool", dst2d[:, i * step:(i + 1) * step], src2d[:, i * step:(i + 1) * step], writes=[t], sem="pc")
                SC._wait("pool", {"pool_pc": SC.cnt["pool_pc"]})
            for l in range(2):
                cast2d(wkv_b[l], xattn_w_kv[l], d_wkv[l], 2 * D, 2)
                cast2d(wq_b[l], xattn_w_q[l], d_wq[l], D, 1)
                cast2d(wo_b[l], xattn_w_o[l], d_wo[l], D, 1)
                cast2d(wup_b[l], ffn_w_up[l], d_wup[l], 2 * DFF, 4)
                cast2d(wd_b[l], ffn_w_down[l], d_wd[l], D, 1)
                if l == 0:
                    cast2d(wp_b.rearrange("g r n -> (g r) n"), pool_w[0].rearrange("g r n -> (g r) n"), d_wp, 256, 1)

        def rms_stats_st(xt_ap, xt, npart, sq, ss, lnv, rstd, width=D, out_bias=0.0):
            def a():
                op("dve", lambda e: e.tensor_tensor(out=sq[0:npart, 0:width], in0=xt_ap, in1=xt_ap, op=ALU.mult),
                   reads=[xt], writes=[sq])
                op("dve", lambda e: e.tensor_reduce(out=ss[0:npart, 0:1], in_=sq[0:npart, 0:width], axis=AX.X, op=ALU.add),
                   reads=[sq], writes=[ss])

            def b():
                op("act", lambda e: e.activation(out=lnv[0:npart, 0:1], in_=ss[0:npart, 0:1], func=AF.Ln,
                                                 scale=1.0 / width, bias=EPS), reads=[ss], writes=[lnv])
                op("act", lambda e: e.activation(out=rstd[0:npart, 0:1], in_=lnv[0:npart, 0:1], func=AF.Exp,
                                                 scale=-0.5, bias=out_bias), reads=[lnv], writes=[rstd])
            return [a, b]

        def rms_stats(*a, **k):
            for f in rms_stats_st(*a, **k):
                f()

        def norm_st(xt, gain, hb, sq, ss, lnv, rstd, npart=128):
            def c():
                op("dve", lambda e: e.scalar_tensor_tensor(out=hb[0:npart, :], in0=xt[0:npart, :], scalar=rstd[0:npart, 0:1],
                                                          in1=gain[0:npart, :], op0=ALU.mult, op1=ALU.mult),
                   reads=[xt, rstd, gain], writes=[hb])
            return rms_stats_st(xt[0:npart, :], xt, npart, sq, ss, lnv, rstd) + [c]

        def norm_to_bf16(*a, **k):
            for f in norm_st(*a, **k):
                f()

        def transpose_h(hb, pt, npart=128):
            op("pe", lambda e: [e.transpose(out=pt[:, c * npart:(c + 1) * npart], in_=hb[0:npart, c * 128:(c + 1) * 128],
                                            identity=ident[0:npart, 0:npart]) for c in range(8)],
               reads=[hb, ident], writes=[pt])

        def qk_norm_rope_st(src, H, gain, rope, dst, tmp, ss, lnv, rstd, out_bias):
            W = H * 64
            sq, ta, tb = tmp
            s3 = src[:, 0:W].rearrange("p (h d) -> p h d", d=64)
            a3 = ta[:, 0:W].rearrange("p (h d) -> p h d", d=64)
            b3 = tb[:, 0:W].rearrange("p (h d) -> p h d", d=64)
            a5 = ta[:, 0:W].rearrange("p (h a f q) -> p h a f q", a=2, f=2, q=16)
            s5 = sq[:, 0:W].rearrange("p (h a f q) -> p h a f q", a=2, f=2, q=16)
            nsin = bc_mid(rope[:, 64:96].rearrange("p (a q) -> p a q", a=2), H)
            psin = bc_mid(rope[:, 96:128].rearrange("p (a q) -> p a q", a=2), H)

            def a():
                op("dve", lambda e: e.tensor_tensor(out=sq[:, 0:W], in0=src[:, 0:W], in1=src[:, 0:W], op=ALU.mult),
                   reads=[src], writes=[sq])
                op("dve", lambda e: e.tensor_reduce(out=ss[:, 0:H], in_=sq[:, 0:W].rearrange("p (h d) -> p h d", d=64),
                                                    axis=AX.X, op=ALU.add), reads=[sq], writes=[ss])

            def b():
                op("act", lambda e: e.activation(out=lnv[:, 0:H], in_=ss[:, 0:H], func=AF.Ln, scale=1.0 / 64, bias=EPS),
                   reads=[ss], writes=[lnv])
                op("act", lambda e: e.activation(out=rstd[:, 0:H], in_=lnv[:, 0:H], func=AF.Exp, scale=-0.5, bias=out_bias),
                   reads=[lnv], writes=[rstd])

            def c():
                op("dve", lambda e: e.tensor_tensor(out=a3, in0=s3, in1=bc_last(rstd[:, 0:H], 64), op=ALU.mult),
                   reads=[src, rstd], writes=[ta])
                op("dve", lambda e: e.tensor_tensor(out=a3, in0=a3, in1=bc_mid(gain[:, 0:64], H), op=ALU.mult),
                   reads=[ta, gain], writes=[ta])
                op("dve", lambda e: e.tensor_tensor(out=b3, in0=a3, in1=bc_mid(rope[:, 0:64], H), op=ALU.mult),
                   reads=[ta, rope], writes=[tb])
                op("dve", lambda e: e.tensor_tensor(out=s5[:, :, :, 0, :], in0=a5[:, :, :, 1, :], in1=nsin, op=ALU.mult),
                   reads=[ta, rope], writes=[sq])
                op("dve", lambda e: e.tensor_tensor(out=s5[:, :, :, 1, :], in0=a5[:, :, :, 0, :], in1=psin, op=ALU.mult),
                   reads=[ta, rope], writes=[sq])
                op("dve", lambda e: e.tensor_tensor(out=dst[:, 0:W], in0=tb[:, 0:W], in1=sq[:, 0:W], op=ALU.add),
                   reads=[tb, sq], writes=[dst])
            return [a, b, c]

        def phase_attention(src, dsrc, dst, ddst):
            with ExitStack() as ph:
                PB, PT1 = alloc_psum(ph, 7)
                PT = [PT1[0], PT1[0]]
                wqkv = sbuf(ph, "wqkv", [128, 8 * 1536], BF16)
                wo = sbuf(ph, "wo_a", [64, 16 * D], BF16)
                KT = sbuf(ph, "KT", [128, 4 * S], BF16)
                Vs = sbuf(ph, "Vs", [128, NT * 4 * 65], BF16)
                gain = sbuf(ph, "gain_a", [128, D])
                qg = sbuf(ph, "qg", [128, 64])
                kg = sbuf(ph, "kg", [128, 64])
                xs = [sbuf(ph, f"xs{i}", [128, D]) for i in range(3)]
                rp = [sbuf(ph, f"rp{i}", [128, 128]) for i in range(3)]
                hb = [sbuf(ph, f"hb{i}", [128, D], BF16) for i in range(2)]
                hT = [sbuf(ph, f"hT{i}", [128, 8 * 128], BF16) for i in range(2)]
                sq = sbuf(ph, "sq", [128, D])
                ta = sbuf(ph, "ta", [128, D])
                tb = sbuf(ph, "tb", [128, D])
                qs = sbuf(ph, "qs", [128, D])
                qr = sbuf(ph, "qr", [128, D], BF16)
                QT = [sbuf(ph, f"QT{i}", [128, 16 * 128], BF16) for i in range(2)]
                Pb = [sbuf(ph, f"Pb{i}", [128, 512], BF16) for i in range(3)]
                OT = [sbuf(ph, f"OT{i}", [64, 16 * 128], BF16) for i in range(2)]
                rden = sbuf(ph, "rden", [128, 512])
                bcs = sbuf(ph, "bcs", [64, 512])
                xn = [sbuf(ph, f"xn{i}", [128, D]) for i in range(2)]
                ss = sbuf(ph, "ss", [128, 16]); lnv = sbuf(ph, "lnv", [128, 16]); rstd = sbuf(ph, "rstd", [128, 16])
                ss1 = sbuf(ph, "ss1", [128, 1]); lnv1 = sbuf(ph, "lnv1", [128, 1]); rstd1 = sbuf(ph, "rstd1", [128, 1])

                load_weight_bf16(wqkv, attn_w_qkv[0], 8, 1536, "w0")
                dma("pool", wo[:].rearrange("d (h n) -> d h n", h=16), attn_w_o[0].rearrange("(h d) n -> d h n", d=64),
                    writes=[wo], sem="w1")
                dma("sp", gain[:], bc_part(attn_norm), writes=[gain], sem="c0")
                dma("sp", qg[:], bc_part(attn_q_gain), writes=[qg], sem="c0")
                dma("sp", kg[:], bc_part(attn_k_gain), writes=[kg], sem="c0")
                op("pool", lambda e: e.memset(KT[64:128, :], 0.0), writes=[KT])
                for q in QT:
                    op("pool", lambda e, q=q: e.memset(q[64:128, :], 0.0), writes=[q])
                op("dve", lambda e: e.memset(Vs[:].rearrange("p (t e) -> p t e", e=65)[:, :, 64:65], 1.0), writes=[Vs])
                precast_all()

                def load_tile(t, i):
                    dma("sp", xs[i][:], src[t * 128:(t + 1) * 128, :], reads=[dsrc], writes=[xs[i]], sem=f"lx{i}")
                    dma("sp", rp[i][:], rope_in[t * 128:(t + 1) * 128, :], writes=[rp[i]], sem=f"lr{i}")

                sqn = sbuf(ph, "sqn", [128, D])

                def kv_stA(t):
                    i, j = t % 3, t % 2
                    sts = [lambda: load_tile(t + 1, (t + 1) % 3) if t + 1 < NT else None]
                    sts += norm_st(xs[i], gain, hb[j], sqn, ss1, lnv1, rstd1)
                    sts.append(lambda: transpose_h(hb[j], PT[0]))
                    sts.append(lambda: op("dve", lambda e: e.tensor_copy(out=hT[j][:], in_=PT[0][:]), reads=[PT[0]], writes=[hT[j]]))
                    return sts

                def kv_stB(t):
                    i, j = t % 3, t % 2
                    pk = PB[j]
                    pt = PB7b
                    sts = [lambda: op("pe", lambda e: [e.matmul(pk[:], lhsT=hT[j][:, c * 128:(c + 1) * 128],
                                                                rhs=wqkv[:, c * 1536 + 1024:c * 1536 + 1536],
                                                                start=(c == 0), stop=(c == 7)) for c in range(8)],
                                      reads=[hT[j], wqkv], writes=[pk]),
                           lambda: op("act", lambda e: e.activation(out=qs[:, 0:512], in_=pk[:], func=AF.Copy),
                                      reads=[pk], writes=[qs])]
                    sts += qk_norm_rope_st(qs, 4, kg, rp[i], qr, (sq, ta, tb), ss, lnv, rstd, 0.0)
                    sts.append(lambda: op("act", lambda e: e.activation(
                        out=Vs[:, t * 260:(t + 1) * 260].rearrange("p (g e) -> p g e", e=65)[:, :, 0:64],
                        in_=qs[:, 256:512].rearrange("p (g d) -> p g d", d=64), func=AF.Copy), reads=[qs], writes=[Vs]))
                    sts.append(lambda: op("pe", lambda e: [e.transpose(out=pt[0:64, g * 128:(g + 1) * 128],
                                                                       in_=qr[:, g * 64:(g + 1) * 64], identity=ident[:])
                                                           for g in range(4)], reads=[qr, ident], writes=[pt]))
                    sts.append(lambda: op("dve", lambda e: e.tensor_copy(
                        out=KT[0:64, :].rearrange("p (g s) -> p g s", g=4)[:, :, t * 128:(t + 1) * 128],
                        in_=pt[0:64, 0:512].rearrange("p (g s) -> p g s", g=4)), reads=[pt], writes=[KT]))
                    return sts

                PB7b = PT[0]
                load_tile(0, 0)
                for f in kv_stA(0):
                    f()
                for t in range(NT):
                    sa = kv_stA(t + 1) if t + 1 < NT else []
                    sb_ = kv_stB(t)
                    for k in range(max(len(sa), len(sb_))):
                        if k < len(sb_):
                            sb_[k]()
                        if k < len(sa):
                            sa[k]()

                def prep_stages(qb):
                    i, j = qb % 3, qb % 2
                    sts = [lambda: load_tile(qb, i)]
                    sts += norm_st(xs[i], gain, hb[j], sqn, ss1, lnv1, rstd1)
                    sts.append(lambda: transpose_h(hb[j], PT[0]))
                    sts.append(lambda: op("dve", lambda e: e.tensor_copy(out=hT[j][:], in_=PT[0][:]), reads=[PT[0]], writes=[hT[j]]))

                    def qproj(half):
                        pk = PB[half]
                        op("pe", lambda e: [e.matmul(pk[:], lhsT=hT[j][:, c * 128:(c + 1) * 128],
                                                     rhs=wqkv[:, c * 1536 + half * 512:c * 1536 + half * 512 + 512],
                                                     start=(c == 0), stop=(c == 7)) for c in range(8)],
                           reads=[hT[j], wqkv], writes=[pk])

                    def qcopy(half):
                        pk = PB[half]
                        op("act", lambda e: e.activation(out=qs[:, half * 512:(half + 1) * 512], in_=pk[:], func=AF.Copy),
                           reads=[pk], writes=[qs])
                    def qproj4(half, part):
                        pk = PB[half]
                        op("pe", lambda e: [e.matmul(pk[:], lhsT=hT[j][:, c * 128:(c + 1) * 128],
                                                     rhs=wqkv[:, c * 1536 + half * 512:c * 1536 + half * 512 + 512],
                                                     start=(c == 0), stop=(c == 7)) for c in range(part * 4, part * 4 + 4)],
                           reads=[hT[j], wqkv], writes=[pk])
                    sts.append(lambda: (qproj(0), qproj(1)))
                    sts.append(lambda: (qcopy(0), qcopy(1)))
                    sts += qk_norm_rope_st(qs, 16, qg, rp[i], qr, (sq, ta, tb), ss, lnv, rstd, float(np.log(0.125)))

                    def qtr(half):
                        pt = PT[0]
                        op("pe", lambda e: [e.transpose(out=pt[0:64, h * 128:(h + 1) * 128],
                                                        in_=qr[:, (half * 8 + h) * 64:(half * 8 + h + 1) * 64],
                                                        identity=ident[:]) for h in range(8)],
                           reads=[qr, ident], writes=[pt])

                    def qev(half):
                        pt = PT[0]
                        op("dve", lambda e: e.tensor_copy(out=QT[j][0:64, half * 1024:(half + 1) * 1024], in_=pt[0:64, :]),
                           reads=[pt], writes=[QT[j]])
                    sts.append(lambda: qtr(0))
                    sts.append(lambda: qev(0))
                    sts.append(lambda: qtr(1))
                    sts.append(lambda: qev(1))
                    return sts

                for f in prep_stages(0):
                    f()
                SB_ = [PB[2], PB[3], PB[4]]
                OB_ = [PB[5], PB[6]]
                steps = [(qb, g, kt) for qb in range(NT) for g in range(4) for kt in range(NT)]
                nsteps = len(steps)
                DEPTH = 3
                deferred = {}

                def defer(i, fn):
                    deferred.setdefault(i, []).append(fn)

                def s_mm(i):
                    qb, g, kt = steps[i]
                    sbk = SB_[i % DEPTH]
                    jq = qb % 2
                    op("pe", lambda e: e.matmul(sbk[:], lhsT=KT[:, g * S + kt * 128:g * S + (kt + 1) * 128],
                                                rhs=QT[jq][:, g * 512:(g + 1) * 512], start=True, stop=True),
                       reads=[KT, QT[jq]], writes=[sbk])

                def normalize(qb, g, ob):
                    jq = qb % 2
                    bcb = PB[g % 2]
                    op("pe", lambda e: e.matmul(bcb[0:64, :], lhsT=ones_f[64:65, 0:64], rhs=rden[64:65, :],
                                                start=True, stop=True), reads=[ones_f, rden], writes=[bcb])
                    op("dve", lambda e: e.tensor_copy(out=bcs[:], in_=bcb[0:64, :]), reads=[bcb], writes=[bcs])
                    op("dve", lambda e: e.tensor_tensor(out=OT[jq][:, g * 512:(g + 1) * 512], in0=ob[0:64, :], in1=bcs[:],
                                                        op=ALU.mult), reads=[ob, bcs], writes=[OT[jq]])

                def wo_stages(qb):
                    jq = qb % 2
                    xi = xs[qb % 3]
                    xo = xn[jq]
                    sts = []

                    def mm16(half):
                        pk = PB[half]
                        op("pe", lambda e: [e.matmul(pk[:], lhsT=OT[jq][:, h * 128:(h + 1) * 128],
                                                     rhs=wo[:, h * D + half * 512:h * D + half * 512 + 512],
                                                     start=(h == 0), stop=(h == 15)) for h in range(16)],
                           reads=[OT[jq], wo], writes=[pk])

                    def resid(half):
                        pk = PB[half]
                        op("dve", lambda e: e.tensor_tensor(out=xo[:, half * 512:(half + 1) * 512], in0=pk[:],
                                                            in1=xi[:, half * 512:(half + 1) * 512], op=ALU.add),
                           reads=[pk, xi], writes=[xo])
                    for half in range(2):
                        sts.append(lambda half=half: mm16(half))
                        sts.append(lambda half=half: resid(half))
                    sts.append(lambda: dma("sp", dst[qb * 128:(qb + 1) * 128, :], xo[:], reads=[xo], writes=[ddst], sem=f"so{jq}"))
                    return sts

                for i in range(min(DEPTH, nsteps)):
                    s_mm(i)
                for i, (qb, g, kt) in enumerate(steps):
                    if qb + 1 < NT:
                        loc = g * NT + kt
                        if loc == 0:
                            cur_prep = prep_stages(qb + 1)
                            spacing = max(1, (4 * NT - 8) // len(cur_prep))
                        if loc % spacing == 0 and loc // spacing < len(cur_prep):
                            cur_prep[loc // spacing]()
                    ob = OB_[g % 2]
                    sbk = SB_[i % DEPTH]
                    pb = Pb[i % 3]
                    op("act", lambda e: e.activation(out=pb[:], in_=sbk[:], func=AF.Exp), reads=[sbk], writes=[pb])
                    op("pe", lambda e: e.matmul(ob[0:65, :], lhsT=Vs[:, (kt * 4 + g) * 65:(kt * 4 + g + 1) * 65],
                                                rhs=pb[:], start=(kt == 0), stop=(kt == NT - 1)),
                       reads=[Vs, pb], writes=[ob])
                    if i + DEPTH < nsteps:
                        s_mm(i + DEPTH)
                    for fn in deferred.pop(i, []):
                        fn()
                    if kt == NT - 1:
                        op("dve", lambda e: e.reciprocal(out=rden[64:65, :], in_=ob[64:65, :]), reads=[ob], writes=[rden])
                        dd_ = max(1, min(8, NT // 2))
                        if i + dd_ + 16 >= nsteps:
                            normalize(qb, g, ob)
                            if g == 3:
                                for f in wo_stages(qb):
                                    f()
                        else:
                            defer(i + dd_, lambda qb=qb, g=g, ob=ob: normalize(qb, g, ob))
                            if g == 3:
                                for k_, f in enumerate(wo_stages(qb)):
                                    defer(i + dd_ + WO_OFF + k_ * WO_STRIDE, f)
                SC.barrier()

        def phase_xattn(l, src, dsrc, dst, ddst):
            with ExitStack() as ph:
                PB, PT = alloc_psum(ph, 6)
                wq = sbuf(ph, "wq", [128, 8 * D], BF16)
                wkv = sbuf(ph, "wkv", [128, 8 * 2 * D], BF16)
                wo = sbuf(ph, "wo_x", [128, 8 * D], BF16)
                KmT = sbuf(ph, "KmT", [128, 8 * MEM], BF16)
                Vm = sbuf(ph, "Vm", [128, 2 * D], BF16)
                gain = sbuf(ph, "gain_x", [128, D])
                mgain = sbuf(ph, "mgain", [128, D])
                xs = [sbuf(ph, f"xs{i}", [128, D]) for i in range(8)]
                hb = [sbuf(ph, f"hb{i}", [128, D], BF16) for i in range(2)]
                hT = sbuf(ph, "hT", [128, 8 * 512], BF16)
                memT = sbuf(ph, "memT", [128, 8 * MEM], BF16)
                qT = sbuf(ph, "qT", [128, 8 * 512], BF16)
                Pb = [sbuf(ph, f"Pb{i}", [128, 512], BF16) for i in range(4)]
                OT = sbuf(ph, "OT", [128, 8 * 512], BF16)
                rden = [sbuf(ph, f"rden{i}", [128, 512]) for i in range(2)]
                xn = [sbuf(ph, f"xn{i}", [128, D]) for i in range(2)]
                sq = sbuf(ph, "sq", [128, D])
                ss1 = sbuf(ph, "ss1", [128, 1]); lnv1 = sbuf(ph, "lnv1", [128, 1]); rstd1 = sbuf(ph, "rstd1", [128, 1])

                load_weight_sb(wkv, wkv_b[l] if PRECAST else None, 8, 2 * D, "w0", d_wkv[l], src32=xattn_w_kv[l])
                load_weight_sb(wq, wq_b[l] if PRECAST else None, 8, D, "w1", d_wq[l], src32=xattn_w_q[l])
                load_weight_sb(wo, wo_b[l] if PRECAST else None, 8, D, "w2", d_wo[l], src32=xattn_w_o[l])
                dma("sp", gain[:], bc_part(xattn_norm[l:l + 1, :]), writes=[gain], sem="c0")
                dma("sp", mgain[:], bc_part(mem_norm[l:l + 1, :]), writes=[mgain], sem="c0")

                for mt in range(2):
                    dma("sp", xs[mt][:], mem_in[mt * 128:(mt + 1) * 128, :], writes=[xs[mt]], sem=f"lx{mt}")
                for mt in range(2):
                    norm_to_bf16(xs[mt], mgain, hb[mt], sq, ss1, lnv1, rstd1)
                    transpose_h(hb[mt], PT[mt])
                    op("dve", lambda e, mt=mt: e.tensor_copy(
                        out=memT[:].rearrange("p (c m) -> p c m", c=8)[:, :, mt * 128:(mt + 1) * 128],
                        in_=PT[mt][:].rearrange("p (c m) -> p c m", c=8)), reads=[PT[mt]], writes=[memT])
                for dc in range(8):
                    pk = PB[dc % 2]
                    op("pe", lambda e, pk=pk, dc=dc: [e.matmul(pk[:, 0:MEM], lhsT=wkv[:, c * 2048 + dc * 128:c * 2048 + (dc + 1) * 128],
                                                               rhs=memT[:, c * MEM:(c + 1) * MEM], start=(c == 0), stop=(c == 7))
                                                      for c in range(8)], reads=[wkv, memT], writes=[pk])
                    op("dve", lambda e, pk=pk, dc=dc: e.tensor_copy(out=KmT[:, dc * MEM:(dc + 1) * MEM], in_=pk[:, 0:MEM]),
                       reads=[pk], writes=[KmT])
                for mt in range(2):
                    for half in range(2):
                        pk = PB[2 + (mt * 2 + half) % 2]
                        op("pe", lambda e, pk=pk, mt=mt, half=half: [
                            e.matmul(pk[:], lhsT=memT[:, c * MEM + mt * 128:c * MEM + (mt + 1) * 128],
                                     rhs=wkv[:, c * 2048 + 1024 + half * 512:c * 2048 + 1024 + (half + 1) * 512],
                                     start=(c == 0), stop=(c == 7)) for c in range(8)], reads=[wkv, memT], writes=[pk])
                        op("act", lambda e, pk=pk, mt=mt, half=half: e.activation(
                            out=Vm[:, mt * D + half * 512:mt * D + (half + 1) * 512], in_=pk[:], func=AF.Copy),
                           reads=[pk], writes=[Vm])

                def load_blk(blk):
                    for jj in range(4):
                        t = blk * 4 + jj
                        i = (blk % 2) * 4 + jj
                        dma("sp", xs[i][:], src[t * 128:(t + 1) * 128, :], reads=[dsrc], writes=[xs[i]], sem=f"lx{i}")

                def norm_tile(blk, jj):
                    i = (blk % 2) * 4 + jj
                    norm_to_bf16(xs[i], gain, hb[jj % 2], sq, ss1, lnv1, rstd1)

                def tr_evac(jj):
                    k = jj % 2
                    transpose_h(hb[k], PT[k])
                    op("dve", lambda e: e.tensor_copy(
                        out=hT[:].rearrange("p (c m) -> p c m", c=8)[:, :, jj * 128:(jj + 1) * 128],
                        in_=PT[k][:].rearrange("p (c m) -> p c m", c=8)), reads=[PT[k]], writes=[hT])

                load_blk(0)
                for jj in range(4):
                    norm_tile(0, jj)
                    tr_evac(jj)
                for blk in range(NB):
                    if blk + 1 < NB:
                        load_blk(blk + 1)
                    for dc in range(8):
                        pk = PB[dc % 2]
                        op("pe", lambda e, pk=pk, dc=dc: [e.matmul(pk[:], lhsT=wq[:, c * D + dc * 128:c * D + (dc + 1) * 128],
                                                                   rhs=hT[:, c * 512:(c + 1) * 512], start=(c == 0), stop=(c == 7))
                                                          for c in range(8)], reads=[wq, hT], writes=[pk])
                        op("act", lambda e, pk=pk, dc=dc: e.activation(out=qT[:, dc * 512:(dc + 1) * 512], in_=pk[:],
                                                                       func=AF.Copy, scale=1.0 / 16.0), reads=[pk], writes=[qT])
                    for h in range(4):
                        pbs = [Pb[(h % 2) * 2 + m] for m in range(2)]
                        for m in range(2):
                            sbk = PB[2 + m]
                            op("pe", lambda e, sbk=sbk, m=m: [
                                e.matmul(sbk[:], lhsT=KmT[:, (2 * h + jd) * MEM + m * 128:(2 * h + jd) * MEM + (m + 1) * 128],
                                         rhs=qT[:, (2 * h + jd) * 512:(2 * h + jd + 1) * 512], start=(jd == 0), stop=(jd == 1))
                                for jd in range(2)], reads=[KmT, qT], writes=[sbk])
                            op("act", lambda e, sbk=sbk, m=m: e.activation(out=pbs[m][:], in_=sbk[:], func=AF.Exp),
                               reads=[sbk], writes=[pbs[m]])
                        dbk = PB[4]
                        rd = rden[h % 2]
                        op("pe", lambda e: [e.matmul(dbk[:], lhsT=ones_b[:], rhs=pbs[m][:], start=(m == 0), stop=(m == 1))
                                            for m in range(2)], reads=[ones_b, pbs[0], pbs[1]], writes=[dbk])
                        op("act", lambda e: e.activation(out=rd[:], in_=dbk[:], func=AF.Ln), reads=[dbk], writes=[rd])
                        op("act", lambda e: e.activation(out=rd[:], in_=rd[:], func=AF.Exp, scale=-1.0), reads=[rd], writes=[rd])
                        for jd in range(2):
                            obk = PB[5] if jd else PB[0]
                            op("pe", lambda e, obk=obk, jd=jd: [
                                e.matmul(obk[:], lhsT=Vm[:, m * D + h * 256 + jd * 128:m * D + h * 256 + (jd + 1) * 128],
                                         rhs=pbs[m][:], start=(m == 0), stop=(m == 1)) for m in range(2)],
                               reads=[Vm, pbs[0], pbs[1]], writes=[obk])
                            op("dve", lambda e, obk=obk, jd=jd: e.tensor_tensor(
                                out=OT[:, (2 * h + jd) * 512:(2 * h + jd + 1) * 512], in0=obk[:], in1=rd[:], op=ALU.mult),
                               reads=[obk, rd], writes=[OT])
                    for jj in range(4):
                        t = blk * 4 + jj
                        xi = xs[(blk % 2) * 4 + jj]
                        xo = xn[jj % 2]
                        if blk + 1 < NB:
                            norm_tile(blk + 1, jj)
                        for half in range(2):
                            pk = PB[half + 1]
                            op("pe", lambda e, pk=pk, half=half: [
                                e.matmul(pk[:], lhsT=OT[:, dc * 512 + jj * 128:dc * 512 + (jj + 1) * 128],
                                         rhs=wo[:, dc * D + half * 512:dc * D + (half + 1) * 512], start=(dc == 0), stop=(dc == 7))
                                for dc in range(8)], reads=[OT, wo], writes=[pk])
                            op("dve", lambda e, pk=pk, half=half: e.tensor_tensor(
                                out=xo[:, half * 512:(half + 1) * 512], in0=pk[:], in1=xi[:, half * 512:(half + 1) * 512],
                                op=ALU.add), reads=[pk, xi], writes=[xo])
                        dma("sp", dst[t * 128:(t + 1) * 128, :], xo[:], reads=[xo], writes=[ddst], sem=f"so{jj % 2}")
                        if blk + 1 < NB:
                            tr_evac(jj)
                SC.barrier()

        def phase_ffn(l, src, dsrc, dst, ddst, final):
            with ExitStack() as ph:
                PB, PT = alloc_psum(ph, 6)
                wup = sbuf(ph, "wup", [128, 8 * 2 * DFF], BF16)
                wd = sbuf(ph, "wd", [128, 22 * D], BF16)
                gain = sbuf(ph, "gain_f", [128, D])
                fgain = sbuf(ph, "fgain", [128, D]) if final else None
                cp = sbuf(ph, "cp", [128, 4 * NCH])
                xs = [sbuf(ph, f"xs{i}", [128, D]) for i in range(2)]
                xr = [sbuf(ph, f"xr{i}", [128, D]) for i in range(2)]
                hb = sbuf(ph, "hb", [128, D], BF16)
                hT = sbuf(ph, "hT", [128, 8 * 512], BF16)
                hTh = sbuf(ph, "hTh", [128, 8 * 16], BF16)
                uh = sbuf(ph, "uh", [128, NCH * 16])
                G = sbuf(ph, "G", [128, 22 * 512], BF16)
                accg = [sbuf(ph, f"accg{i}", [128, 512]) for i in range(2)]
                accv = [sbuf(ph, f"accv{i}", [128, 512]) for i in range(2)]
                sq = sbuf(ph, "sq", [128, D])
                ss2 = sbuf(ph, "ss2", [128, 1]); lnv2 = sbuf(ph, "lnv2", [128, 1]); rstd2 = sbuf(ph, "rstd2", [128, 1])
                sqf = sq
                cpr = sq
                xh = xr[0]
                ss1 = sbuf(ph, "ss1", [128, 1]); lnv1 = sbuf(ph, "lnv1", [128, 1]); rstd1 = sbuf(ph, "rstd1", [128, 1])

                wup_p = [T(wup.t, f"wup_p{p_}") for p_ in range(4)]
                for p_ in range(4):
                    g_lo, g_hi = p_ * 6, min(22, (p_ + 1) * 6)
                    for hh in range(2):
                        load_weight_sb(wup, wup_b[l] if PRECAST else None, 8, 2 * DFF, f"wu{p_}", d_wup[l], col_lo=(hh * 22 + g_lo) * 128,
                                       col_hi=(hh * 22 + g_hi) * 128, deps_t=wup_p[p_], src32=ffn_w_up[l])
                load_weight_sb(wd, wd_b[l] if PRECAST else None, 22, D, "w1", d_wd[l], src32=ffn_w_down[l])
                dma("sp", gain[:], bc_part(ffn_norm[l:l + 1, :]), writes=[gain], sem="c0")
                if final:
                    dma("sp", fgain[:], bc_part(final_norm), writes=[fgain], sem="c0")
                for k in range(3):
                    dma("sp", cpr[0:NCH, k * 128:(k + 1) * 128], ffn_conv_w[l, k].rearrange("(c p) -> c p", p=128),
                        writes=[cpr], sem="c1")
                dma("sp", cpr[0:NCH, 384:512], ffn_conv_b[l].rearrange("(c p) -> c p", p=128), writes=[cpr], sem="c1")
                op("pe", lambda e: [e.transpose(out=PB[0][:, k * NCH:(k + 1) * NCH], in_=cpr[0:NCH, k * 128:(k + 1) * 128],
                                                identity=identf[0:NCH, 0:NCH]) for k in range(4)],
                   reads=[cpr, identf], writes=[PB[0]])
                op("dve", lambda e: e.tensor_copy(out=cp[:], in_=PB[0][:, 0:4 * NCH]), reads=[PB[0]], writes=[cp])

                def cw(k, ch):
                    return cp[:, k * NCH + ch:k * NCH + ch + 1]

                op("dve", lambda e: e.memset(xh[0:16, :], 0.0), writes=[xh])
                for m in range(NB - 1):
                    dma("sp", xh[2 * m:2 * m + 2, :], src[512 * (m + 1) - 1:512 * (m + 1) + 1, :], reads=[dsrc], writes=[xh],
                        sem="lh")
                op("dve", lambda e: e.memset(uh[:], 0.0), writes=[uh])
                if NB > 1:
                    norm_to_bf16(xh, gain, hb, sq, ss1, lnv1, rstd1, npart=16)
                    transpose_h(hb, PT[0], npart=16)
                    op("dve", lambda e: e.tensor_copy(out=hTh[:], in_=PT[0][:, 0:128]), reads=[PT[0]], writes=[hTh])
                    for hh in range(2):
                        pk = PB[1 + hh]
                        op("pe", lambda e, pk=pk, hh=hh: [
                            e.matmul(pk[:, cc * 16:(cc + 1) * 16], lhsT=wup[:, c * 2 * DFF + (hh * 22 + cc) * 128:c * 2 * DFF + (hh * 22 + cc + 1) * 128],
                                     rhs=hTh[:, c * 16:(c + 1) * 16], start=(c == 0), stop=(c == 7))
                            for cc in range(22) for c in range(8)], reads=wup_p + [hTh], writes=[pk])
                        op("dve", lambda e, pk=pk, hh=hh: e.tensor_copy(out=uh[:, hh * 352:(hh + 1) * 352], in_=pk[:, 0:352]),
                           reads=[pk], writes=[uh])
                    uh4 = uh[:].rearrange("p (c m two) -> p c m two", c=NCH, two=2)
                    op("dve", lambda e: e.tensor_tensor(out=uh4[:, :, :, 0], in0=uh4[:, :, :, 0],
                                                        in1=bc_last(cp[:, 0:NCH], 8), op=ALU.mult), reads=[uh, cp], writes=[uh])
                    op("dve", lambda e: e.tensor_tensor(out=uh4[:, :, :, 1], in0=uh4[:, :, :, 1],
                                                        in1=bc_last(cp[:, 2 * NCH:3 * NCH], 8), op=ALU.mult), reads=[uh, cp], writes=[uh])

                def conv_chunk(blk, ch, ub, acc):
                    op("act", lambda e: e.activation(out=acc[:], in_=ub[:], func=AF.Identity, scale=cw(1, ch), bias=cw(3, ch)),
                       reads=[ub, cp], writes=[acc])
                    op("dve", lambda e: e.scalar_tensor_tensor(out=acc[:, 1:512], in0=ub[:, 0:511], scalar=cw(0, ch),
                                                               in1=acc[:, 1:512], op0=ALU.mult, op1=ALU.add),
                       reads=[ub, cp, acc], writes=[acc])
                    op("dve", lambda e: e.scalar_tensor_tensor(out=acc[:, 0:511], in0=ub[:, 1:512], scalar=cw(2, ch),
                                                               in1=acc[:, 0:511], op0=ALU.mult, op1=ALU.add),
                       reads=[ub, cp, acc], writes=[acc])
                    li = ch * 16 + 2 * (blk - 1)
                    ri = ch * 16 + 2 * blk + 1
                    if 0 < blk < NB - 1:
                        op("dve", lambda e: e.tensor_tensor(out=acc[:, 0:512:511], in0=acc[:, 0:512:511], in1=uh[:, li:li + 4:3],
                                                            op=ALU.add), reads=[acc, uh], writes=[acc])
                    elif blk > 0:
                        op("dve", lambda e: e.tensor_tensor(out=acc[:, 0:1], in0=acc[:, 0:1], in1=uh[:, li:li + 1], op=ALU.add),
                           reads=[acc, uh], writes=[acc])
                    elif blk < NB - 1:
                        op("dve", lambda e: e.tensor_tensor(out=acc[:, 511:512], in0=acc[:, 511:512], in1=uh[:, ri:ri + 1], op=ALU.add),
                           reads=[acc, uh], writes=[acc])

                def load_x(t, i):
                    dma("sp", xs[i][:], src[t * 128:(t + 1) * 128, :], reads=[dsrc], writes=[xs[i]], sem=f"lx{i}")

                def norm_stages(t):
                    sts = [lambda: load_x(t + 1, (t + 1) % 2) if t + 1 < NT else None]
                    sts += norm_st(xs[t % 2], gain, hb, sq, ss1, lnv1, rstd1)
                    return sts

                def tr_evac(jj):
                    transpose_h(hb, PT[jj % 2])
                    op("dve", lambda e: e.tensor_copy(
                        out=hT[:].rearrange("p (c m) -> p c m", c=8)[:, :, jj * 128:(jj + 1) * 128],
                        in_=PT[jj % 2][:].rearrange("p (c m) -> p c m", c=8)), reads=[PT[jj % 2]], writes=[hT])

                load_x(0, 0)
                for jj in range(4):
                    for f in norm_stages(jj):
                        f()
                    tr_evac(jj)
                for blk in range(NB):
                    for g in range(22):
                        k = g % 2
                        ug, uv = PB[2 * k], PB[2 * k + 1]
                        for (ub, ch) in ((ug, g), (uv, 22 + g)):
                            op("pe", lambda e, ub=ub, ch=ch: [
                                e.matmul(ub[:], lhsT=wup[:, c * 2 * DFF + ch * 128:c * 2 * DFF + (ch + 1) * 128],
                                         rhs=hT[:, c * 512:(c + 1) * 512], start=(c == 0), stop=(c == 7)) for c in range(8)],
                               reads=[wup_p[g // 6], hT], writes=[ub])
                        conv_chunk(blk, g, ug, accg[k])
                        conv_chunk(blk, 22 + g, uv, accv[k])
                        op("act", lambda e, k=k: e.activation(out=accg[k][:], in_=accg[k][:], func=AF.Silu),
                           reads=[accg[k]], writes=[accg[k]])
                        op("pool", lambda e, k=k, g=g: e.tensor_tensor(out=G[:, g * 512:(g + 1) * 512], in0=accg[k][:], in1=accv[k][:],
                                                                       op=ALU.mult), reads=[accg[k], accv[k]], writes=[G])
                    for jj in range(4):
                        t = blk * 4 + jj
                        nxt = blk + 1 < NB
                        xi = xr[jj % 2]
                        dma("sp", xi[:], src[t * 128:(t + 1) * 128, :], reads=[dsrc], writes=[xi], sem=f"lr{jj % 2}")
                        if nxt:
                            for f in norm_stages(t + 4):
                                f()
                        for half in range(2):
                            pk = PB[4 + half]
                            op("pe", lambda e, pk=pk, half=half, jj=jj: [
                                e.matmul(pk[:], lhsT=G[:, g * 512 + jj * 128:g * 512 + (jj + 1) * 128],
                                         rhs=wd[:, g * D + half * 512:g * D + (half + 1) * 512], start=(g == 0), stop=(g == 21))
                                for g in range(22)], reads=[G, wd], writes=[pk])
                        if nxt:
                            tr_evac(jj)
                        for half in range(2):
                            pk = PB[4 + half]
                            op("dve", lambda e, pk=pk, half=half, xi=xi: e.tensor_tensor(
                                out=xi[:, half * 512:(half + 1) * 512], in0=pk[:], in1=xi[:, half * 512:(half + 1) * 512],
                                op=ALU.add), reads=[pk, xi], writes=[xi])
                        if final:
                            rms_stats(xi[:, :], xi, 128, sqf, ss2, lnv2, rstd2)
                            op("dve", lambda e, xi=xi: e.scalar_tensor_tensor(out=xi[:], in0=xi[:], scalar=rstd2[:, 0:1],
                                                                              in1=fgain[:], op0=ALU.mult, op1=ALU.mult),
                               reads=[xi, rstd2, fgain], writes=[xi])
                        dma("sp", dst[t * 128:(t + 1) * 128, :], xi[:], reads=[xi], writes=[ddst], sem=f"so{jj % 2}")
                SC.barrier()

        def phase_pool(src, dsrc, dst, ddst):
            PADW = S + 16
            with ExitStack() as ph:
                PB, PT = alloc_psum(ph, 6)
                HT = sbuf(ph, "HT", [128, 8 * PADW], BF16)
                wp = sbuf(ph, "wp", [128, 8 * 256], BF16)
                gain = sbuf(ph, "gain_p", [128, D])
                pscale = sbuf(ph, "pscale", [128, D])
                invc = sbuf(ph, "invc", [128, 64])
                xs = [sbuf(ph, f"xs{i}", [128, D]) for i in range(3)]
                hb = [sbuf(ph, f"hb{i}", [128, D], BF16) for i in range(2)]
                Wa = [sbuf(ph, f"Wa{i}", [128, PADW]) for i in range(2)]
                Wb = [sbuf(ph, f"Wb{i}", [128, PADW]) for i in range(2)]
                edge = [sbuf(ph, f"edge{i}", [128, 16]) for i in range(2)]
                edgeb = [sbuf(ph, f"edgeb{i}", [128, 16], BF16) for i in range(2)]
                xn = [sbuf(ph, f"xn{i}", [128, D]) for i in range(2)]
                sq = sbuf(ph, "sq", [128, D])
                ss1 = sbuf(ph, "ss1", [128, 1]); lnv1 = sbuf(ph, "lnv1", [128, 1]); rstd1 = sbuf(ph, "rstd1", [128, 1])

                if PRECAST:
                    dma("sp", wp[:].rearrange("p (c n) -> p c n", c=8), wp_b.rearrange("g (j p) n -> p (g j) n", p=128),
                        reads=[d_wp], writes=[wp], sem="w0")
                else:
                    dma("pool", wp[:].rearrange("p (c n) -> p c n", c=8), pool_w[0].rearrange("g (j p) n -> p (g j) n", p=128),
                        writes=[wp], sem="w0")
                dma("sp", gain[:], bc_part(pool_norm), writes=[gain], sem="c0")
                dma("sp", pscale[:], bc_part(pool_scale), writes=[pscale], sem="c0")
                dma("sp", invc[:], bc_part(invc_in), writes=[invc], sem="c0")
                H3 = HT[:].rearrange("p (c w) -> p c w", c=8)
                op("pool", lambda e: e.memset(H3[:, :, 0:8], 0.0), writes=[HT])
                op("pool", lambda e: e.memset(H3[:, :, 8 + S:16 + S], 0.0), writes=[HT])

                def load_x(t, i):
                    dma("sp", xs[i][:], src[t * 128:(t + 1) * 128, :], reads=[dsrc], writes=[xs[i]], sem=f"lx{i}")

                load_x(0, 0)
                for t in range(NT):
                    if t + 1 < NT:
                        load_x(t + 1, (t + 1) % 3)
                    k = t % 2
                    norm_to_bf16(xs[t % 3], gain, hb[k], sq, ss1, lnv1, rstd1)
                    transpose_h(hb[k], PT[k])
                    op("act", lambda e, k=k, t=t: e.activation(out=H3[:, :, 8 + t * 128:8 + (t + 1) * 128],
                                                               in_=PT[k][:].rearrange("p (c m) -> p c m", c=8), func=AF.Copy),
                       reads=[PT[k]], writes=[HT])
                for c in range(8):
                    grp = c // 2
                    eng = "pool" if c in (5, 7) else "dve"
                    Hc = HT[:, c * PADW:(c + 1) * PADW]
                    ks = 1 if eng == "pool" else 0
                    A_, B_ = Wa[ks], Wb[ks]
                    ed, edb = edge[ks], edgeb[ks]
                    op(eng, lambda e: e.tensor_tensor(out=A_[:, 1:PADW], in0=Hc[:, 0:PADW - 1], in1=Hc[:, 1:PADW], op=ALU.add),
                       reads=[HT], writes=[A_])
                    cur, other = A_, B_
                    lo, hi = 1, PADW
                    sh = 1
                    for lvl in range(grp):
                        nlo, nhi = lo + sh, hi - sh
                        op(eng, lambda e, cur=cur, other=other, nlo=nlo, nhi=nhi, sh=sh: e.tensor_tensor(
                            out=other[:, nlo:nhi], in0=cur[:, nlo - sh:nhi - sh], in1=cur[:, nlo + sh:nhi + sh], op=ALU.add),
                           reads=[cur], writes=[other])
                        cur, other = other, cur
                        lo, hi = nlo, nhi
                        sh *= 2
                    w = 2 ** (grp + 1)
                    cur_e = bass.AP(cur.t[:, 8:16].tensor, cur.t[:, 8:16].offset, [list(cur.t[:, 8:16].ap[0]), [S - 8, 2], [1, 8]])
                    h_e = bass.AP(Hc[:, 8:16].tensor, Hc[:, 8:16].offset, [list(Hc[:, 8:16].ap[0]), [S - 8, 2], [1, 8]])
                    op(eng, lambda e: e.tensor_tensor(out=ed[:].rearrange("p (a b) -> p a b", a=2), in0=cur_e,
                                                      in1=invc[:, grp * 16:(grp + 1) * 16].rearrange("p (a b) -> p a b", a=2),
                                                      op=ALU.mult), reads=[cur, invc], writes=[ed])
                    op(eng, lambda e: e.tensor_tensor(out=edb[:].rearrange("p (a b) -> p a b", a=2),
                                                      in0=ed[:].rearrange("p (a b) -> p a b", a=2), in1=h_e, op=ALU.subtract),
                       reads=[ed, HT], writes=[edb])
                    if eng == "dve":
                        op(eng, lambda e: e.scalar_tensor_tensor(out=Hc[:, 8:8 + S], in0=cur[:, 8:8 + S], scalar=1.0 / w,
                                                                 in1=Hc[:, 8:8 + S], op0=ALU.mult, op1=ALU.subtract),
                           reads=[cur, HT], writes=[HT])
                    else:
                        op(eng, lambda e: e.tensor_scalar(out=cur[:, 8:8 + S], in0=cur[:, 8:8 + S], scalar1=1.0 / w, scalar2=0.0,
                                                          op0=ALU.mult, op1=ALU.add), reads=[cur], writes=[cur])
                        op(eng, lambda e: e.tensor_tensor(out=Hc[:, 8:8 + S], in0=cur[:, 8:8 + S], in1=Hc[:, 8:8 + S],
                                                          op=ALU.subtract), reads=[cur, HT], writes=[HT])
                    op(eng, lambda e: e.tensor_copy(out=h_e, in_=edb[:].rearrange("p (a b) -> p a b", a=2)),
                       reads=[edb], writes=[HT])
                load_x(0, 0)
                for t in range(NT):
                    if t + 1 < NT:
                        load_x(t + 1, (t + 1) % 3)
                    xi = xs[t % 3]
                    xo = xn[t % 2]
                    for half in range(2):
                        pk = PB[(t % 2) * 2 + half]
                        op("pe", lambda e, pk=pk, half=half, t=t: [
                            e.matmul(pk[:, gg * 256:(gg + 1) * 256],
                                     lhsT=HT[:, (4 * half + 2 * gg + jc) * PADW + 8 + t * 128:(4 * half + 2 * gg + jc) * PADW + 8 + (t + 1) * 128],
                                     rhs=wp[:, (4 * half + 2 * gg + jc) * 256:(4 * half + 2 * gg + jc + 1) * 256],
                                     start=(jc == 0), stop=(jc == 1)) for gg in range(2) for jc in range(2)],
                           reads=[HT, wp], writes=[pk])
                        op("dve", lambda e, pk=pk, half=half: e.tensor_tensor(
                            out=xo[:, half * 512:(half + 1) * 512], in0=pk[:], in1=pscale[:, half * 512:(half + 1) * 512],
                            op=ALU.mult), reads=[pk, pscale], writes=[xo])
                        op("dve", lambda e, half=half: e.tensor_tensor(
                            out=xo[:, half * 512:(half + 1) * 512], in0=xo[:, half * 512:(half + 1) * 512],
                            in1=xi[:, half * 512:(half + 1) * 512], op=ALU.add), reads=[xo, xi], writes=[xo])
                    dma("sp", dst[t * 128:(t + 1) * 128, :], xo[:], reads=[xo], writes=[ddst], sem=f"so{t % 2}")
                SC.barrier()

        SC.barrier()
        plan = [
            ("attn", lambda s, ds, d, dd: phase_attention(s, ds, d, dd)),
            ("xattn0", lambda s, ds, d, dd: phase_xattn(0, s, ds, d, dd)),
            ("ffn0", lambda s, ds, d, dd: phase_ffn(0, s, ds, d, dd, False)),
            ("pool", lambda s, ds, d, dd: phase_pool(s, ds, d, dd)),
            ("xattn1", lambda s, ds, d, dd: phase_xattn(1, s, ds, d, dd)),
            ("ffn1", lambda s, ds, d, dd: phase_ffn(1, s, ds, d, dd, True)),
        ]
        if stop_after is not None:
            plan = plan[:stop_after]
        cur, dcur = x_in, dX
        for i, (name, fn) in enumerate(plan):
            last = i == len(plan) - 1
            if last:
                nxt, dnxt = out, dO
            else:
                nxt, dnxt = (rA, dA) if i % 2 == 0 else (rB, dB)
            fn(cur, dcur, nxt, dnxt)
            cur, dcur = nxt, dnxt
        SC.barrier()
        build.stats = (SC.nops, SC.nwaits)
    return nc


def rope_table(S):
    t = np.arange(S)
    row = (t // 64).astype(np.float32)
    col = (t % 64).astype(np.float32)
    inv_freq = (10000.0 ** (-np.arange(16, dtype=np.float32) / 16)).astype(np.float32)
    ang = np.stack([row[:, None] * inv_freq, col[:, None] * inv_freq], axis=1).astype(np.float32)
    c, s = np.cos(ang).astype(np.float32), np.sin(ang).astype(np.float32)
    cos64 = np.concatenate([c[:, 0], c[:, 0], c[:, 1], c[:, 1]], axis=1)
    return np.ascontiguousarray(np.concatenate([cos64, -s.reshape(S, 32), s.reshape(S, 32)], axis=1).astype(np.float32))


def invc_table(S):
    tab = np.zeros((4, 16), np.float32)
    toks = np.concatenate([np.arange(8), np.arange(S - 8, S)])
    for g, w in enumerate((2, 4, 8, 16)):
        lo = np.clip(toks - w // 2, 0, S)
        hi = np.clip(toks + w - w // 2, 0, S)
        tab[g] = 1.0 / (hi - lo).astype(np.float32)
    return tab.reshape(1, 64)


_NC_CACHE = {}


def kernel(**inputs):
    S = inputs["x"].shape[1]
    B = inputs["x"].shape[0]
    if S not in _NC_CACHE:
        _NC_CACHE[S] = build(S)
    nc = _NC_CACHE[S]
    shared = {k: np.ascontiguousarray(np.asarray(v, dtype=np.float32)) for k, v in inputs.items() if k not in ("x", "mem")}
    shared["final_norm"] = shared["final_norm"].reshape(1, D)
    shared["rope"] = rope_table(S)
    shared["invc"] = invc_table(S)
    in_maps = []
    for b in range(B):
        m = dict(shared)
        m["x"] = np.ascontiguousarray(np.asarray(inputs["x"][b], dtype=np.float32))
        m["mem"] = np.ascontiguousarray(np.asarray(inputs["mem"][b], dtype=np.float32))
        in_maps.append(m)
    res = run_bass_kernel_spmd(nc, in_maps, core_ids=list(range(B)))
    return np.stack([np.asarray(r["out"], dtype=np.float32) for r in res.results], axis=0)
```

```python
import numpy as np
from contextlib import ExitStack
import concourse.bass as bass
import concourse.mybir as mybir
from concourse.bass_utils import run_bass_kernel_spmd

F32 = mybir.dt.float32
BF16 = mybir.dt.bfloat16
AF = mybir.ActivationFunctionType
ALU = mybir.AluOpType
AX = mybir.AxisListType

D = 1024
DFF = 2816
NCH = 44
EPS = 1e-6
MEM = 256


class Buf:
    __slots__ = ("name", "w", "r")

    def __init__(self, name):
        self.name = name
        self.w = {}
        self.r = {}


class T:
    __slots__ = ("t", "b")

    def __init__(self, t, name):
        self.t = t
        self.b = Buf(name)

    def __getitem__(self, k):
        return self.t[k]


class Sched:
    def __init__(self, nc, stack):
        self.nc = nc
        self.stack = stack
        self.engs = {"pe": nc.tensor, "act": nc.scalar, "dve": nc.vector, "pool": nc.gpsimd, "sp": nc.sync}
        self.sems = {}
        self.cnt = {}
        self.seen = {}
        self.nops = 0
        self.nwaits = 0
        for k in ("pe", "act", "dve", "pool"):
            self._sem(k)

    def _sem(self, name):
        if name not in self.sems:
            self.sems[name] = self.stack.enter_context(self.nc.semaphore("s_" + name))
            self.cnt[name] = 0
        return self.sems[name]

    def _deps(self, eng, reads, writes):
        deps = {}
        for t in reads:
            for s, v in t.b.w.items():
                if deps.get(s, 0) < v:
                    deps[s] = v
        for t in writes:
            for d in (t.b.w, t.b.r):
                for s, v in d.items():
                    if s == eng and eng == "pe":
                        continue
                    if deps.get(s, 0) < v:
                        deps[s] = v
        return deps

    def _wait(self, eng, deps):
        e = self.engs[eng]
        for s, v in deps.items():
            if s not in ("pe", "act", "dve", "pool"):
                v = self.cnt[s]
            if self.seen.get((eng, s), 0) < v:
                e.wait_ge(self.sems[s], v)
                self.seen[(eng, s)] = v
                self.nwaits += 1

    def op(self, eng, fn, reads=(), writes=()):
        self._wait(eng, self._deps(eng, reads, writes))
        ins = fn(self.engs[eng])
        if isinstance(ins, (list, tuple)):
            ins = ins[-1]
        self.cnt[eng] += 1
        v = self.cnt[eng]
        ins.then_inc(self.sems[eng], 1)
        for t in reads:
            t.b.r[eng] = v
        for t in writes:
            t.b.w[eng] = v
        self.nops += 1

    def dma(self, queue, out, in_, reads=(), writes=(), sem=None, **kw):
        self._sem(sem)
        self._wait(queue, self._deps("__dma__", reads, writes))
        ins = self.engs[queue].dma_start(out=out, in_=in_, **kw)
        self.cnt[sem] += 16
        v = self.cnt[sem]
        ins.then_inc(self.sems[sem], 16)
        for t in reads:
            t.b.r[sem] = v
        for t in writes:
            t.b.w[sem] = v
        self.nops += 1

    def barrier(self):
        deps = {s: v for s, v in self.cnt.items() if v > 0}
        for eng in ("sp", "pe", "act", "dve", "pool"):
            self._wait(eng, deps)


def bc_last(ap, n):
    return bass.AP(ap.tensor, ap.offset, [list(ap.ap[0]), list(ap.ap[1]), [0, n]])


def bc_mid(ap, n):
    return bass.AP(ap.tensor, ap.offset, [list(ap.ap[0]), [0, n]] + [list(a) for a in ap.ap[1:]])


def bc_part(ap, n=128):
    return bass.AP(ap.tensor, ap.offset, [[0, n]] + [list(a) for a in ap.ap[1:]])


def build(S=4096, nlayers_dbg=None, stop_after=None):
    NT = S // 128
    NB = S // 512
    nc = bass.Bass("TRN2", target_bir_lowering=False)

    def din(name, shape):
        return nc.dram_tensor(name, list(shape), F32, kind="ExternalInput").ap()

    x_in = din("x", [S, D])
    mem_in = din("mem", [MEM, D])
    attn_norm = din("attn_norm", [1, D])
    attn_w_qkv = din("attn_w_qkv", [1, D, 1536])
    attn_q_gain = din("attn_q_gain", [1, 64])
    attn_k_gain = din("attn_k_gain", [1, 64])
    attn_w_o = din("attn_w_o", [1, D, D])
    pool_norm = din("pool_norm", [1, D])
    pool_w = din("pool_w", [1, 4, 256, 256])
    pool_scale = din("pool_scale", [1, D])
    xattn_norm = din("xattn_norm", [2, D])
    mem_norm = din("mem_norm", [2, D])
    xattn_w_q = din("xattn_w_q", [2, D, D])
    xattn_w_kv = din("xattn_w_kv", [2, D, 2 * D])
    xattn_w_o = din("xattn_w_o", [2, D, D])
    ffn_norm = din("ffn_norm", [2, D])
    ffn_w_up = din("ffn_w_up", [2, D, 2 * DFF])
    ffn_conv_w = din("ffn_conv_w", [2, 3, 2 * DFF])
    ffn_conv_b = din("ffn_conv_b", [2, 2 * DFF])
    ffn_w_down = din("ffn_w_down", [2, DFF, D])
    final_norm = din("final_norm", [1, D])
    rope_in = din("rope", [S, 128])
    invc_in = din("invc", [1, 64])
    out = nc.dram_tensor("out", [S, D], F32, kind="ExternalOutput").ap()
    rA = nc.dram_tensor("resA", [S, D], F32, kind="Internal").ap()
    rB = nc.dram_tensor("resB", [S, D], F32, kind="Internal").ap()
    dA = T(None, "resA")
    dB = T(None, "resB")
    dX = T(None, "x")
    dO = T(None, "out")

    with ExitStack() as st:
        SC = Sched(nc, st)
        op = SC.op
        dma = SC.dma

        uid = [0]

        def sbuf(stack, name, shape, dt=F32):
            uid[0] += 1
            name = f"{name}_{uid[0]}"
            return T(stack.enter_context(nc.sbuf_tensor(name, list(shape), dt)), name)

        def psum(stack, name, shape, dt=F32):
            return T(stack.enter_context(nc.psum_tensor(name, list(shape), dt)), name)

        ident = sbuf(st, "ident", [128, 128], BF16)
        identf = sbuf(st, "identf", [128, 128], F32)
        ones_f = sbuf(st, "ones_f", [128, 128], F32)
        ones_b = sbuf(st, "ones_b", [128, 128], BF16)
        for idt in (ident, identf):
            op("pool", lambda e, idt=idt: e.memset(idt[:], 0.0), writes=[idt])
            op("pool", lambda e, idt=idt: e.affine_select(out=idt[:], in_=idt[:], pattern=[[-1, 128]],
                                                         compare_op=ALU.not_equal, fill=1.0, base=0,
                                                         channel_multiplier=1), reads=[idt], writes=[idt])
        op("dve", lambda e: e.memset(ones_f[:], 1.0), writes=[ones_f])
        op("dve", lambda e: e.memset(ones_b[:], 1.0), writes=[ones_b])

        def alloc_psum(stack, nf32):
            uid[0] += 1
            pb = [psum(stack, f"pb{i}_{uid[0]}", [128, 512], F32) for i in range(nf32)]
            pt = [psum(stack, f"pt{i}_{uid[0]}", [128, 1024], BF16) for i in range(8 - nf32)]
            return pb, pt

        def load_weight_bf16(w_t, src2d, nchunk, ncols, sem, col_split=1, p=128):
            dst3 = w_t[:].rearrange("p (c n) -> p c n", c=nchunk)
            src3 = src2d.rearrange("(c p) n -> p c n", p=p)
            step = ncols // col_split
            for i in range(col_split):
                dma("pool", dst3[:, :, i * step:(i + 1) * step], src3[:, :, i * step:(i + 1) * step],
                    writes=[w_t], sem=sem)

        def rms_stats_st(xt_ap, xt, npart, sq, ss, lnv, rstd, width=D, out_bias=0.0):
            def a():
                op("dve", lambda e: e.tensor_tensor(out=sq[0:npart, 0:width], in0=xt_ap, in1=xt_ap, op=ALU.mult),
                   reads=[xt], writes=[sq])
                op("dve", lambda e: e.tensor_reduce(out=ss[0:npart, 0:1], in_=sq[0:npart, 0:width], axis=AX.X, op=ALU.add),
                   reads=[sq], writes=[ss])

            def b():
                op("act", lambda e: e.activation(out=lnv[0:npart, 0:1], in_=ss[0:npart, 0:1], func=AF.Ln,
                                                 scale=1.0 / width, bias=EPS), reads=[ss], writes=[lnv])
                op("act", lambda e: e.activation(out=rstd[0:npart, 0:1], in_=lnv[0:npart, 0:1], func=AF.Exp,
                                                 scale=-0.5, bias=out_bias), reads=[lnv], writes=[rstd])
            return [a, b]

        def rms_stats(*a, **k):
            for f in rms_stats_st(*a, **k):
                f()

        def norm_st(xt, gain, hb, sq, ss, lnv, rstd, npart=128):
            def c():
                op("dve", lambda e: e.scalar_tensor_tensor(out=hb[0:npart, :], in0=xt[0:npart, :], scalar=rstd[0:npart, 0:1],
                                                          in1=gain[0:npart, :], op0=ALU.mult, op1=ALU.mult),
                   reads=[xt, rstd, gain], writes=[hb])
            return rms_stats_st(xt[0:npart, :], xt, npart, sq, ss, lnv, rstd) + [c]

        def norm_to_bf16(*a, **k):
            for f in norm_st(*a, **k):
                f()

        def transpose_h(hb, pt, npart=128):
            op("pe", lambda e: [e.transpose(out=pt[:, c * npart:(c + 1) * npart], in_=hb[0:npart, c * 128:(c + 1) * 128],
                                            identity=ident[0:npart, 0:npart]) for c in range(8)],
               reads=[hb, ident], writes=[pt])

        def qk_norm_rope_st(src, H, gain, rope, dst, tmp, ss, lnv, rstd, out_bias):
            W = H * 64
            sq, ta, tb = tmp
            s3 = src[:, 0:W].rearrange("p (h d) -> p h d", d=64)
            a3 = ta[:, 0:W].rearrange("p (h d) -> p h d", d=64)
            b3 = tb[:, 0:W].rearrange("p (h d) -> p h d", d=64)
            a5 = ta[:, 0:W].rearrange("p (h a f q) -> p h a f q", a=2, f=2, q=16)
            s5 = sq[:, 0:W].rearrange("p (h a f q) -> p h a f q", a=2, f=2, q=16)
            nsin = bc_mid(rope[:, 64:96].rearrange("p (a q) -> p a q", a=2), H)
            psin = bc_mid(rope[:, 96:128].rearrange("p (a q) -> p a q", a=2), H)

            def a():
                op("dve", lambda e: e.tensor_tensor(out=sq[:, 0:W], in0=src[:, 0:W], in1=src[:, 0:W], op=ALU.mult),
                   reads=[src], writes=[sq])
                op("dve", lambda e: e.tensor_reduce(out=ss[:, 0:H], in_=sq[:, 0:W].rearrange("p (h d) -> p h d", d=64),
                                                    axis=AX.X, op=ALU.add), reads=[sq], writes=[ss])

            def b():
                op("act", lambda e: e.activation(out=lnv[:, 0:H], in_=ss[:, 0:H], func=AF.Ln, scale=1.0 / 64, bias=EPS),
                   reads=[ss], writes=[lnv])
                op("act", lambda e: e.activation(out=rstd[:, 0:H], in_=lnv[:, 0:H], func=AF.Exp, scale=-0.5, bias=out_bias),
                   reads=[lnv], writes=[rstd])

            def c():
                op("dve", lambda e: e.tensor_tensor(out=a3, in0=s3, in1=bc_last(rstd[:, 0:H], 64), op=ALU.mult),
                   reads=[src, rstd], writes=[ta])
                op("dve", lambda e: e.tensor_tensor(out=a3, in0=a3, in1=bc_mid(gain[:, 0:64], H), op=ALU.mult),
                   reads=[ta, gain], writes=[ta])
                op("dve", lambda e: e.tensor_tensor(out=b3, in0=a3, in1=bc_mid(rope[:, 0:64], H), op=ALU.mult),
                   reads=[ta, rope], writes=[tb])
                op("dve", lambda e: e.tensor_tensor(out=s5[:, :, :, 0, :], in0=a5[:, :, :, 1, :], in1=nsin, op=ALU.mult),
                   reads=[ta, rope], writes=[sq])
                op("dve", lambda e: e.tensor_tensor(out=s5[:, :, :, 1, :], in0=a5[:, :, :, 0, :], in1=psin, op=ALU.mult),
                   reads=[ta, rope], writes=[sq])
                op("dve", lambda e: e.tensor_tensor(out=dst[:, 0:W], in0=tb[:, 0:W], in1=sq[:, 0:W], op=ALU.add),
                   reads=[tb, sq], writes=[dst])
            return [a, b, c]

        def phase_attention(src, dsrc, dst, ddst):
            with ExitStack() as ph:
                PB, PT1 = alloc_psum(ph, 7)
                PT = [PT1[0], PT1[0]]
                wqkv = sbuf(ph, "wqkv", [128, 8 * 1536], BF16)
                wo = sbuf(ph, "wo_a", [64, 16 * D], BF16)
                KT = sbuf(ph, "KT", [128, 4 * S], BF16)
                Vs = sbuf(ph, "Vs", [128, NT * 4 * 65], BF16)
                gain = sbuf(ph, "gain_a", [128, D])
                qg = sbuf(ph, "qg", [128, 64])
                kg = sbuf(ph, "kg", [128, 64])
                xs = [sbuf(ph, f"xs{i}", [128, D]) for i in range(3)]
                rp = [sbuf(ph, f"rp{i}", [128, 128]) for i in range(3)]
                hb = [sbuf(ph, f"hb{i}", [128, D], BF16) for i in range(2)]
                hT = [sbuf(ph, f"hT{i}", [128, 8 * 128], BF16) for i in range(2)]
                sq = sbuf(ph, "sq", [128, D])
                ta = sbuf(ph, "ta", [128, D])
                tb = sbuf(ph, "tb", [128, D])
                qs = sbuf(ph, "qs", [128, D])
                qr = sbuf(ph, "qr", [128, D], BF16)
                QT = [sbuf(ph, f"QT{i}", [128, 16 * 128], BF16) for i in range(2)]
                Pb = [sbuf(ph, f"Pb{i}", [128, 512], BF16) for i in range(3)]
                OT = [sbuf(ph, f"OT{i}", [64, 16 * 128], BF16) for i in range(2)]
                rden = sbuf(ph, "rden", [128, 512])
                bcs = sbuf(ph, "bcs", [64, 512])
                xn = [sbuf(ph, f"xn{i}", [128, D]) for i in range(2)]
                ss = sbuf(ph, "ss", [128, 16]); lnv = sbuf(ph, "lnv", [128, 16]); rstd = sbuf(ph, "rstd", [128, 16])
                ss1 = sbuf(ph, "ss1", [128, 1]); lnv1 = sbuf(ph, "lnv1", [128, 1]); rstd1 = sbuf(ph, "rstd1", [128, 1])

                load_weight_bf16(wqkv, attn_w_qkv[0], 8, 1536, "w0")
                dma("pool", wo[:].rearrange("d (h n) -> d h n", h=16), attn_w_o[0].rearrange("(h d) n -> d h n", d=64),
                    writes=[wo], sem="w1")
                dma("sp", gain[:], bc_part(attn_norm), writes=[gain], sem="c0")
                dma("sp", qg[:], bc_part(attn_q_gain), writes=[qg], sem="c0")
                dma("sp", kg[:], bc_part(attn_k_gain), writes=[kg], sem="c0")
                op("pool", lambda e: e.memset(KT[64:128, :], 0.0), writes=[KT])
                for q in QT:
                    op("pool", lambda e, q=q: e.memset(q[64:128, :], 0.0), writes=[q])
                op("dve", lambda e: e.memset(Vs[:].rearrange("p (t e) -> p t e", e=65)[:, :, 64:65], 1.0), writes=[Vs])

                def load_tile(t, i):
                    dma("sp", xs[i][:], src[t * 128:(t + 1) * 128, :], reads=[dsrc], writes=[xs[i]], sem=f"lx{i}")
                    dma("sp", rp[i][:], rope_in[t * 128:(t + 1) * 128, :], writes=[rp[i]], sem=f"lr{i}")

                sqn = sbuf(ph, "sqn", [128, D])

                def kv_stA(t):
                    i, j = t % 3, t % 2
                    sts = [lambda: load_tile(t + 1, (t + 1) % 3) if t + 1 < NT else None]
                    sts += norm_st(xs[i], gain, hb[j], sqn, ss1, lnv1, rstd1)
                    sts.append(lambda: transpose_h(hb[j], PT[0]))
                    sts.append(lambda: op("dve", lambda e: e.tensor_copy(out=hT[j][:], in_=PT[0][:]), reads=[PT[0]], writes=[hT[j]]))
                    return sts

                def kv_stB(t):
                    i, j = t % 3, t % 2
                    pk = PB[j]
                    pt = PB7b
                    sts = [lambda: op("pe", lambda e: [e.matmul(pk[:], lhsT=hT[j][:, c * 128:(c + 1) * 128],
                                                                rhs=wqkv[:, c * 1536 + 1024:c * 1536 + 1536],
                                                                start=(c == 0), stop=(c == 7)) for c in range(8)],
                                      reads=[hT[j], wqkv], writes=[pk]),
                           lambda: op("act", lambda e: e.activation(out=qs[:, 0:512], in_=pk[:], func=AF.Copy),
                                      reads=[pk], writes=[qs])]
                    sts += qk_norm_rope_st(qs, 4, kg, rp[i], qr, (sq, ta, tb), ss, lnv, rstd, 0.0)
                    sts.append(lambda: op("act", lambda e: e.activation(
                        out=Vs[:, t * 260:(t + 1) * 260].rearrange("p (g e) -> p g e", e=65)[:, :, 0:64],
                        in_=qs[:, 256:512].rearrange("p (g d) -> p g d", d=64), func=AF.Copy), reads=[qs], writes=[Vs]))
                    sts.append(lambda: op("pe", lambda e: [e.transpose(out=pt[0:64, g * 128:(g + 1) * 128],
                                                                       in_=qr[:, g * 64:(g + 1) * 64], identity=ident[:])
                                                           for g in range(4)], reads=[qr, ident], writes=[pt]))
                    sts.append(lambda: op("dve", lambda e: e.tensor_copy(
                        out=KT[0:64, :].rearrange("p (g s) -> p g s", g=4)[:, :, t * 128:(t + 1) * 128],
                        in_=pt[0:64, 0:512].rearrange("p (g s) -> p g s", g=4)), reads=[pt], writes=[KT]))
                    return sts

                PB7b = PT[0]
                load_tile(0, 0)
                for f in kv_stA(0):
                    f()
                for t in range(NT):
                    sa = kv_stA(t + 1) if t + 1 < NT else []
                    sb_ = kv_stB(t)
                    for k in range(max(len(sa), len(sb_))):
                        if k < len(sb_):
                            sb_[k]()
                        if k < len(sa):
                            sa[k]()

                def prep_stages(qb):
                    i, j = qb % 3, qb % 2
                    sts = [lambda: load_tile(qb, i)]
                    sts += norm_st(xs[i], gain, hb[j], sqn, ss1, lnv1, rstd1)
                    sts.append(lambda: transpose_h(hb[j], PT[0]))
                    sts.append(lambda: op("dve", lambda e: e.tensor_copy(out=hT[j][:], in_=PT[0][:]), reads=[PT[0]], writes=[hT[j]]))

                    def qproj(half):
                        pk = PB[half]
                        op("pe", lambda e: [e.matmul(pk[:], lhsT=hT[j][:, c * 128:(c + 1) * 128],
                                                     rhs=wqkv[:, c * 1536 + half * 512:c * 1536 + half * 512 + 512],
                                                     start=(c == 0), stop=(c == 7)) for c in range(8)],
                           reads=[hT[j], wqkv], writes=[pk])

                    def qcopy(half):
                        pk = PB[half]
                        op("act", lambda e: e.activation(out=qs[:, half * 512:(half + 1) * 512], in_=pk[:], func=AF.Copy),
                           reads=[pk], writes=[qs])
                    sts.append(lambda: (qproj(0), qproj(1)))
                    sts.append(lambda: (qcopy(0), qcopy(1)))
                    sts += qk_norm_rope_st(qs, 16, qg, rp[i], qr, (sq, ta, tb), ss, lnv, rstd, float(np.log(0.125)))

                    def qtr(half):
                        pt = PT[0]
                        op("pe", lambda e: [e.transpose(out=pt[0:64, h * 128:(h + 1) * 128],
                                                        in_=qr[:, (half * 8 + h) * 64:(half * 8 + h + 1) * 64],
                                                        identity=ident[:]) for h in range(8)],
                           reads=[qr, ident], writes=[pt])

                    def qev(half):
                        pt = PT[0]
                        op("dve", lambda e: e.tensor_copy(out=QT[j][0:64, half * 1024:(half + 1) * 1024], in_=pt[0:64, :]),
                           reads=[pt], writes=[QT[j]])
                    sts.append(lambda: qtr(0))
                    sts.append(lambda: qev(0))
                    sts.append(lambda: qtr(1))
                    sts.append(lambda: qev(1))
                    return sts

                for f in prep_stages(0):
                    f()
                SB_ = [PB[2], PB[3], PB[4]]
                OB_ = [PB[5], PB[6]]
                steps = [(qb, g, kt) for qb in range(NT) for g in range(4) for kt in range(NT)]
                nsteps = len(steps)
                DEPTH = 3
                deferred = {}

                def defer(i, fn):
                    deferred.setdefault(i, []).append(fn)

                def s_mm(i):
                    qb, g, kt = steps[i]
                    sbk = SB_[i % DEPTH]
                    jq = qb % 2
                    op("pe", lambda e: e.matmul(sbk[:], lhsT=KT[:, g * S + kt * 128:g * S + (kt + 1) * 128],
                                                rhs=QT[jq][:, g * 512:(g + 1) * 512], start=True, stop=True),
                       reads=[KT, QT[jq]], writes=[sbk])

                def normalize(qb, g, ob):
                    jq = qb % 2
                    bcb = PB[g % 2]
                    op("pe", lambda e: e.matmul(bcb[0:64, :], lhsT=ones_f[64:65, 0:64], rhs=rden[64:65, :],
                                                start=True, stop=True), reads=[ones_f, rden], writes=[bcb])
                    op("dve", lambda e: e.tensor_copy(out=bcs[:], in_=bcb[0:64, :]), reads=[bcb], writes=[bcs])
                    op("dve", lambda e: e.tensor_tensor(out=OT[jq][:, g * 512:(g + 1) * 512], in0=ob[0:64, :], in1=bcs[:],
                                                        op=ALU.mult), reads=[ob, bcs], writes=[OT[jq]])

                def wo_store(qb):
                    jq = qb % 2
                    xi = xs[qb % 3]
                    xo = xn[jq]
                    for half in range(2):
                        pk = PB[half]
                        op("pe", lambda e: [e.matmul(pk[:], lhsT=OT[jq][:, h * 128:(h + 1) * 128],
                                                     rhs=wo[:, h * D + half * 512:h * D + half * 512 + 512],
                                                     start=(h == 0), stop=(h == 15)) for h in range(16)],
                           reads=[OT[jq], wo], writes=[pk])
                        op("dve", lambda e: e.tensor_tensor(out=xo[:, half * 512:(half + 1) * 512], in0=pk[:],
                                                            in1=xi[:, half * 512:(half + 1) * 512], op=ALU.add),
                           reads=[pk, xi], writes=[xo])
                    dma("sp", dst[qb * 128:(qb + 1) * 128, :], xo[:], reads=[xo], writes=[ddst], sem=f"so{jq}")

                for i in range(min(DEPTH, nsteps)):
                    s_mm(i)
                for i, (qb, g, kt) in enumerate(steps):
                    if qb + 1 < NT:
                        loc = g * NT + kt
                        if loc == 0:
                            cur_prep = prep_stages(qb + 1)
                            spacing = max(1, (4 * NT - 8) // len(cur_prep))
                        if loc % spacing == 0 and loc // spacing < len(cur_prep):
                            cur_prep[loc // spacing]()
                    ob = OB_[g % 2]
                    sbk = SB_[i % DEPTH]
                    pb = Pb[i % 3]
                    op("act", lambda e: e.activation(out=pb[:], in_=sbk[:], func=AF.Exp), reads=[sbk], writes=[pb])
                    op("pe", lambda e: e.matmul(ob[0:65, :], lhsT=Vs[:, (kt * 4 + g) * 65:(kt * 4 + g + 1) * 65],
                                                rhs=pb[:], start=(kt == 0), stop=(kt == NT - 1)),
                       reads=[Vs, pb], writes=[ob])
                    if i + DEPTH < nsteps:
                        s_mm(i + DEPTH)
                    for fn in deferred.pop(i, []):
                        fn()
                    if kt == NT - 1:
                        op("dve", lambda e: e.reciprocal(out=rden[64:65, :], in_=ob[64:65, :]), reads=[ob], writes=[rden])
                        dd_ = max(1, min(8, NT // 2))
                        if i + dd_ >= nsteps:
                            normalize(qb, g, ob)
                            if g == 3:
                                wo_store(qb)
                        else:
                            defer(i + dd_, lambda qb=qb, g=g, ob=ob: normalize(qb, g, ob))
                            if g == 3:
                                defer(min(i + dd_ + 4, nsteps - 1) if i + dd_ + 4 < nsteps else i + dd_, lambda qb=qb: wo_store(qb))
                SC.barrier()

        def phase_xattn(l, src, dsrc, dst, ddst):
            with ExitStack() as ph:
                PB, PT1 = alloc_psum(ph, 7)
                PT = [PT1[0], PT1[0]]
                wq = sbuf(ph, "wq", [128, 8 * D], BF16)
                wkv = sbuf(ph, "wkv", [128, 8 * 2 * D], BF16)
                wo = sbuf(ph, "wo_x", [128, 8 * D], BF16)
                KmT = sbuf(ph, "KmT", [128, 8 * MEM], BF16)
                Vm = sbuf(ph, "Vm", [128, 2 * D], BF16)
                gain = sbuf(ph, "gain_x", [128, D])
                mgain = sbuf(ph, "mgain", [128, D])
                xs = [sbuf(ph, f"xs{i}", [128, D]) for i in range(8)]
                hb = [sbuf(ph, f"hb{i}", [128, D], BF16) for i in range(2)]
                hT = sbuf(ph, "hT", [128, 8 * 512], BF16)
                memT = sbuf(ph, "memT", [128, 8 * MEM], BF16)
                qT = sbuf(ph, "qT", [128, 8 * 512], BF16)
                Pb = [sbuf(ph, f"Pb{i}", [128, 512], BF16) for i in range(4)]
                OT = sbuf(ph, "OT", [128, 8 * 512], BF16)
                rden = [sbuf(ph, f"rden{i}", [128, 512]) for i in range(2)]
                xn = [sbuf(ph, f"xn{i}", [128, D]) for i in range(2)]
                sq = sbuf(ph, "sq", [128, D])
                ss1 = sbuf(ph, "ss1", [128, 1]); lnv1 = sbuf(ph, "lnv1", [128, 1]); rstd1 = sbuf(ph, "rstd1", [128, 1])

                load_weight_bf16(wkv, xattn_w_kv[l], 8, 2 * D, "w0", col_split=2)
                load_weight_bf16(wq, xattn_w_q[l], 8, D, "w1")
                load_weight_bf16(wo, xattn_w_o[l], 8, D, "w2")
                dma("sp", gain[:], bc_part(xattn_norm[l:l + 1, :]), writes=[gain], sem="c0")
                dma("sp", mgain[:], bc_part(mem_norm[l:l + 1, :]), writes=[mgain], sem="c0")

                for mt in range(2):
                    dma("sp", xs[mt][:], mem_in[mt * 128:(mt + 1) * 128, :], writes=[xs[mt]], sem=f"lx{mt}")
                for mt in range(2):
                    norm_to_bf16(xs[mt], mgain, hb[mt], sq, ss1, lnv1, rstd1)
                    transpose_h(hb[mt], PT[mt])
                    op("dve", lambda e, mt=mt: e.tensor_copy(
                        out=memT[:].rearrange("p (c m) -> p c m", c=8)[:, :, mt * 128:(mt + 1) * 128],
                        in_=PT[mt][:].rearrange("p (c m) -> p c m", c=8)), reads=[PT[mt]], writes=[memT])
                for dc in range(8):
                    pk = PB[dc % 2]
                    op("pe", lambda e, pk=pk, dc=dc: [e.matmul(pk[:, 0:MEM], lhsT=wkv[:, c * 2048 + dc * 128:c * 2048 + (dc + 1) * 128],
                                                               rhs=memT[:, c * MEM:(c + 1) * MEM], start=(c == 0), stop=(c == 7))
                                                      for c in range(8)], reads=[wkv, memT], writes=[pk])
                    op("dve", lambda e, pk=pk, dc=dc: e.tensor_copy(out=KmT[:, dc * MEM:(dc + 1) * MEM], in_=pk[:, 0:MEM]),
                       reads=[pk], writes=[KmT])
                for mt in range(2):
                    for half in range(2):
                        pk = PB[2 + (mt * 2 + half) % 2]
                        op("pe", lambda e, pk=pk, mt=mt, half=half: [
                            e.matmul(pk[:], lhsT=memT[:, c * MEM + mt * 128:c * MEM + (mt + 1) * 128],
                                     rhs=wkv[:, c * 2048 + 1024 + half * 512:c * 2048 + 1024 + (half + 1) * 512],
                                     start=(c == 0), stop=(c == 7)) for c in range(8)], reads=[wkv, memT], writes=[pk])
                        op("act", lambda e, pk=pk, mt=mt, half=half: e.activation(
                            out=Vm[:, mt * D + half * 512:mt * D + (half + 1) * 512], in_=pk[:], func=AF.Copy),
                           reads=[pk], writes=[Vm])

                def load_blk(blk):
                    for jj in range(4):
                        t = blk * 4 + jj
                        i = (blk % 2) * 4 + jj
                        dma("sp", xs[i][:], src[t * 128:(t + 1) * 128, :], reads=[dsrc], writes=[xs[i]], sem=f"lx{i}")

                def norm_tile(blk, jj):
                    i = (blk % 2) * 4 + jj
                    norm_to_bf16(xs[i], gain, hb[jj % 2], sq, ss1, lnv1, rstd1)

                def tr_evac(jj):
                    k = jj % 2
                    transpose_h(hb[k], PT[k])
                    op("dve", lambda e: e.tensor_copy(
                        out=hT[:].rearrange("p (c m) -> p c m", c=8)[:, :, jj * 128:(jj + 1) * 128],
                        in_=PT[k][:].rearrange("p (c m) -> p c m", c=8)), reads=[PT[k]], writes=[hT])

                load_blk(0)
                for jj in range(4):
                    norm_tile(0, jj)
                    tr_evac(jj)
                for blk in range(NB):
                    if blk + 1 < NB:
                        load_blk(blk + 1)
                    for dc in range(8):
                        pk = PB[dc % 2]
                        op("pe", lambda e, pk=pk, dc=dc: [e.matmul(pk[:], lhsT=wq[:, c * D + dc * 128:c * D + (dc + 1) * 128],
                                                                   rhs=hT[:, c * 512:(c + 1) * 512], start=(c == 0), stop=(c == 7))
                                                          for c in range(8)], reads=[wq, hT], writes=[pk])
                        op("act", lambda e, pk=pk, dc=dc: e.activation(out=qT[:, dc * 512:(dc + 1) * 512], in_=pk[:],
                                                                       func=AF.Copy, scale=1.0 / 16.0), reads=[pk], writes=[qT])
                    SBK = [[PB[2], PB[3]], [PB[4], PB[5]]]

                    def s_stage(h):
                        for m in range(2):
                            sbk = SBK[h % 2][m]
                            op("pe", lambda e: [
                                e.matmul(sbk[:], lhsT=KmT[:, (2 * h + jd) * MEM + m * 128:(2 * h + jd) * MEM + (m + 1) * 128],
                                         rhs=qT[:, (2 * h + jd) * 512:(2 * h + jd + 1) * 512], start=(jd == 0), stop=(jd == 1))
                                for jd in range(2)], reads=[KmT, qT], writes=[sbk])

                    def rest(h):
                        pbs = [Pb[(h % 2) * 2 + m] for m in range(2)]
                        for m in range(2):
                            sbk = SBK[h % 2][m]
                            op("act", lambda e: e.activation(out=pbs[m][:], in_=sbk[:], func=AF.Exp), reads=[sbk], writes=[pbs[m]])
                        dbk = PB[6]
                        rd = rden[h % 2]
                        op("pe", lambda e: [e.matmul(dbk[:], lhsT=ones_b[:], rhs=pbs[m][:], start=(m == 0), stop=(m == 1))
                                            for m in range(2)], reads=[ones_b, pbs[0], pbs[1]], writes=[dbk])
                        op("act", lambda e: e.activation(out=rd[:], in_=dbk[:], func=AF.Ln), reads=[dbk], writes=[rd])
                        op("act", lambda e: e.activation(out=rd[:], in_=rd[:], func=AF.Exp, scale=-1.0), reads=[rd], writes=[rd])
                        for jd in range(2):
                            obk = PB[jd]
                            op("pe", lambda e: [
                                e.matmul(obk[:], lhsT=Vm[:, m * D + h * 256 + jd * 128:m * D + h * 256 + (jd + 1) * 128],
                                         rhs=pbs[m][:], start=(m == 0), stop=(m == 1)) for m in range(2)],
                               reads=[Vm, pbs[0], pbs[1]], writes=[obk])
                            op("dve", lambda e: e.tensor_tensor(
                                out=OT[:, (2 * h + jd) * 512:(2 * h + jd + 1) * 512], in0=obk[:], in1=rd[:], op=ALU.mult),
                               reads=[obk, rd], writes=[OT])

                    s_stage(0)
                    for h in range(4):
                        if h + 1 < 4:
                            s_stage(h + 1)
                        rest(h)
                    for jj in range(4):
                        t = blk * 4 + jj
                        xi = xs[(blk % 2) * 4 + jj]
                        xo = xn[jj % 2]
                        if blk + 1 < NB:
                            norm_tile(blk + 1, jj)
                        for half in range(2):
                            pk = PB[half + 2]
                            op("pe", lambda e, pk=pk, half=half: [
                                e.matmul(pk[:], lhsT=OT[:, dc * 512 + jj * 128:dc * 512 + (jj + 1) * 128],
                                         rhs=wo[:, dc * D + half * 512:dc * D + (half + 1) * 512], start=(dc == 0), stop=(dc == 7))
                                for dc in range(8)], reads=[OT, wo], writes=[pk])
                            op("dve", lambda e, pk=pk, half=half: e.tensor_tensor(
                                out=xo[:, half * 512:(half + 1) * 512], in0=pk[:], in1=xi[:, half * 512:(half + 1) * 512],
                                op=ALU.add), reads=[pk, xi], writes=[xo])
                        dma("sp", dst[t * 128:(t + 1) * 128, :], xo[:], reads=[xo], writes=[ddst], sem=f"so{jj % 2}")
                        if blk + 1 < NB:
                            tr_evac(jj)
                SC.barrier()

        def phase_ffn(l, src, dsrc, dst, ddst, final):
            with ExitStack() as ph:
                PB, PT = alloc_psum(ph, 6)
                wup = sbuf(ph, "wup", [128, 8 * 2 * DFF], BF16)
                wd = sbuf(ph, "wd", [128, 22 * D], BF16)
                gain = sbuf(ph, "gain_f", [128, D])
                fgain = sbuf(ph, "fgain", [128, D]) if final else None
                cp = sbuf(ph, "cp", [128, 4 * NCH])
                xs = [sbuf(ph, f"xs{i}", [128, D]) for i in range(2)]
                xr = [sbuf(ph, f"xr{i}", [128, D]) for i in range(2)]
                hb = sbuf(ph, "hb", [128, D], BF16)
                hT = sbuf(ph, "hT", [128, 8 * 512], BF16)
                hTh = sbuf(ph, "hTh", [128, 8 * 16], BF16)
                uh = sbuf(ph, "uh", [128, NCH * 16])
                G = sbuf(ph, "G", [128, 22 * 512], BF16)
                accg = [sbuf(ph, f"accg{i}", [128, 512]) for i in range(2)]
                accv = [sbuf(ph, f"accv{i}", [128, 512]) for i in range(2)]
                sq = sbuf(ph, "sq", [128, D])
                ss2 = sbuf(ph, "ss2", [128, 1]); lnv2 = sbuf(ph, "lnv2", [128, 1]); rstd2 = sbuf(ph, "rstd2", [128, 1])
                sqf = sq
                cpr = sq
                xh = xr[0]
                ss1 = sbuf(ph, "ss1", [128, 1]); lnv1 = sbuf(ph, "lnv1", [128, 1]); rstd1 = sbuf(ph, "rstd1", [128, 1])

                load_weight_bf16(wup, ffn_w_up[l], 8, 2 * DFF, "w0", col_split=4)
                load_weight_bf16(wd, ffn_w_down[l], 22, D, "w1", col_split=2)
                dma("sp", gain[:], bc_part(ffn_norm[l:l + 1, :]), writes=[gain], sem="c0")
                if final:
                    dma("sp", fgain[:], bc_part(final_norm), writes=[fgain], sem="c0")
                for k in range(3):
                    dma("sp", cpr[0:NCH, k * 128:(k + 1) * 128], ffn_conv_w[l, k].rearrange("(c p) -> c p", p=128),
                        writes=[cpr], sem="c1")
                dma("sp", cpr[0:NCH, 384:512], ffn_conv_b[l].rearrange("(c p) -> c p", p=128), writes=[cpr], sem="c1")
                op("pe", lambda e: [e.transpose(out=PB[0][:, k * NCH:(k + 1) * NCH], in_=cpr[0:NCH, k * 128:(k + 1) * 128],
                                                identity=identf[0:NCH, 0:NCH]) for k in range(4)],
                   reads=[cpr, identf], writes=[PB[0]])
                op("dve", lambda e: e.tensor_copy(out=cp[:], in_=PB[0][:, 0:4 * NCH]), reads=[PB[0]], writes=[cp])

                def cw(k, ch):
                    return cp[:, k * NCH + ch:k * NCH + ch + 1]

                op("dve", lambda e: e.memset(xh[0:16, :], 0.0), writes=[xh])
                for m in range(NB - 1):
                    dma("sp", xh[2 * m:2 * m + 2, :], src[512 * (m + 1) - 1:512 * (m + 1) + 1, :], reads=[dsrc], writes=[xh],
                        sem="lh")
                op("dve", lambda e: e.memset(uh[:], 0.0), writes=[uh])
                if NB > 1:
                    norm_to_bf16(xh, gain, hb, sq, ss1, lnv1, rstd1, npart=16)
                    transpose_h(hb, PT[0], npart=16)
                    op("dve", lambda e: e.tensor_copy(out=hTh[:], in_=PT[0][:, 0:128]), reads=[PT[0]], writes=[hTh])
                    for hh in range(2):
                        pk = PB[1 + hh]
                        op("pe", lambda e, pk=pk, hh=hh: [
                            e.matmul(pk[:, cc * 16:(cc + 1) * 16], lhsT=wup[:, c * 2 * DFF + (hh * 22 + cc) * 128:c * 2 * DFF + (hh * 22 + cc + 1) * 128],
                                     rhs=hTh[:, c * 16:(c + 1) * 16], start=(c == 0), stop=(c == 7))
                            for cc in range(22) for c in range(8)], reads=[wup, hTh], writes=[pk])
                        op("dve", lambda e, pk=pk, hh=hh: e.tensor_copy(out=uh[:, hh * 352:(hh + 1) * 352], in_=pk[:, 0:352]),
                           reads=[pk], writes=[uh])
                    uh4 = uh[:].rearrange("p (c m two) -> p c m two", c=NCH, two=2)
                    op("dve", lambda e: e.tensor_tensor(out=uh4[:, :, :, 0], in0=uh4[:, :, :, 0],
                                                        in1=bc_last(cp[:, 0:NCH], 8), op=ALU.mult), reads=[uh, cp], writes=[uh])
                    op("dve", lambda e: e.tensor_tensor(out=uh4[:, :, :, 1], in0=uh4[:, :, :, 1],
                                                        in1=bc_last(cp[:, 2 * NCH:3 * NCH], 8), op=ALU.mult), reads=[uh, cp], writes=[uh])

                def conv_chunk(blk, ch, ub, acc):
                    op("act", lambda e: e.activation(out=acc[:], in_=ub[:], func=AF.Identity, scale=cw(1, ch), bias=cw(3, ch)),
                       reads=[ub, cp], writes=[acc])
                    op("dve", lambda e: e.scalar_tensor_tensor(out=acc[:, 1:512], in0=ub[:, 0:511], scalar=cw(0, ch),
                                                               in1=acc[:, 1:512], op0=ALU.mult, op1=ALU.add),
                       reads=[ub, cp, acc], writes=[acc])
                    op("dve", lambda e: e.scalar_tensor_tensor(out=acc[:, 0:511], in0=ub[:, 1:512], scalar=cw(2, ch),
                                                               in1=acc[:, 0:511], op0=ALU.mult, op1=ALU.add),
                       reads=[ub, cp, acc], writes=[acc])
                    li = ch * 16 + 2 * (blk - 1)
                    ri = ch * 16 + 2 * blk + 1
                    if 0 < blk < NB - 1:
                        op("dve", lambda e: e.tensor_tensor(out=acc[:, 0:512:511], in0=acc[:, 0:512:511], in1=uh[:, li:li + 4:3],
                                                            op=ALU.add), reads=[acc, uh], writes=[acc])
                    elif blk > 0:
                        op("dve", lambda e: e.tensor_tensor(out=acc[:, 0:1], in0=acc[:, 0:1], in1=uh[:, li:li + 1], op=ALU.add),
                           reads=[acc, uh], writes=[acc])
                    elif blk < NB - 1:
                        op("dve", lambda e: e.tensor_tensor(out=acc[:, 511:512], in0=acc[:, 511:512], in1=uh[:, ri:ri + 1], op=ALU.add),
                           reads=[acc, uh], writes=[acc])

                def load_x(t, i):
                    dma("sp", xs[i][:], src[t * 128:(t + 1) * 128, :], reads=[dsrc], writes=[xs[i]], sem=f"lx{i}")

                def norm_stages(t):
                    sts = [lambda: load_x(t + 1, (t + 1) % 2) if t + 1 < NT else None]
                    sts += norm_st(xs[t % 2], gain, hb, sq, ss1, lnv1, rstd1)
                    return sts

                def tr_evac(jj):
                    transpose_h(hb, PT[jj % 2])
                    op("dve", lambda e: e.tensor_copy(
                        out=hT[:].rearrange("p (c m) -> p c m", c=8)[:, :, jj * 128:(jj + 1) * 128],
                        in_=PT[jj % 2][:].rearrange("p (c m) -> p c m", c=8)), reads=[PT[jj % 2]], writes=[hT])

                load_x(0, 0)
                for jj in range(4):
                    for f in norm_stages(jj):
                        f()
                    tr_evac(jj)
                for blk in range(NB):
                    for g in range(22):
                        k = g % 2
                        ug, uv = PB[2 * k], PB[2 * k + 1]
                        for (ub, ch) in ((ug, g), (uv, 22 + g)):
                            op("pe", lambda e, ub=ub, ch=ch: [
                                e.matmul(ub[:], lhsT=wup[:, c * 2 * DFF + ch * 128:c * 2 * DFF + (ch + 1) * 128],
                                         rhs=hT[:, c * 512:(c + 1) * 512], start=(c == 0), stop=(c == 7)) for c in range(8)],
                               reads=[wup, hT], writes=[ub])
                        conv_chunk(blk, g, ug, accg[k])
                        conv_chunk(blk, 22 + g, uv, accv[k])
                        op("act", lambda e, k=k: e.activation(out=accg[k][:], in_=accg[k][:], func=AF.Silu),
                           reads=[accg[k]], writes=[accg[k]])
                        op("pool", lambda e, k=k, g=g: e.tensor_tensor(out=G[:, g * 512:(g + 1) * 512], in0=accg[k][:], in1=accv[k][:],
                                                                       op=ALU.mult), reads=[accg[k], accv[k]], writes=[G])
                    for jj in range(4):
                        t = blk * 4 + jj
                        nxt = blk + 1 < NB
                        xi = xr[jj % 2]
                        dma("sp", xi[:], src[t * 128:(t + 1) * 128, :], reads=[dsrc], writes=[xi], sem=f"lr{jj % 2}")
                        if nxt:
                            for f in norm_stages(t + 4):
                                f()
                        for half in range(2):
                            pk = PB[4 + half]
                            op("pe", lambda e, pk=pk, half=half, jj=jj: [
                                e.matmul(pk[:], lhsT=G[:, g * 512 + jj * 128:g * 512 + (jj + 1) * 128],
                                         rhs=wd[:, g * D + half * 512:g * D + (half + 1) * 512], start=(g == 0), stop=(g == 21))
                                for g in range(22)], reads=[G, wd], writes=[pk])
                        if nxt:
                            tr_evac(jj)
                        for half in range(2):
                            pk = PB[4 + half]
                            op("dve", lambda e, pk=pk, half=half, xi=xi: e.tensor_tensor(
                                out=xi[:, half * 512:(half + 1) * 512], in0=pk[:], in1=xi[:, half * 512:(half + 1) * 512],
                                op=ALU.add), reads=[pk, xi], writes=[xi])
                        if final:
                            rms_stats(xi[:, :], xi, 128, sqf, ss2, lnv2, rstd2)
                            op("dve", lambda e, xi=xi: e.scalar_tensor_tensor(out=xi[:], in0=xi[:], scalar=rstd2[:, 0:1],
                                                                              in1=fgain[:], op0=ALU.mult, op1=ALU.mult),
                               reads=[xi, rstd2, fgain], writes=[xi])
                        dma("sp", dst[t * 128:(t + 1) * 128, :], xi[:], reads=[xi], writes=[ddst], sem=f"so{jj % 2}")
                SC.barrier()

        def phase_pool(src, dsrc, dst, ddst):
            PADW = S + 16
            with ExitStack() as ph:
                PB, PT = alloc_psum(ph, 6)
                HT = sbuf(ph, "HT", [128, 8 * PADW], BF16)
                wp = sbuf(ph, "wp", [128, 8 * 256], BF16)
                gain = sbuf(ph, "gain_p", [128, D])
                pscale = sbuf(ph, "pscale", [128, D])
                invc = sbuf(ph, "invc", [128, 64])
                xs = [sbuf(ph, f"xs{i}", [128, D]) for i in range(3)]
                hb = [sbuf(ph, f"hb{i}", [128, D], BF16) for i in range(2)]
                Wa = [sbuf(ph, f"Wa{i}", [128, PADW]) for i in range(2)]
                Wb = [sbuf(ph, f"Wb{i}", [128, PADW]) for i in range(2)]
                edge = [sbuf(ph, f"edge{i}", [128, 16]) for i in range(2)]
                edgeb = [sbuf(ph, f"edgeb{i}", [128, 16], BF16) for i in range(2)]
                xn = [sbuf(ph, f"xn{i}", [128, D]) for i in range(2)]
                sq = sbuf(ph, "sq", [128, D])
                ss1 = sbuf(ph, "ss1", [128, 1]); lnv1 = sbuf(ph, "lnv1", [128, 1]); rstd1 = sbuf(ph, "rstd1", [128, 1])

                dma("pool", wp[:].rearrange("p (c n) -> p c n", c=8), pool_w[0].rearrange("g (j p) n -> p (g j) n", p=128),
                    writes=[wp], sem="w0")
                dma("sp", gain[:], bc_part(pool_norm), writes=[gain], sem="c0")
                dma("sp", pscale[:], bc_part(pool_scale), writes=[pscale], sem="c0")
                dma("sp", invc[:], bc_part(invc_in), writes=[invc], sem="c0")
                H3 = HT[:].rearrange("p (c w) -> p c w", c=8)
                op("pool", lambda e: e.memset(H3[:, :, 0:8], 0.0), writes=[HT])
                op("pool", lambda e: e.memset(H3[:, :, 8 + S:16 + S], 0.0), writes=[HT])

                def load_x(t, i):
                    dma("sp", xs[i][:], src[t * 128:(t + 1) * 128, :], reads=[dsrc], writes=[xs[i]], sem=f"lx{i}")

                load_x(0, 0)
                for t in range(NT):
                    if t + 1 < NT:
                        load_x(t + 1, (t + 1) % 3)
                    k = t % 2
                    norm_to_bf16(xs[t % 3], gain, hb[k], sq, ss1, lnv1, rstd1)
                    transpose_h(hb[k], PT[k])
                    op("act", lambda e, k=k, t=t: e.activation(out=H3[:, :, 8 + t * 128:8 + (t + 1) * 128],
                                                               in_=PT[k][:].rearrange("p (c m) -> p c m", c=8), func=AF.Copy),
                       reads=[PT[k]], writes=[HT])
                for c in range(8):
                    grp = c // 2
                    eng = "pool" if c in (5, 7) else "dve"
                    Hc = HT[:, c * PADW:(c + 1) * PADW]
                    ks = 1 if eng == "pool" else 0
                    A_, B_ = Wa[ks], Wb[ks]
                    ed, edb = edge[ks], edgeb[ks]
                    op(eng, lambda e: e.tensor_tensor(out=A_[:, 1:PADW], in0=Hc[:, 0:PADW - 1], in1=Hc[:, 1:PADW], op=ALU.add),
                       reads=[HT], writes=[A_])
                    cur, other = A_, B_
                    lo, hi = 1, PADW
                    sh = 1
                    for lvl in range(grp):
                        nlo, nhi = lo + sh, hi - sh
                        op(eng, lambda e, cur=cur, other=other, nlo=nlo, nhi=nhi, sh=sh: e.tensor_tensor(
                            out=other[:, nlo:nhi], in0=cur[:, nlo - sh:nhi - sh], in1=cur[:, nlo + sh:nhi + sh], op=ALU.add),
                           reads=[cur], writes=[other])
                        cur, other = other, cur
                        lo, hi = nlo, nhi
                        sh *= 2
                    w = 2 ** (grp + 1)
                    cur_e = bass.AP(cur.t[:, 8:16].tensor, cur.t[:, 8:16].offset, [list(cur.t[:, 8:16].ap[0]), [S - 8, 2], [1, 8]])
                    h_e = bass.AP(Hc[:, 8:16].tensor, Hc[:, 8:16].offset, [list(Hc[:, 8:16].ap[0]), [S - 8, 2], [1, 8]])
                    op(eng, lambda e: e.tensor_tensor(out=ed[:].rearrange("p (a b) -> p a b", a=2), in0=cur_e,
                                                      in1=invc[:, grp * 16:(grp + 1) * 16].rearrange("p (a b) -> p a b", a=2),
                                                      op=ALU.mult), reads=[cur, invc], writes=[ed])
                    op(eng, lambda e: e.tensor_tensor(out=edb[:].rearrange("p (a b) -> p a b", a=2),
                                                      in0=ed[:].rearrange("p (a b) -> p a b", a=2), in1=h_e, op=ALU.subtract),
                       reads=[ed, HT], writes=[edb])
                    if eng == "dve":
                        op(eng, lambda e: e.scalar_tensor_tensor(out=Hc[:, 8:8 + S], in0=cur[:, 8:8 + S], scalar=1.0 / w,
                                                                 in1=Hc[:, 8:8 + S], op0=ALU.mult, op1=ALU.subtract),
                           reads=[cur, HT], writes=[HT])
                    else:
                        op(eng, lambda e: e.tensor_scalar(out=cur[:, 8:8 + S], in0=cur[:, 8:8 + S], scalar1=1.0 / w, scalar2=0.0,
                                                          op0=ALU.mult, op1=ALU.add), reads=[cur], writes=[cur])
                        op(eng, lambda e: e.tensor_tensor(out=Hc[:, 8:8 + S], in0=cur[:, 8:8 + S], in1=Hc[:, 8:8 + S],
                                                          op=ALU.subtract), reads=[cur, HT], writes=[HT])
                    op(eng, lambda e: e.tensor_copy(out=h_e, in_=edb[:].rearrange("p (a b) -> p a b", a=2)),
                       reads=[edb], writes=[HT])
                load_x(0, 0)
                for t in range(NT):
                    if t + 1 < NT:
                        load_x(t + 1, (t + 1) % 3)
                    xi = xs[t % 3]
                    xo = xn[t % 2]
                    for half in range(2):
                        pk = PB[(t % 2) * 2 + half]
                        op("pe", lambda e, pk=pk, half=half, t=t: [
                            e.matmul(pk[:, gg * 256:(gg + 1) * 256],
                                     lhsT=HT[:, (4 * half + 2 * gg + jc) * PADW + 8 + t * 128:(4 * half + 2 * gg + jc) * PADW + 8 + (t + 1) * 128],
                                     rhs=wp[:, (4 * half + 2 * gg + jc) * 256:(4 * half + 2 * gg + jc + 1) * 256],
                                     start=(jc == 0), stop=(jc == 1)) for gg in range(2) for jc in range(2)],
                           reads=[HT, wp], writes=[pk])
                        op("dve", lambda e, pk=pk, half=half: e.tensor_tensor(
                            out=xo[:, half * 512:(half + 1) * 512], in0=pk[:], in1=pscale[:, half * 512:(half + 1) * 512],
                            op=ALU.mult), reads=[pk, pscale], writes=[xo])
                        op("dve", lambda e, half=half: e.tensor_tensor(
                            out=xo[:, half * 512:(half + 1) * 512], in0=xo[:, half * 512:(half + 1) * 512],
                            in1=xi[:, half * 512:(half + 1) * 512], op=ALU.add), reads=[xo, xi], writes=[xo])
                    dma("sp", dst[t * 128:(t + 1) * 128, :], xo[:], reads=[xo], writes=[ddst], sem=f"so{t % 2}")
                SC.barrier()

        SC.barrier()
        plan = [
            ("attn", lambda s, ds, d, dd: phase_attention(s, ds, d, dd)),
            ("xattn0", lambda s, ds, d, dd: phase_xattn(0, s, ds, d, dd)),
            ("ffn0", lambda s, ds, d, dd: phase_ffn(0, s, ds, d, dd, False)),
            ("pool", lambda s, ds, d, dd: phase_pool(s, ds, d, dd)),
            ("xattn1", lambda s, ds, d, dd: phase_xattn(1, s, ds, d, dd)),
            ("ffn1", lambda s, ds, d, dd: phase_ffn(1, s, ds, d, dd, True)),
        ]
        if stop_after is not None:
            plan = plan[:stop_after]
        cur, dcur = x_in, dX
        for i, (name, fn) in enumerate(plan):
            last = i == len(plan) - 1
            if last:
                nxt, dnxt = out, dO
            else:
                nxt, dnxt = (rA, dA) if i % 2 == 0 else (rB, dB)
            fn(cur, dcur, nxt, dnxt)
            cur, dcur = nxt, dnxt
        SC.barrier()
        build.stats = (SC.nops, SC.nwaits)
    return nc


def rope_table(S):
    t = np.arange(S)
    row = (t // 64).astype(np.float32)
    col = (t % 64).astype(np.float32)
    inv_freq = (10000.0 ** (-np.arange(16, dtype=np.float32) / 16)).astype(np.float32)
    ang = np.stack([row[:, None] * inv_freq, col[:, None] * inv_freq], axis=1).astype(np.float32)
    c, s = np.cos(ang).astype(np.float32), np.sin(ang).astype(np.float32)
    cos64 = np.concatenate([c[:, 0], c[:, 0], c[:, 1], c[:, 1]], axis=1)
    return np.ascontiguousarray(np.concatenate([cos64, -s.reshape(S, 32), s.reshape(S, 32)], axis=1).astype(np.float32))


def invc_table(S):
    tab = np.zeros((4, 16), np.float32)
    toks = np.concatenate([np.arange(8), np.arange(S - 8, S)])
    for g, w in enumerate((2, 4, 8, 16)):
        lo = np.clip(toks - w // 2, 0, S)
        hi = np.clip(toks + w - w // 2, 0, S)
        tab[g] = 1.0 / (hi - lo).astype(np.float32)
    return tab.reshape(1, 64)


_NC_CACHE = {}


def kernel(**inputs):
    S = inputs["x"].shape[1]
    B = inputs["x"].shape[0]
    if S not in _NC_CACHE:
        _NC_CACHE[S] = build(S)
    nc = _NC_CACHE[S]
    shared = {k: np.ascontiguousarray(np.asarray(v, dtype=np.float32)) for k, v in inputs.items() if k not in ("x", "mem")}
    shared["final_norm"] = shared["final_norm"].reshape(1, D)
    shared["rope"] = rope_table(S)
    shared["invc"] = invc_table(S)
    in_maps = []
    for b in range(B):
        m = dict(shared)
        m["x"] = np.ascontiguousarray(np.asarray(inputs["x"][b], dtype=np.float32))
        m["mem"] = np.ascontiguousarray(np.asarray(inputs["mem"][b], dtype=np.float32))
        in_maps.append(m)
    res = run_bass_kernel_spmd(nc, in_maps, core_ids=list(range(B)))
    return np.stack([np.asarray(r["out"], dtype=np.float32) for r in res.results], axis=0)
```
